# Optimizing a Trainium2 kernel written in Bass

```python
import math
import jax, jax.numpy as jnp
from jax import lax
import numpy as np

D_MODEL = 1024
BATCH = 8
SEQ = 4096
DEPTH = 4

CTX_LEN = 256
GRID_W = 64
EPS = 1e-6
N_MOD = 6
A_HEADS = 8
A_DIM = 64
A_QBLOCK = 128
ROPE_THETA = 10000.0
B_HEADS = 8
B_DIM = 128
B_CHUNK = 64
SHORT_CONV = 5
C_WIDTH = 1024
C_BLOCKS = 16
LRU_C = 8.0
N_BRANCH = 3
P_HEADS = 8
P_KEYS = 128
P_EXPERTS = P_KEYS * P_KEYS
P_QDIM = 256
P_TOPK = 16
P_BLOCK = 128

A_QKV = 3 * A_HEADS * 2 * A_DIM
B_QKV = 3 * B_HEADS * B_DIM
IN_SPLITS = (A_QKV, B_QKV, B_HEADS * B_DIM, 2 * B_HEADS, 2 * B_HEADS, C_WIDTH, C_WIDTH, N_BRANCH * D_MODEL)
IN_WIDTH = A_QKV + B_QKV + B_HEADS * B_DIM + 4 * B_HEADS + 2 * C_WIDTH + N_BRANCH * D_MODEL

kernel_name = "hybrid_diffattn_gdn_rglru_peer_prefix_dit"


def rmsnorm(x, g=None):
    xf = x.astype(jnp.float32)
    y = xf * lax.rsqrt(jnp.mean(xf * xf, axis=-1, keepdims=True) + EPS)
    if g is not None:
        y = y * g.astype(jnp.float32)
    return y.astype(x.dtype)


def l2norm(x):
    return x * lax.rsqrt(jnp.sum(x * x, axis=-1, keepdims=True) + EPS)


def adaln(cond, w, b):
    m = (jax.nn.silu(cond) @ w + b)[:, None, :]
    return jnp.split(m, N_MOD, axis=-1)


def modulate(h, shift, scale):
    return h * (1.0 + scale) + shift


def dwconv(x, w, b=None):
    pad = SHORT_CONV // 2
    y = lax.conv_general_dilated(x, w[:, None, :].astype(x.dtype), window_strides=(1,),
                                 padding=[(pad, pad)], dimension_numbers=('NWC', 'WIO', 'NWC'),
                                 feature_group_count=x.shape[-1])
    return y if b is None else y + b.astype(x.dtype)


def split_in(p):
    offs = np.cumsum(IN_SPLITS)[:-1].tolist()
    return jnp.split(p, offs, axis=-1)


def flip_if(t, on, axis):
    return jnp.flip(t, axis=axis) if on else t


def axial_rope_tables(rows, dtype):
    row = jnp.repeat(jnp.arange(rows, dtype=jnp.float32), GRID_W)
    col = jnp.tile(jnp.arange(GRID_W, dtype=jnp.float32), rows)
    nf = A_DIM // 4
    inv = ROPE_THETA ** (-jnp.arange(nf, dtype=jnp.float32) / nf)
    ar, ac = row[:, None] * inv, col[:, None] * inv
    return tuple(t.astype(dtype) for t in (jnp.cos(ar), jnp.sin(ar), jnp.cos(ac), jnp.sin(ac)))


def rope_half(x, cos, sin):
    x1, x2 = jnp.split(x, 2, axis=-1)
    cs, sn = cos[:, None, None, :], sin[:, None, None, :]
    return jnp.concatenate([x1 * cs - x2 * sn, x2 * cs + x1 * sn], axis=-1)


def axial_rope(x, tabs):
    cr, sr, cc, sc = tabs
    xr, xc = jnp.split(x, 2, axis=-1)
    return jnp.concatenate([rope_half(xr, cr, sr), rope_half(xc, cc, sc)], axis=-1)


def diff_attend(q, k, v, lam):
    s = jnp.einsum('bqhcd,bkhcd->bhcqk', q, k).astype(jnp.float32) * (A_DIM ** -0.5)
    p = jax.nn.softmax(s, axis=-1)
    a = p[:, :, 0] - lam * p[:, :, 1]
    return jnp.einsum('bhqk,bkhe->bqhe', a.astype(v.dtype), v)


def diff_attention(qkv_c, qkv_l, rope, layer, need_ctx, qn_g, kn_g, lq1, lk1, lq2, lk2, sub_g):
    lam_init = 0.8 - 0.6 * math.exp(-0.3 * layer)
    f = lambda t: t.astype(jnp.float32)
    lam = jnp.exp(jnp.sum(f(lq1) * f(lk1))) - jnp.exp(jnp.sum(f(lq2) * f(lk2))) + lam_init

    def heads(qkv, rotate):
        q, k, v = jnp.split(qkv, 3, axis=-1)
        bn, tn, _ = q.shape
        q = rmsnorm(q.reshape(bn, tn, A_HEADS, 2, A_DIM), qn_g)
        k = rmsnorm(k.reshape(bn, tn, A_HEADS, 2, A_DIM), kn_g)
        if rotate:
            q, k = axial_rope(q, rope), axial_rope(k, rope)
        return q, k, v.reshape(bn, tn, A_HEADS, 2 * A_DIM)

    qc, kc, vc = heads(qkv_c, False)
    ql, kl, vl = heads(qkv_l, True)
    k_all = jnp.concatenate([kc, kl], axis=1)
    v_all = jnp.concatenate([vc, vl], axis=1)
    bn, tn = ql.shape[:2]
    nblk = tn // A_QBLOCK
    qb = jnp.moveaxis(ql.reshape(bn, nblk, A_QBLOCK, A_HEADS, 2, A_DIM), 1, 0)
    ol = lax.map(lambda qx: diff_attend(qx, k_all, v_all, lam), qb)
    ol = jnp.moveaxis(ol, 0, 1).reshape(bn, tn, A_HEADS, 2 * A_DIM)

    def finish(o):
        return (rmsnorm(o, sub_g) * (1.0 - lam_init)).reshape(o.shape[0], o.shape[1], -1)

    yc = finish(diff_attend(qc, kc, vc, lam)) if need_ctx else None
    return yc, finish(ol)


def gdn_chunk_scan(q, k, v, g, beta, s0):
    bn, hn, tn, _ = q.shape
    n = tn // B_CHUNK
    rs = lambda t: t.reshape(bn, hn, n, B_CHUNK, *t.shape[3:])
    q, k, v, g, beta = rs(q), rs(k), rs(v), rs(g), rs(beta)
    g = jnp.cumsum(g, axis=-1)
    idx = jnp.arange(B_CHUNK)
    lower = idx[:, None] >= idx[None, :]
    strict = idx[:, None] > idx[None, :]
    decay = jnp.exp(jnp.where(lower, g[..., :, None] - g[..., None, :], -jnp.inf))
    kb = k * beta[..., None]
    l_mat = jnp.where(strict, jnp.einsum('bhnid,bhnjd->bhnij', kb, k) * decay, 0.0)
    a_mat = l_mat + jnp.eye(B_CHUNK, dtype=q.dtype)
    u = lax.linalg.triangular_solve(a_mat, v * beta[..., None], left_side=True, lower=True, unit_diagonal=True)
    w = lax.linalg.triangular_solve(a_mat, kb * jnp.exp(g)[..., None], left_side=True, lower=True, unit_diagonal=True)
    qk = jnp.einsum('bhnid,bhnjd->bhnij', q, k) * decay
    qg = q * jnp.exp(g)[..., None]
    g_last = g[..., -1]
    kd = k * jnp.exp(g_last[..., None] - g)[..., None]

    def step(s, xs):
        qk_n, u_n, w_n, qg_n, kd_n, gl_n = xs
        v_new = u_n - jnp.einsum('bhcd,bhde->bhce', w_n, s)
        o_n = jnp.einsum('bhcd,bhde->bhce', qg_n, s) + jnp.einsum('bhij,bhje->bhie', qk_n, v_new)
        s = s * jnp.exp(gl_n)[..., None, None] + jnp.einsum('bhcd,bhce->bhde', kd_n, v_new)
        return s, o_n

    xs = tuple(jnp.moveaxis(t, 2, 0) for t in (qk, u, w, qg, kd, g_last))
    s_fin, o = lax.scan(step, s0, xs)
    return jnp.moveaxis(o, 0, 2).reshape(bn, hn, tn, -1), s_fin


def gated_deltanet(parts_c, parts_l, conv_w, a_log, dt_bias, norm_g):
    def prep(parts):
        qkv, z, b_raw, a_raw = parts
        qkv = jax.nn.silu(dwconv(qkv, conv_w)).astype(jnp.float32)
        bn, tn, _ = qkv.shape
        q, k, v = [jnp.moveaxis(t.reshape(bn, tn, B_HEADS, B_DIM), 1, 2) for t in jnp.split(qkv, 3, axis=-1)]
        q = l2norm(q) * (B_DIM ** -0.5)
        k = l2norm(k)
        beta = jax.nn.sigmoid(b_raw.astype(jnp.float32)).reshape(bn, tn, 2, B_HEADS)
        g = -jnp.exp(a_log.astype(jnp.float32)) * jax.nn.softplus(
            a_raw.astype(jnp.float32).reshape(bn, tn, 2, B_HEADS) + dt_bias.astype(jnp.float32))
        return q, k, v, jnp.transpose(g, (2, 0, 3, 1)), jnp.transpose(beta, (2, 0, 3, 1)), z

    qc, kc, vc, gc, bc, zc = prep(parts_c)
    ql, kl, vl, gl, bl, zl = prep(parts_l)
    s0 = jnp.zeros((qc.shape[0], B_HEADS, B_DIM, B_DIM), jnp.float32)
    oc = jnp.zeros_like(vc)
    ol = jnp.zeros_like(vl)
    for d in range(2):
        fl = lambda t: flip_if(t, d == 1, 2)
        o_c, s_ctx = gdn_chunk_scan(fl(qc), fl(kc), fl(vc), fl(gc[d]), fl(bc[d]), s0)
        o_l, _ = gdn_chunk_scan(fl(ql), fl(kl), fl(vl), fl(gl[d]), fl(bl[d]), s_ctx)
        oc = oc + fl(o_c)
        ol = ol + fl(o_l)

    def finish(o, z):
        o = jnp.moveaxis(o, 1, 2)
        y = rmsnorm(o, norm_g) * jax.nn.silu(z.astype(jnp.float32).reshape(o.shape))
        return y.reshape(o.shape[0], o.shape[1], -1).astype(z.dtype)

    return finish(oc, zc), finish(ol, zl)


def blockdiag(x, w, b):
    xb = x.reshape(*x.shape[:-1], C_BLOCKS, C_WIDTH // C_BLOCKS)
    return jnp.einsum('btnc,nce->btne', xb, w).reshape(x.shape) + b


def linear_scan(a, b, h0):
    b = b.at[:, 0].add(a[:, 0] * h0)

    def combine(lhs, rhs):
        a_l, b_l = lhs
        a_r, b_r = rhs
        return a_l * a_r, a_r * b_l + b_r

    _, h = lax.associative_scan(combine, (a, b), axis=1)
    return h


def rglru_scan(x, w_r, b_r, w_i, b_i, lam, h0):
    f = lambda t: t.astype(jnp.float32)
    r = jax.nn.sigmoid(blockdiag(x, f(w_r), f(b_r)))
    i = jax.nn.sigmoid(blockdiag(x, f(w_i), f(b_i)))
    log_a = -LRU_C * jax.nn.softplus(-f(lam)) * r
    a = jnp.exp(log_a)
    b = jnp.sqrt(-jnp.expm1(2.0 * log_a)) * (i * x)
    return linear_scan(a, b, h0)


def rglru_branch(parts_c, parts_l, conv_w, conv_b, w_r, b_r, w_i, b_i, lam):
    xc_raw, gate_c = parts_c
    xl_raw, gate_l = parts_l
    xc = dwconv(xc_raw, conv_w, conv_b).astype(jnp.float32)
    xl = dwconv(xl_raw, conv_w, conv_b).astype(jnp.float32)
    h0 = jnp.zeros((xc.shape[0], C_WIDTH), jnp.float32)
    hc = jnp.zeros_like(xc)
    hl = jnp.zeros_like(xl)
    for d in range(2):
        fl = lambda t: flip_if(t, d == 1, 1)
        hcd = rglru_scan(fl(xc), w_r[d], b_r[d], w_i[d], b_i[d], lam[d], h0)
        hld = rglru_scan(fl(xl), w_r[d], b_r[d], w_i[d], b_i[d], lam[d], hcd[:, -1])
        hc = hc + fl(hcd)
        hl = hl + fl(hld)
    yc = (hc * jax.nn.gelu(gate_c.astype(jnp.float32))).astype(gate_c.dtype)
    yl = (hl * jax.nn.gelu(gate_l.astype(jnp.float32))).astype(gate_l.dtype)
    return yc, yl


def merge_branches(ys, gate_logits, w_branch, w_out):
    gates = jnp.split(jax.nn.sigmoid(gate_logits.astype(jnp.float32)).astype(gate_logits.dtype), N_BRANCH, axis=-1)
    y = gates[0] * (ys[0] @ w_branch[0])
    for k in range(1, N_BRANCH):
        y = y + gates[k] * (ys[k] @ w_branch[k])
    return y @ w_out


def token_mixer(hc, hl, rope, layer, need_ctx, w_in,
                attn_qn_g, attn_kn_g, lam_q1, lam_k1, lam_q2, lam_k2, attn_sub_g,
                gdn_conv_w, gdn_a_log, gdn_dt_bias, gdn_norm_g,
                lru_conv_w, lru_conv_b, lru_w_r, lru_b_r, lru_w_i, lru_b_i, lru_lambda,
                w_branch, w_out):
    pc = split_in(hc @ w_in)
    pl = split_in(hl @ w_in)
    ya_c, ya_l = diff_attention(pc[0], pl[0], rope, layer, need_ctx, attn_qn_g, attn_kn_g,
                                lam_q1, lam_k1, lam_q2, lam_k2, attn_sub_g)
    yb_c, yb_l = gated_deltanet(pc[1:5], pl[1:5], gdn_conv_w, gdn_a_log, gdn_dt_bias, gdn_norm_g)
    yr_c, yr_l = rglru_branch(pc[5:7], pl[5:7], lru_conv_w, lru_conv_b, lru_w_r, lru_b_r,
                              lru_w_i, lru_b_i, lru_lambda)
    yl = merge_branches((ya_l, yb_l, yr_l), pl[7], w_branch, w_out)
    yc = merge_branches((ya_c, yb_c, yr_c), pc[7], w_branch, w_out) if need_ctx else None
    return yc, yl


def peer_ffn(h, wq, subkeys, u, v):
    ntok, dm = h.shape
    q = rmsnorm((h @ wq).reshape(ntok, P_HEADS, 2, P_QDIM // 2))
    s = jnp.einsum('nhcd,hckd->nhck', q, subkeys).astype(jnp.float32)
    s_top, i_top = lax.top_k(s, P_TOPK)
    cand = (s_top[..., 0, :, None] + s_top[..., 1, None, :]).reshape(ntok, P_HEADS, P_TOPK * P_TOPK)
    cand_idx = (i_top[..., 0, :, None] * P_KEYS + i_top[..., 1, None, :]).reshape(ntok, P_HEADS, P_TOPK * P_TOPK)
    best, pos = lax.top_k(cand, P_TOPK)
    idx = jnp.take_along_axis(cand_idx, pos, axis=-1)
    gate = jax.nn.softmax(best, axis=-1).astype(h.dtype)
    nb = ntok // P_BLOCK

    def block(args):
        hx, ix, gx = args
        act = jax.nn.gelu(jnp.einsum('pd,phkd->phk', hx, jnp.take(u, ix, axis=0))) * gx
        return jnp.einsum('phk,phkd->pd', act, jnp.take(v, ix, axis=0))

    out = lax.map(block, (h.reshape(nb, P_BLOCK, dm),
                          idx.reshape(nb, P_BLOCK, P_HEADS, P_TOPK),
                          gate.reshape(nb, P_BLOCK, P_HEADS, P_TOPK)))
    return out.reshape(ntok, dm)


def setup_inputs(seed: int = 0) -> dict:
    key = jax.random.key(seed)
    ks = iter(jax.random.split(key, 48))
    L, D = DEPTH, D_MODEL

    def nrm(shape, scale):
        return scale * jax.random.normal(next(ks), shape, jnp.float32)

    def unif(shape, lo, hi):
        return jax.random.uniform(next(ks), shape, jnp.float32, minval=lo, maxval=hi)

    a_lru = unif((L, 2, C_WIDTH), 0.9, 0.999) ** (1.0 / LRU_C)
    dt = jnp.exp(unif((L, 2, B_HEADS), math.log(1e-3), math.log(1e-1)))
    blk = C_WIDTH // C_BLOCKS
    return {
        "x": nrm((BATCH, SEQ, D), 1.0),
        "c": nrm((BATCH, D), 1.0),
        "ctx": nrm((BATCH, CTX_LEN, D), 1.0),
        "c_ctx": nrm((D,), 1.0),
        "w_ada": nrm((L, D, N_MOD * D), D ** -0.5),
        "b_ada": nrm((L, N_MOD * D), 0.02),
        "norm1_g": 1.0 + nrm((L, D), 0.02),
        "norm2_g": 1.0 + nrm((L, D), 0.02),
        "w_in": nrm((L, D, IN_WIDTH), D ** -0.5),
        "attn_qn_g": 1.0 + nrm((L, A_DIM), 0.02),
        "attn_kn_g": 1.0 + nrm((L, A_DIM), 0.02),
        "lam_q1": nrm((L, A_DIM), 0.1),
        "lam_k1": nrm((L, A_DIM), 0.1),
        "lam_q2": nrm((L, A_DIM), 0.1),
        "lam_k2": nrm((L, A_DIM), 0.1),
        "attn_sub_g": 1.0 + nrm((L, 2 * A_DIM), 0.02),
        "gdn_conv_w": nrm((L, SHORT_CONV, B_QKV), SHORT_CONV ** -0.5),
        "gdn_a_log": jnp.log(unif((L, 2, B_HEADS), 1.0, 16.0)),
        "gdn_dt_bias": dt + jnp.log(-jnp.expm1(-dt)),
        "gdn_norm_g": 1.0 + nrm((L, B_DIM), 0.02),
        "lru_conv_w": nrm((L, SHORT_CONV, C_WIDTH), SHORT_CONV ** -0.5),
        "lru_conv_b": nrm((L, C_WIDTH), 0.02),
        "lru_w_r": nrm((L, 2, C_BLOCKS, blk, blk), blk ** -0.5),
        "lru_b_r": nrm((L, 2, C_WIDTH), 0.02),
        "lru_w_i": nrm((L, 2, C_BLOCKS, blk, blk), blk ** -0.5),
        "lru_b_i": nrm((L, 2, C_WIDTH), 0.02),
        "lru_lambda": jnp.log(a_lru) - jnp.log1p(-a_lru),
        "w_branch": nrm((L, N_BRANCH, C_WIDTH, D), C_WIDTH ** -0.5),
        "w_out": nrm((L, D, D), D ** -0.5),
        "peer_wq": nrm((L, D, P_HEADS * P_QDIM), D ** -0.5),
        "peer_subkeys": nrm((L, P_HEADS, 2, P_KEYS, P_QDIM // 2), (P_QDIM // 2) ** -0.5),
        "peer_u": nrm((L, P_EXPERTS, D), D ** -0.5),
        "peer_v": nrm((L, P_EXPERTS, D), 0.3),
    }


def reference(x, c, ctx, c_ctx, w_ada, b_ada, norm1_g, norm2_g, w_in,
              attn_qn_g, attn_kn_g, lam_q1, lam_k1, lam_q2, lam_k2, attn_sub_g,
              gdn_conv_w, gdn_a_log, gdn_dt_bias, gdn_norm_g,
              lru_conv_w, lru_conv_b, lru_w_r, lru_b_r, lru_w_i, lru_b_i, lru_lambda,
              w_branch, w_out, peer_wq, peer_subkeys, peer_u, peer_v):
    n_lat = x.shape[1]
    ROWS = n_lat // GRID_W
    rope = axial_rope_tables(ROWS, x.dtype)
    dm = x.shape[-1]
    xl, xc = x, ctx
    for l in range(DEPTH):
        need_ctx = l < DEPTH - 1
        ml = adaln(c, w_ada[l], b_ada[l])
        mc = adaln(c_ctx[None, :], w_ada[l], b_ada[l])
        hl = modulate(rmsnorm(xl, norm1_g[l]), ml[0], ml[1])
        hc = modulate(rmsnorm(xc, norm1_g[l]), mc[0], mc[1])
        yc, yl = token_mixer(hc, hl, rope, l, need_ctx, w_in[l],
                             attn_qn_g[l], attn_kn_g[l], lam_q1[l], lam_k1[l], lam_q2[l], lam_k2[l], attn_sub_g[l],
                             gdn_conv_w[l], gdn_a_log[l], gdn_dt_bias[l], gdn_norm_g[l],
                             lru_conv_w[l], lru_conv_b[l], lru_w_r[l], lru_b_r[l], lru_w_i[l], lru_b_i[l], lru_lambda[l],
                             w_branch[l], w_out[l])
        xl = xl + ml[2] * yl
        hl = modulate(rmsnorm(xl, norm2_g[l]), ml[3], ml[4])
        xl = xl + ml[5] * peer_ffn(hl.reshape(-1, dm), peer_wq[l], peer_subkeys[l], peer_u[l], peer_v[l]).reshape(xl.shape)
        if need_ctx:
            xc = xc + mc[2] * yc
            hc = modulate(rmsnorm(xc, norm2_g[l]), mc[3], mc[4])
            xc = xc + mc[5] * peer_ffn(hc.reshape(-1, dm), peer_wq[l], peer_subkeys[l], peer_u[l], peer_v[l]).reshape(xc.shape)
    return xl
```

```python
import math
import os
from contextlib import ExitStack

import numpy as np
import concourse.bass as bass
import concourse.mybir as mybir
from concourse.bass_utils import run_bass_kernel_spmd

F32 = mybir.dt.float32
BF16 = mybir.dt.bfloat16
U32 = mybir.dt.uint32
AF = mybir.ActivationFunctionType
ALU = mybir.AluOpType
AX = mybir.AxisListType

D = 1024
KC = 8
N_MOD = 6
EPS = 1e-6
A_HEADS = 8
A_DIM = 64
ROPE_THETA = 10000.0
GRID_W = 64
B_HEADS = 8
B_DIM = 128
SHORT_CONV = 5
C_WIDTH = 1024
LRU_C = 8.0
P_HEADS = 8
P_KEYS = 128
P_TOPK = 16
A_QKV = 3072
B_QKV = 3072
OFF_AQ, OFF_AK, OFF_AV = 0, 1024, 2048
OFF_BQKV = 3072
OFF_Z = 6144
OFF_BETA = 7168
OFF_ALPHA = 7184
OFF_LX = 7200
OFF_LG = 8224
OFF_MG = 9248
IN_WIDTH = 12320

SP = {}
_o = 0
for _n, _w in (("b_ada", 48), ("n1g", 8), ("n2g", 8), ("gq", 1), ("gk", 1), ("subg", 1), ("lam", 4),
               ("gconv", 120), ("lconv", 40), ("lconvb", 8), ("lbr", 16), ("lbi", 16), ("llam", 16)):
    SP[_n] = _o
    _o += _w
NSP = _o
NRV = 160
C_ID, C_ONES, C_BD64, C_PERM, C_UI, C_LI, C_SL, C_SU, C_IOTA0, C_IOTA1 = range(10)
C_OFF = 10
NCONST = 17

EPOCH = 60000
DMA_ND = 8
DMA_GEN = 3500


class T:
    __slots__ = ("w", "r", "pw", "excl")

    def __init__(self, excl=False):
        self.w = {}
        self.r = {}
        self.pw = {}
        self.excl = excl

    def new_version(self):
        pw = dict(self.w)
        _merge(pw, self.r.values())
        _merge(pw, self.pw.values())
        self.pw = pw
        self.w = {}
        self.r = {}


def _merge(d, toks):
    for tok in toks:
        k = id(tok[0])
        if k not in d or d[k][1] < tok[1]:
            d[k] = tok


class Sched:
    ENGS = ("pe", "act", "dve", "pool", "sp")

    def __init__(self, nc, es):
        self.nc = nc
        self.es = es
        self.q = {e: [] for e in self.ENGS}
        self.cnt = {e: 0 for e in self.ENGS}
        self.sems = {e: [] for e in self.ENGS}
        self.seen = {e: {} for e in self.ENGS}
        self.dcnt = {e: 0 for e in self.ENGS}
        self.dsems = {e: [] for e in self.ENGS}
        self.nsem = 0

    def _newsem(self, name):
        self.nsem += 1
        return self.es.enter_context(self.nc.semaphore(f"{name}{self.nsem}"))

    def _wait(self, eng, deps, is_pe_op):
        for (sem, val, src) in deps:
            if is_pe_op and src == "pe":
                continue
            k = id(sem)
            if self.seen[eng].get(k, 0) >= val:
                continue
            self.seen[eng][k] = val
            self.q[eng].append(lambda e, s=sem, v=val: e.wait_ge(s, v))

    def _deps(self, reads, writes, awrites):
        deps = {}
        for t in reads:
            _merge(deps, t.w.values())
            _merge(deps, t.pw.values())
        for t in writes:
            _merge(deps, t.w.values())
            _merge(deps, t.r.values())
            _merge(deps, t.pw.values())
        for t in awrites:
            _merge(deps, t.pw.values())
        return deps

    def _record(self, tok, reads, writes, awrites):
        for t in reads:
            _merge(t.r, [tok])
        for t in writes:
            t.w = {id(tok[0]): tok}
            t.r = {}
            t.pw = {}
        for t in awrites:
            _merge(t.w, [tok])

    @staticmethod
    def _split(reads, writes):
        ex = [t for t in reads if t.excl]
        if not ex:
            return reads, writes
        return [t for t in reads if not t.excl], list(writes) + ex

    def op(self, eng, fn, reads=(), writes=(), awrites=()):
        reads, writes = self._split(reads, writes)
        deps = self._deps(reads, writes, awrites)
        self._wait(eng, deps.values(), eng == "pe")
        i = self.cnt[eng]
        self.cnt[eng] += 1
        ep = i // EPOCH
        while len(self.sems[eng]) <= ep:
            self.sems[eng].append(self._newsem(eng))
        sem = self.sems[eng][ep]
        self.q[eng].append(lambda e, f=fn, s=sem: f(e).then_inc(s, 1))
        tok = (sem, i % EPOCH + 1, eng)
        self._record(tok, reads, writes, awrites)
        return tok

    def dma(self, eng, fn, reads=(), writes=(), awrites=()):
        reads, writes = self._split(reads, writes)
        deps = self._deps(reads, writes, awrites)
        j = self.dcnt[eng]
        self.dcnt[eng] += 1
        gen, within = divmod(j, DMA_ND * DMA_GEN)
        slot = within % DMA_ND
        use = within // DMA_ND
        while len(self.dsems[eng]) <= gen:
            self.dsems[eng].append([self._newsem("d" + eng) for _ in range(DMA_ND)])
        sem = self.dsems[eng][gen][slot]
        if use > 0:
            _merge(deps, [(sem, 16 * use, "dma")])
        self._wait(eng, deps.values(), False)
        self.q[eng].append(lambda e, f=fn, s=sem: f(e).then_inc(s, 16))
        tok = (sem, 16 * (use + 1), "dma")
        self._record(tok, reads, writes, awrites)
        return tok

    def _all_tokens(self):
        toks = []
        for e in self.ENGS:
            if self.cnt[e] > 0:
                i = self.cnt[e] - 1
                toks.append((self.sems[e][i // EPOCH], i % EPOCH + 1, e))
            for gi, gen in enumerate(self.dsems[e]):
                n_in = min(max(self.dcnt[e] - gi * DMA_ND * DMA_GEN, 0), DMA_ND * DMA_GEN)
                for slot, sem in enumerate(gen):
                    uses = (n_in - slot + DMA_ND - 1) // DMA_ND if n_in > slot else 0
                    if uses > 0:
                        toks.append((sem, 16 * uses, "dma"))
        return toks

    def barrier(self):
        toks = self._all_tokens()
        for e in self.ENGS:
            self._wait(e, toks, False)

    def finish(self):
        self._wait("sp", self._all_tokens(), False)

    def emit(self):
        with self.nc.Block() as block:
            @block.sync
            def _(e):
                for f in self.q["sp"]:
                    f(e)

            @block.tensor
            def _(e):
                for f in self.q["pe"]:
                    f(e)

            @block.scalar
            def _(e):
                for f in self.q["act"]:
                    f(e)

            @block.vector
            def _(e):
                for f in self.q["dve"]:
                    f(e)

            @block.gpsimd
            def _(e):
                for f in self.q["pool"]:
                    f(e)


class Ring:
    def __init__(self, tiles):
        self.tiles = [(t, T()) for t in tiles]
        self.i = 0

    def next(self):
        r = self.tiles[self.i % len(self.tiles)]
        self.i += 1
        return r


def interleave(gens):
    gens = list(gens)
    while gens:
        nxt = []
        for g in gens:
            try:
                next(g)
                nxt.append(g)
            except StopIteration:
                pass
        gens = nxt


class Cfg:
    def __init__(self, TC=256, TL=4096, L=4, debug=False, stop=None, stop_layer=0):
        self.TC, self.TL, self.L, self.debug, self.stop, self.stop_layer = TC, TL, L, debug, stop, stop_layer


def build(cfg):
    TC, TL, L = cfg.TC, cfg.TL, cfg.L
    TT = TC + TL
    NCH = TT // 128
    NCC = TC // 128
    TP = TT + 6
    tiles = [(c0, min(512, TC - c0)) for c0 in range(0, TC, 512)] + [(c0, min(512, TT - c0)) for c0 in range(TC, TT, 512)]

    def ppos(c0):
        return c0 + 2 if c0 < TC else c0 + 4

    nc = bass.Bass("TRN2", target_bir_lowering=False)
    kind_dbg = "ExternalOutput" if cfg.debug else "Internal"

    def dram(name, shape, dt=F32, kind="ExternalInput"):
        return nc.dram_tensor(name, list(shape), dt, kind=kind).ap()

    xT_d = dram("xT", [D, TT])
    cond_d = dram("cond", [128, KC * 2])
    consts_d = dram("consts", [128, NCONST * 128])
    rope_d = dram("rope", [2, 128, TL])
    sp_d = dram("spar", [L, 128, NSP])
    rv_d = dram("rvec", [L, NRV])
    w_ada_d = dram("w_ada", [L, D, N_MOD * D])
    w_in_d = dram("w_in", [L, D, IN_WIDTH])
    w_br_d = dram("w_branch", [L, 3, D, D])
    w_out_d = dram("w_out", [L, D, D])
    lruw_d = dram("lruw", [L, 2, 2, KC, 128, 128])
    wq_d = dram("peer_wq", [L, D, 2048])
    skT_d = dram("skT", [L, 16, 128, 128])
    pu_d = dram("peer_u", [L, P_KEYS * P_KEYS, D])
    pv_d = dram("peer_v", [L, P_KEYS * P_KEYS, D])
    out_d = dram("outT", [D, TL], kind="ExternalOutput")
    X_d = dram("X", [D, TT], kind=kind_dbg)
    QKT_d = dram("QKT", [2048, TT], BF16, kind=kind_dbg)
    VTOK_d = dram("VTOK", [TT, D], BF16, kind=kind_dbg)
    GQKV_d = dram("GQKV", [3072, TT], kind=kind_dbg)
    ZS_d = dram("ZS", [TT, D], kind=kind_dbg)
    SG_d = dram("SG", [3072, TT], BF16, kind=kind_dbg)
    YA_d = dram("YA", [D, TT], BF16, kind=kind_dbg)
    YB_d = dram("YB", [D, TT], BF16, kind=kind_dbg)
    YR_d = dram("YR", [D, TT], BF16, kind=kind_dbg)
    OD_d = dram("OD", [2, TT, D], kind=kind_dbg)
    H2_d = dram("H2", [TT, D], kind=kind_dbg)
    HT_d = dram("HTdbg", [D, TT], BF16, kind=kind_dbg) if cfg.debug else None
    DBGF_d = dram("DBGF", [128, TT], kind="ExternalOutput") if cfg.debug else None
    DBGI_d = dram("DBGI", [128, TT], U32, kind="ExternalOutput") if cfg.debug else None
    TX, TQKT, TVTOK, TGQKV, TZS, TSG, TYA, TYB, TYR, TOD, TH2 = [T() for _ in range(11)]

    es = ExitStack()
    with es:
        S = Sched(nc, es)

        uniq = [0]

        def sbuf(stack, name, shape, dt=F32):
            uniq[0] += 1
            return stack.enter_context(nc.sbuf_tensor(f"{name}_{uniq[0]}", list(shape), dt))

        def ring(stack, name, n, shape, dt=F32):
            return Ring([sbuf(stack, f"{name}{i}", shape, dt) for i in range(n)])

        def act(out, in_, func, reads, writes, aw=(), **kw):
            return S.op("act", lambda e: e.activation(out=out, in_=in_, func=func, **kw), reads, writes, aw)

        def tt(eng, out, in0, in1, op, reads, writes, aw=()):
            return S.op(eng, lambda e: e.tensor_tensor(out=out, in0=in0, in1=in1, op=op), reads, writes, aw)

        def ts(eng, out, in0, s1, op0, reads, writes, s2=None, op1=None, aw=()):
            if op1 is None:
                return S.op(eng, lambda e: e.tensor_scalar(out=out, in0=in0, scalar1=s1, scalar2=None, op0=op0), reads, writes, aw)
            return S.op(eng, lambda e: e.tensor_scalar(out=out, in0=in0, scalar1=s1, scalar2=s2, op0=op0, op1=op1), reads, writes, aw)

        def stt(out, in0, scalar, in1, op0, op1, reads, writes, accum_out=None, aw=()):
            if accum_out is None:
                return S.op("dve", lambda e: e.scalar_tensor_tensor(out=out, in0=in0, scalar=scalar, in1=in1, op0=op0, op1=op1), reads, writes, aw)
            return S.op("dve", lambda e: e.scalar_tensor_tensor(out=out, in0=in0, scalar=scalar, in1=in1, op0=op0, op1=op1, accum_out=accum_out), reads, writes, aw)

        def cp(eng, out, in_, reads, writes, aw=()):
            if eng == "act":
                return act(out, in_, AF.Copy, reads, writes, aw)
            return S.op(eng, lambda e: e.tensor_copy(out=out, in_=in_), reads, writes, aw)

        def recip(out, in_, reads, writes):
            return S.op("dve", lambda e: e.reciprocal(out=out, in_=in_), reads, writes)

        def mm(out, lhsT, rhs, reads, writes, start=True, stop=True):
            return S.op("pe", lambda e: e.matmul(out, lhsT, rhs, start=start, stop=stop), reads, writes)

        def tr(out, in_, reads, writes):
            return S.op("pe", lambda e: e.transpose(out, in_, IDENT), list(reads) + [Tc], writes)

        def dma(q, out, in_, reads=(), writes=(), awrites=()):
            return S.dma(q, lambda e: e.dma_start(out=out, in_=in_), reads, writes, awrites)

        evc = [0]

        def evac(out, in_, reads, writes, aw=()):
            evc[0] += 1
            return cp("act" if evc[0] % 2 else "dve", out, in_, reads, writes, aw)

        PB = [(es.enter_context(nc.psum_tensor(f"pb{i}", [128, 512], F32)), T(excl=True)) for i in range(8)]
        psA = Ring([PB[i][0] for i in range(4)])
        psA.tiles = [PB[i] for i in range(4)]
        psB = Ring([PB[i][0] for i in range(4, 8)])
        psB.tiles = [PB[i] for i in range(4, 8)]
        psAll = Ring([PB[i][0] for i in range(8)])
        psAll.tiles = PB
        psQ = Ring([PB[0][0]])
        psQ.tiles = [(PB[k // 4][0][:, (k % 4) * 128:(k % 4 + 1) * 128], PB[k // 4][1]) for k in range(32)]

        consts = sbuf(es, "consts", [128, NCONST, 128])
        Tc = T()
        dma("sp", consts[:].rearrange("p c f -> p (c f)"), consts_d[:, :], writes=[Tc])
        IDENT, ONES, PERM = consts[:, C_ID, :], consts[:, C_ONES, :], consts[:, C_PERM, :]
        UI, LI, SL, SU = consts[:, C_UI, :], consts[:, C_LI, :], consts[:, C_SL, :], consts[:, C_SU, :]
        cbf = sbuf(es, "cbf", [128, 2, 128], BF16)
        cp("pool", cbf[:, 0, :], consts[:, C_ONES, :], [Tc], [Tc])
        cp("pool", cbf[:, 1, :], consts[:, C_BD64, :], [Tc], [Tc])
        ONES_BF, BD64_BF = cbf[:, 0, :], cbf[:, 1, :]
        cst = sbuf(es, "cst", [128, 4])
        S.op("pool", lambda e: e.memset(cst[:, 0:1], EPS), (), [Tc])
        S.op("pool", lambda e: e.memset(cst[:, 1:2], 1.0), (), [Tc])
        S.op("pool", lambda e: e.memset(cst[:, 2:3], 0.0), (), [Tc])
        EPS_AP, ONE_AP, ZERO_AP = cst[:, 0:1], cst[:, 1:2], cst[:, 2:3]
        scond = sbuf(es, "scond", [128, KC, 2])
        dma("sp", scond[:].rearrange("p k j -> p (k j)"), cond_d[:, :], writes=[Tc])
        act(scond[:], scond[:], AF.Silu, [Tc], [Tc])

        dma("sp", X_d[:, :], xT_d[:, :], awrites=[TX])
        S.barrier()

        Xv = X_d.rearrange("(kc p) t -> p kc t", p=128)

        for l in range(L):
            need_ctx = l < L - 1
            lam_init = 0.8 - 0.6 * math.exp(-0.3 * l)
            les = ExitStack()
            with les:
                Tp = T()
                spl = sbuf(les, "spl", [128, NSP])
                dma("sp", spl[:], sp_d[l], writes=[Tp])
                rvl = sbuf(les, "rvl", [128, NRV])
                dma("sp", rvl[:], rv_d[l:l + 1, :].to_broadcast([128, NRV]), writes=[Tp])
                mod = sbuf(les, "mod", [128, 2, 48])
                with ExitStack() as pes:
                    wring = ring(pes, "wada", 2, [128, KC, 512])
                    wav = w_ada_d[l].rearrange("(kc p) f -> p kc f", p=128)
                    pm, Tpm = PB[0]
                    for fg in range(12):
                        wt, Tw = wring.next()
                        dma("sp", wt[:], wav[:, :, fg * 512:(fg + 1) * 512], writes=[Tw])
                        for j in range(4):
                            fc = fg * 4 + j
                            for kc in range(KC):
                                mm(pm[:, 2 * fc:2 * fc + 2], wt[:, kc, j * 128:(j + 1) * 128], scond[:, kc, :], [Tw, Tc], [Tpm],
                                   start=(kc == 0), stop=(kc == KC - 1))
                    pmv = pm[:, 0:96].rearrange("p (f j) -> p j f", j=2)
                    for j in range(2):
                        tt("dve", mod[:, j, :], pmv[:, j, :], spl[:, SP["b_ada"]:SP["b_ada"] + 48], ALU.add, [Tpm, Tp], [Tp])
                S.barrier()
                gs1 = sbuf(les, "gs1", [128, 2, KC])
                gs2 = sbuf(les, "gs2", [128, 2, KC])
                for j in range(2):
                    stt(gs1[:, j, :], mod[:, j, 8:16], 1.0, spl[:, SP["n1g"]:SP["n1g"] + 8], ALU.add, ALU.mult, [Tp], [Tp])
                    stt(gs2[:, j, :], mod[:, j, 32:40], 1.0, spl[:, SP["n2g"]:SP["n2g"] + 8], ALU.add, ALU.mult, [Tp], [Tp])
                sh1 = lambda j, kc: mod[:, j, 0 + kc:1 + kc]
                gt1 = lambda j, kc: mod[:, j, 16 + kc:17 + kc]
                sh2 = lambda j, kc: mod[:, j, 24 + kc:25 + kc]
                gt2 = lambda j, kc: mod[:, j, 40 + kc:41 + kc]
                lamt = sbuf(les, "lamt", [128, 8])
                tt("dve", lamt[0:64, 0:1], spl[0:64, SP["lam"]:SP["lam"] + 1], spl[0:64, SP["lam"] + 1:SP["lam"] + 2], ALU.mult, [Tp], [Tp])
                tt("dve", lamt[0:64, 1:2], spl[0:64, SP["lam"] + 2:SP["lam"] + 3], spl[0:64, SP["lam"] + 3:SP["lam"] + 4], ALU.mult, [Tp], [Tp])
                pl_, Tpl = PB[1]
                mm(pl_[:, 0:2], consts[0:64, C_ONES, :], lamt[0:64, 0:2], [Tp, Tc], [Tpl])
                act(lamt[:, 2:4], pl_[:, 0:2], AF.Exp, [Tpl], [Tp])
                tt("dve", lamt[:, 4:5], lamt[:, 3:4], lamt[:, 2:3], ALU.subtract, [Tp], [Tp])
                ts("dve", lamt[:, 5:6], lamt[:, 4:5], -lam_init, ALU.add, [Tp], [Tp])
                NEGLAM = lamt[:, 5:6]
                ts("dve", lamt[:, 6:7], spl[:, SP["gq"]:SP["gq"] + 1], 0.125, ALU.mult, [Tp], [Tp])
                ts("dve", lamt[:, 7:8], spl[:, SP["subg"]:SP["subg"] + 1], 1.0 - lam_init, ALU.mult, [Tp], [Tp])
                GQ, GK, SUBG = lamt[:, 6:7], spl[:, SP["gk"]:SP["gk"] + 1], lamt[:, 7:8]

                BETA = sbuf(les, "BETA", [128, NCH, 16])
                NEGB = sbuf(les, "NEGB", [128, NCH, 16])
                GG = sbuf(les, "GG", [128, NCH, 16])
                EG = sbuf(les, "EG", [128, NCH, 3, 16])
                BGC = sbuf(les, "BGC", [128, NCH, 16])
                TG = T()

                def norm_mod(pes, tag, gs, shf, out_fn, rings):
                    xr, sqr, rtr, tmr = rings
                    for (c0, n) in tiles:
                        j = 1 if c0 < TC else 0
                        xt_, Tx_ = xr.next()
                        dma("sp", xt_[:, :, :n], Xv[:, :, c0:c0 + n], reads=[TX], writes=[Tx_])
                        sq_, Tsq = sqr.next()
                        act(sq_[:, :, :n], xt_[:, :, :n], AF.Square, [Tx_], [Tsq])
                        pn, Tpn = psB.next()
                        for kc in range(KC):
                            mm(pn[:, :n], ONES_BF, sq_[:, kc, :n], [Tsq, Tc], [Tpn], start=(kc == 0), stop=(kc == KC - 1))
                        rt_, Trt = rtr.next()
                        act(rt_[:, :n], pn[:, :n], AF.Sqrt, [Tpn, Tc], [Trt], scale=1.0 / D, bias=EPS_AP)
                        recip(rt_[:, :n], rt_[:, :n], [Trt], [Trt])
                        tm_, Ttm = tmr.next()
                        tt("dve", tm_[:, :, :n], xt_[:, :, :n], rt_[:, :n].unsqueeze(1).to_broadcast([128, KC, n]), ALU.mult, [Tx_, Trt], [Ttm])
                        out_fn(c0, n, j, tm_, Ttm, gs, shf)

                pes = ExitStack()
                with pes:
                    hT = sbuf(pes, "hT", [128, KC, TT], BF16)
                    ThT = T()
                    with ExitStack() as nes:
                        rings = (ring(nes, "nx", 2, [128, KC, 512]), ring(nes, "nsq", 2, [128, KC, 512], BF16),
                                 ring(nes, "nrt", 2, [128, 512]), ring(nes, "ntm", 2, [128, KC, 512]))

                        def out_h(c0, n, j, tm_, Ttm, gs, shf):
                            for kc in range(KC):
                                act(hT[:, kc, c0:c0 + n], tm_[:, kc, :n], AF.Identity, [Ttm, Tp], (), aw=[ThT],
                                    scale=gs[:, j, kc:kc + 1], bias=shf(j, kc))
                        norm_mod(nes, "n1", gs1, sh1, out_h, rings)
                    S.barrier()
                    if cfg.debug:
                        dma("sp", HT_d.rearrange("(kc p) t -> p kc t", p=128), hT[:], reads=[ThT])
                    if cfg.stop == "norm1" and l == cfg.stop_layer:
                        break

                    wiv = w_in_d[l].rearrange("(kc p) f -> p kc f", p=128)
                    wst = ring(pes, "wst", 2, [128, KC, 128])
                    wbf = ring(pes, "wbf", 2, [128, KC, 128], BF16)

                    def load_w(col0, ncols):
                        ws, Tws = wst.next()
                        dma("sp", ws[:, :, :ncols], wiv[:, :, col0:col0 + ncols], writes=[Tws])
                        wb, Twb = wbf.next()
                        cp("pool", wb[:, :, :ncols], ws[:, :, :ncols], [Tws], [Twb])
                        return wb, Twb

                    def proj_feat(wb, Twb, ncols, c0, n, pring=psA):
                        pp, Tpp = pring.next()
                        for kc in range(KC):
                            mm(pp[:ncols, :n], wb[:, kc, :ncols], hT[:, kc, c0:c0 + n], [Twb, ThT], [Tpp], start=(kc == 0), stop=(kc == KC - 1))
                        return pp, Tpp

                    TQKT.new_version()
                    with ExitStack() as aes:
                        qfr = ring(aes, "qf", 2, [128, 512])
                        sqr = ring(aes, "asq", 2, [128, 512], BF16)
                        rtr = ring(aes, "art", 2, [128, 512])
                        qnr = ring(aes, "qn", 2, [128, 512])
                        csr = ring(aes, "cs", 2, [128, 2, 512])
                        t1r = ring(aes, "t1", 2, [128, 512])
                        t2r = ring(aes, "t2", 2, [128, 512])
                        obr = ring(aes, "qob", 3, [128, 512], BF16)
                        CUT = int(os.environ.get("K_CUT", "99"))
                        for ch in range(16):
                            wb, Twb = load_w(ch * 128, 128)
                            gsc = GQ if ch < 8 else GK
                            for (c0, n) in tiles:
                                if CUT < 2:
                                    continue
                                pp, Tpp = proj_feat(wb, Twb, 128, c0, n)
                                if CUT < 3:
                                    continue
                                qf, Tqf = qfr.next()
                                cp("dve", qf[:, :n], pp[:, :n], [Tpp], [Tqf])
                                sq, Tsq = sqr.next()
                                act(sq[:, :n], qf[:, :n], AF.Square, [Tqf], [Tsq])
                                if CUT < 4:
                                    continue
                                pb, Tpb = psB.next()
                                mm(pb[:, :n], BD64_BF, sq[:, :n], [Tsq, Tc], [Tpb])
                                rt, Trt = rtr.next()
                                act(rt[:, :n], pb[:, :n], AF.Sqrt, [Tpb, Tc], [Trt], scale=1.0 / A_DIM, bias=EPS_AP)
                                recip(rt[:, :n], rt[:, :n], [Trt], [Trt])
                                if CUT < 5:
                                    continue
                                qn, Tqn = qnr.next()
                                stt(qn[:, :n], qf[:, :n], gsc, rt[:, :n], ALU.mult, ALU.mult, [Tqf, Trt, Tp], [Tqn])
                                if CUT < 6:
                                    continue
                                ob, Tob = obr.next()
                                if c0 >= TC and not os.environ.get('K_NOROPE'):
                                    cs, Tcs = csr.next()
                                    dma("sp", cs[:, :, :n], rope_d[:, :, c0 - TC:c0 - TC + n].rearrange("c p t -> p c t"), writes=[Tcs])
                                    pr, Tpr = psB.next()
                                    mm(pr[:, :n], PERM, qn[:, :n], [Tqn, Tc], [Tpr])
                                    t1, Tt1 = t1r.next()
                                    tt("dve", t1[:, :n], qn[:, :n], cs[:, 0, :n], ALU.mult, [Tqn, Tcs], [Tt1])
                                    t2, Tt2 = t2r.next()
                                    tt("dve", t2[:, :n], pr[:, :n], cs[:, 1, :n], ALU.mult, [Tpr, Tcs], [Tt2])
                                    tt("pool", ob[:, :n], t1[:, :n], t2[:, :n], ALU.add, [Tt1, Tt2], [Tob])
                                else:
                                    cp("act", ob[:, :n], qn[:, :n], [Tqn], [Tob])
                                dma("sp", QKT_d[ch * 128:(ch + 1) * 128, c0:c0 + n], ob[:, :n], reads=[Tob], awrites=[TQKT])
                    S.barrier()
                    if cfg.stop == "attnprep" and l == cfg.stop_layer:
                        break

                    tes = ExitStack()
                    wst5 = ring(tes, "wst5", 1, [128, KC, 512])
                    wbf5 = ring(tes, "wbf5", 2, [128, KC, 512], BF16)

                    def load_w5(col0, ncols):
                        ws, Tws = wst5.next()
                        dma("sp", ws[:, :, :ncols], wiv[:, :, col0:col0 + ncols], writes=[Tws])
                        wb, Twb = wbf5.next()
                        cp("pool", wb[:, :, :ncols], ws[:, :, :ncols], [Tws], [Twb])
                        return wb, Twb

                    def proj_tok(wb, Twb, ncols, tk):
                        pp, Tpp = psA.next()
                        for kc in range(KC):
                            mm(pp[:, :ncols], hT[:, kc, tk * 128:(tk + 1) * 128], wb[:, kc, :ncols], [Twb, ThT], [Tpp], start=(kc == 0), stop=(kc == KC - 1))
                        return pp, Tpp

                    TVTOK.new_version()
                    with ExitStack() as ves:
                        vob = ring(ves, "vob", 3, [128, 512], BF16)
                        for jb in range(2):
                            wb, Twb = load_w5(OFF_AV + jb * 512, 512)
                            for tk in range(NCH):
                                pp, Tpp = proj_tok(wb, Twb, 512, tk)
                                ob, Tob = vob.next()
                                evac(ob[:], pp[:], [Tpp], [Tob])
                                dma("sp", VTOK_d[tk * 128:(tk + 1) * 128, jb * 512:(jb + 1) * 512], ob[:], reads=[Tob], awrites=[TVTOK])
                    S.barrier()
                    TZS.new_version()
                    with ExitStack() as ves:
                        zob = ring(ves, "zob", 3, [128, 512])
                        for jb in range(2):
                            wb, Twb = load_w5(OFF_Z + jb * 512, 512)
                            for tk in range(NCH):
                                pp, Tpp = proj_tok(wb, Twb, 512, tk)
                                ob, Tob = zob.next()
                                act(ob[:], pp[:], AF.Silu, [Tpp], [Tob])
                                dma("sp", ZS_d[tk * 128:(tk + 1) * 128, jb * 512:(jb + 1) * 512], ob[:], reads=[Tob], awrites=[TZS])
                    S.barrier()
                    with ExitStack() as ves:
                        negea = sbuf(ves, "negea", [128, 16])
                        Tne = T()
                        act(negea[:], rvl[:, 0:16], AF.Exp, [Tp], [Tne])
                        ts("dve", negea[:], negea[:], -1.0, ALU.mult, [Tne], [Tne])
                        xar = ring(ves, "xa", 2, [128, 16])
                        wb, Twb = load_w5(OFF_BETA, 32)
                        for tk in range(NCH):
                            pp, Tpp = proj_tok(wb, Twb, 32, tk)
                            act(BETA[:, tk, :], pp[:, 0:16], AF.Sigmoid, [Tpp], (), aw=[TG])
                            xa, Txa = xar.next()
                            tt("dve", xa[:], pp[:, 16:32], rvl[:, 16:32], ALU.add, [Tpp, Tp], [Txa])
                            act(xa[:], xa[:], AF.Exp, [Txa], [Txa])
                            act(xa[:], xa[:], AF.Ln, [Txa, Tc], [Txa], bias=ONE_AP)
                            tt("dve", GG[:, tk, :], xa[:], negea[:], ALU.mult, [Txa, Tne], (), aw=[TG])
                            ts("dve", NEGB[:, tk, :], BETA[:, tk, :], -1.0, ALU.mult, [TG], (), aw=[TG])
                            pg, Tpg = psB.next()
                            pgv = pg[:, 0:48].rearrange("p (a b) -> p a b", a=3)
                            for d in range(2):
                                A_, R_ = (UI, SL) if d == 0 else (LI, SU)
                                gsl = GG[:, tk, d * 8:(d + 1) * 8]
                                mm(pgv[:, 0, d * 8:(d + 1) * 8], A_, gsl, [TG, Tc], [Tpg])
                                mm(pgv[:, 1, d * 8:(d + 1) * 8], R_, gsl, [TG, Tc], [Tpg])
                                mm(pgv[:, 2, d * 8:(d + 1) * 8], ONES, gsl, [TG, Tc], [Tpg])
                            act(EG[:, tk, :, :], pgv, AF.Exp, [Tpg], (), aw=[TG])
                            tt("dve", BGC[:, tk, :], BETA[:, tk, :], EG[:, tk, 0, :], ALU.mult, [TG], (), aw=[TG])
                    if cfg.debug and cfg.stop in ("prep1", "gdn", "gdnprep") and l == cfg.stop_layer:
                        o_ = 0
                        for arr, w_ in ((GG, NCH * 16), (BETA, NCH * 16), (BGC, NCH * 16), (NEGB, NCH * 16), (EG, NCH * 48)):
                            dma("sp", DBGF_d[:, o_:o_ + w_], arr[:].rearrange("p a b -> p (a b)") if arr is not EG else arr[:].rearrange("p a b c -> p (a b c)"), reads=[TG])
                            o_ += w_
                    tes.close()
                    S.barrier()
                    TSG.new_version()
                    with ExitStack() as ves:
                        sgo = ring(ves, "sgo", 3, [128, 512], BF16)
                        for ch in range(24):
                            wb, Twb = load_w(OFF_MG + ch * 128, 128)
                            for (c0, n) in tiles:
                                pp, Tpp = proj_feat(wb, Twb, 128, c0, n)
                                ob, Tob = sgo.next()
                                act(ob[:, :n], pp[:, :n], AF.Sigmoid, [Tpp], [Tob])
                                dma("sp", SG_d[ch * 128:(ch + 1) * 128, c0:c0 + n], ob[:, :n], reads=[Tob], awrites=[TSG])
                    S.barrier()
                    if cfg.stop == "prep1" and l == cfg.stop_layer:
                        break
                    raw = sbuf(pes, "raw", [128, TP])
                    acc = sbuf(pes, "acc", [128, TP])
                    Traw, Tacc = T(), T()
                    S.op("pool", lambda e, o=raw[:]: e.memset(o, 0.0), (), [Traw])
                    S.op("pool", lambda e, o=acc[:]: e.memset(o, 0.0), (), [Tacc])

                    def fill_raw(col0):
                        wb, Twb = load_w(col0, 128)
                        Traw.new_version()
                        for (c0, n) in tiles:
                            pp, Tpp = proj_feat(wb, Twb, 128, c0, n)
                            evac(raw[:, ppos(c0):ppos(c0) + n], pp[:, :n], [Tpp], (), aw=[Traw])

                    def conv(wcol, bias_ap):
                        a_out = acc[:, 2:TP - 2]
                        if bias_ap is None:
                            act(a_out, raw[:, 0:TP - 4], AF.Identity, [Traw, Tp], [Tacc], scale=spl[:, wcol:wcol + 1], bias=ZERO_AP)
                        else:
                            act(a_out, raw[:, 0:TP - 4], AF.Identity, [Traw, Tp], [Tacc], scale=spl[:, wcol:wcol + 1], bias=bias_ap)
                        for j in range(1, 5):
                            stt(a_out, raw[:, j:TP - 4 + j], spl[:, wcol + j:wcol + j + 1], a_out, ALU.mult, ALU.add, [Traw, Tp, Tacc], [Tacc])

                    TGQKV.new_version()
                    with ExitStack() as ves:
                        sqr = ring(ves, "gsq", 2, [128, 512], BF16)
                        rtr = ring(ves, "grt", 2, [128, 512])
                        gor = ring(ves, "gout", 3, [128, 512])
                        for ch in range(24):
                            typ = ch // 8
                            fill_raw(OFF_BQKV + ch * 128)
                            conv(SP["gconv"] + ch * 5, None)
                            act(acc[:, 2:TP - 2], acc[:, 2:TP - 2], AF.Silu, [Tacc], [Tacc])
                            for (c0, n) in tiles:
                                src = acc[:, ppos(c0):ppos(c0) + n]
                                if typ == 2:
                                    dma("sp", GQKV_d[ch * 128:(ch + 1) * 128, c0:c0 + n], src, reads=[Tacc], awrites=[TGQKV])
                                    continue
                                sq, Tsq = sqr.next()
                                act(sq[:, :n], src, AF.Square, [Tacc], [Tsq])
                                pb, Tpb = psB.next()
                                mm(pb[:, :n], ONES_BF, sq[:, :n], [Tsq, Tc], [Tpb])
                                rt, Trt = rtr.next()
                                act(rt[:, :n], pb[:, :n], AF.Sqrt, [Tpb, Tc], [Trt], scale=1.0, bias=EPS_AP)
                                recip(rt[:, :n], rt[:, :n], [Trt], [Trt])
                                go, Tgo = gor.next()
                                stt(go[:, :n], src, (B_DIM ** -0.5) if typ == 0 else 1.0, rt[:, :n], ALU.mult, ALU.mult, [Tacc, Trt], [Tgo])
                                dma("sp", GQKV_d[ch * 128:(ch + 1) * 128, c0:c0 + n], go[:, :n], reads=[Tgo], awrites=[TGQKV])
                    S.barrier()
                    if cfg.stop == "gdnprep" and l == cfg.stop_layer:
                        break
                    TYR.new_version()
                    with ExitStack() as ves:
                        hsum = sbuf(ves, "hsum", [128, TT])
                        Ths = [T() for _ in tiles]
                        cn = sbuf(ves, "cneg", [128, 3, 16])
                        Tcn = T()
                        act(cn[:, 0, :], spl[:, SP["llam"]:SP["llam"] + 16], AF.Exp, [Tp], [Tcn], scale=-1.0)
                        act(cn[:, 0, :], cn[:, 0, :], AF.Ln, [Tcn, Tc], [Tcn], bias=ONE_AP)
                        ts("dve", cn[:, 1, :], cn[:, 0, :], -2.0 * LRU_C, ALU.mult, [Tcn], [Tcn])
                        ts("dve", cn[:, 2, :], cn[:, 0, :], LRU_C, ALU.mult, [Tcn], [Tcn])
                        ts("dve", cn[:, 0, :], cn[:, 0, :], -LRU_C, ALU.mult, [Tcn], [Tcn])
                        lwr = ring(ves, "lw", 2, [128, 2, 2, 128])
                        rr = ring(ves, "lr", 2, [128, 512])
                        ir = ring(ves, "li", 2, [128, 512])
                        ar = ring(ves, "la", 2, [128, 512])
                        a2r = ring(ves, "la2", 2, [128, 512])
                        thr = ring(ves, "lth", 2, [128, 512])
                        hbr = ring(ves, "lhb", 2, [128, 512])
                        glr = ring(ves, "lgl", 2, [128, 512])
                        yor = ring(ves, "lyo", 3, [128, 512], BF16)
                        ctiles = [t_ for t_ in enumerate(tiles) if t_[1][0] < TC]
                        ltiles = [t_ for t_ in enumerate(tiles) if t_[1][0] >= TC]
                        for c in range(KC):
                            fill_raw(OFF_LX + c * 128)
                            conv(SP["lconv"] + c * 5, spl[:, SP["lconvb"] + c:SP["lconvb"] + c + 1])
                            lw, Tlw = lwr.next()
                            for d in range(2):
                                for ri in range(2):
                                    dma("sp", lw[:, d, ri, :], lruw_d[l, d, ri, c], writes=[Tlw] if (d == 0 and ri == 0) else (), awrites=() if (d == 0 and ri == 0) else [Tlw])
                            for d in range(2):
                                order = (ctiles + ltiles) if d == 0 else (ctiles[::-1] + ltiles[::-1])
                                carry, Tcar = None, None
                                col = d * 8 + c
                                for (ti, (c0, n)) in order:
                                    xs = acc[:, ppos(c0):ppos(c0) + n]
                                    pr, Tpr = psA.next()
                                    mm(pr[:, :n], lw[:, d, 0, :], xs, [Tlw, Tacc], [Tpr])
                                    pi, Tpi = psA.next()
                                    mm(pi[:, :n], lw[:, d, 1, :], xs, [Tlw, Tacc], [Tpi])
                                    r_, Tr = rr.next()
                                    act(r_[:, :n], pr[:, :n], AF.Sigmoid, [Tpr, Tp], [Tr], bias=spl[:, SP["lbr"] + col:SP["lbr"] + col + 1])
                                    i_, Ti = ir.next()
                                    act(i_[:, :n], pi[:, :n], AF.Sigmoid, [Tpi, Tp], [Ti], bias=spl[:, SP["lbi"] + col:SP["lbi"] + col + 1])
                                    a_, Ta = ar.next()
                                    act(a_[:, :n], r_[:, :n], AF.Exp, [Tr, Tcn], [Ta], scale=cn[:, 0, col:col + 1])
                                    a2, Ta2 = a2r.next()
                                    act(a2[:, :n], r_[:, :n], AF.Exp, [Tr, Tcn], [Ta2], scale=cn[:, 1, col:col + 1])
                                    th, Tth = thr.next()
                                    act(th[:, :n], r_[:, :n], AF.Tanh, [Tr, Tcn], [Tth], scale=cn[:, 2, col:col + 1])
                                    stt(a2[:, :n], a2[:, :n], 1.0, th[:, :n], ALU.add, ALU.mult, [Ta2, Tth], [Ta2])
                                    act(a2[:, :n], a2[:, :n], AF.Sqrt, [Ta2], [Ta2])
                                    tt("dve", i_[:, :n], i_[:, :n], a2[:, :n], ALU.mult, [Ti, Ta2], [Ti])
                                    tt("pool", i_[:, :n], i_[:, :n], xs, ALU.mult, [Ti, Tacc], [Ti])
                                    init = 0.0 if carry is None else carry
                                    rds = [Ta, Ti] + ([Tcar] if Tcar is not None else [])
                                    if d == 0:
                                        S.op("dve", lambda e, o=hsum[:, c0:c0 + n], d0=a_[:, :n], d1=i_[:, :n], ini=init:
                                             e.tensor_tensor_scan(out=o, data0=d0, data1=d1, initial=ini, op0=ALU.mult, op1=ALU.add), rds, [Ths[ti]])
                                        carry, Tcar = hsum[:, c0 + n - 1:c0 + n], Ths[ti]
                                    else:
                                        hb, Thb = hbr.next()
                                        S.op("dve", lambda e, o=hb[:, :n][:, ::-1], d0=a_[:, :n][:, ::-1], d1=i_[:, :n][:, ::-1], ini=init:
                                             e.tensor_tensor_scan(out=o, data0=d0, data1=d1, initial=ini, op0=ALU.mult, op1=ALU.add), rds, [Thb])
                                        carry, Tcar = hb[:, 0:1], Thb
                                        tt("pool", hsum[:, c0:c0 + n], hsum[:, c0:c0 + n], hb[:, :n], ALU.add, [Ths[ti], Thb], [Ths[ti]])
                            wb, Twb = load_w(OFF_LG + c * 128, 128)
                            for ti, (c0, n) in enumerate(tiles):
                                pp, Tpp = proj_feat(wb, Twb, 128, c0, n)
                                gl, Tgl = glr.next()
                                act(gl[:, :n], pp[:, :n], AF.Gelu_apprx_tanh, [Tpp], [Tgl])
                                yo, Tyo = yor.next()
                                tt("dve", yo[:, :n], gl[:, :n], hsum[:, c0:c0 + n], ALU.mult, [Tgl, Ths[ti]], [Tyo])
                                dma("sp", YR_d[c * 128:(c + 1) * 128, c0:c0 + n], yo[:, :n], reads=[Tyo], awrites=[TYR])
                S.barrier()
                if cfg.stop == "prep" and l == cfg.stop_layer:
                    break
                TYA.new_version()
                with ExitStack() as aes:
                    kTr = ring(aes, "kTh", 2, [128, TT], BF16)
                    Vr = ring(aes, "Vh", 2, [128, NCH, 128], BF16)
                    qbr = ring(aes, "qb", 2, [128, 512], BF16)
                    ptr = ring(aes, "pt", 4, [128, 512], BF16)
                    r0r = ring(aes, "ar0", 2, [128, 512])
                    t0r = ring(aes, "at0", 2, [128, 512])
                    t1r = ring(aes, "at1", 2, [128, 512])
                    sqr = ring(aes, "asq2", 2, [128, 512], BF16)
                    yar = ring(aes, "aya", 2, [128, 512], BF16)
                    VTv = VTOK_d.rearrange("(n p) e -> p n e", p=128)
                    for h in range(A_HEADS):
                        kT, TkT = kTr.next()
                        dma("sp", kT[:], QKT_d[1024 + h * 128:1024 + (h + 1) * 128, :], reads=[TQKT], writes=[TkT])
                        V, TV = Vr.next()
                        TV.new_version()
                        for k0_ in range(0, NCH, 8):
                            k1_ = min(NCH, k0_ + 8)
                            dma("sp", V[:, k0_:k1_, :], VTv[:, k0_:k1_, h * 128:(h + 1) * 128], reads=[TVTOK], awrites=[TV])
                        blocks = ([(0, TC, 0, NCC)] if need_ctx else []) + [(c0, n, 0, NCH) for (c0, n) in tiles if c0 >= TC]
                        for (q0, nq, k0, k1) in blocks:
                            qb, Tqb = qbr.next()
                            dma("sp", qb[:, :nq], QKT_d[h * 128:(h + 1) * 128, q0:q0 + nq], reads=[TQKT], writes=[Tqb])
                            (O0, TO0), (Z0, TZ0), (O1, TO1), (Z1, TZ1) = PB[0], PB[1], PB[2], PB[3]
                            OZ = ((O0, TO0, Z0, TZ0), (O1, TO1, Z1, TZ1))
                            for kt in range(k0, k1):
                                for c in range(2):
                                    st, Tst = psB.next()
                                    mm(st[:, :nq], kT[c * 64:(c + 1) * 64, kt * 128:(kt + 1) * 128], qb[c * 64:(c + 1) * 64, :nq], [TkT, Tqb], [Tst])
                                    pt, Tpt = ptr.next()
                                    act(pt[:, :nq], st[:, :nq], AF.Exp, [Tst], [Tpt])
                                    O_, TO_, Z_, TZ_ = OZ[c]
                                    mm(O_[:, :nq], V[:, kt, :], pt[:, :nq], [TV, Tpt], [TO_], start=(kt == k0), stop=(kt == k1 - 1))
                                    mm(Z_[:, :nq], ONES_BF, pt[:, :nq], [Tpt, Tc], [TZ_], start=(kt == k0), stop=(kt == k1 - 1))
                            r0, Tr0 = r0r.next()
                            recip(r0[:, :nq], Z0[:, :nq], [TZ0], [Tr0])
                            t0, Tt0 = t0r.next()
                            tt("dve", t0[:, :nq], O0[:, :nq], r0[:, :nq], ALU.mult, [TO0, Tr0], [Tt0])
                            r1, Tr1 = r0r.next()
                            recip(r1[:, :nq], Z1[:, :nq], [TZ1], [Tr1])
                            t1, Tt1 = t1r.next()
                            tt("dve", t1[:, :nq], O1[:, :nq], r1[:, :nq], ALU.mult, [TO1, Tr1], [Tt1])
                            stt(t0[:, :nq], t1[:, :nq], NEGLAM, t0[:, :nq], ALU.mult, ALU.add, [Tt1, Tt0, Tp], [Tt0])
                            sq, Tsq = sqr.next()
                            act(sq[:, :nq], t0[:, :nq], AF.Square, [Tt0], [Tsq])
                            pss, Tpss = psB.next()
                            mm(pss[:, :nq], ONES_BF, sq[:, :nq], [Tsq, Tc], [Tpss])
                            act(r1[:, :nq], pss[:, :nq], AF.Sqrt, [Tpss, Tc], [Tr1], scale=1.0 / 128.0, bias=EPS_AP)
                            recip(r1[:, :nq], r1[:, :nq], [Tr1], [Tr1])
                            ya, Tya = yar.next()
                            stt(ya[:, :nq], t0[:, :nq], SUBG, r1[:, :nq], ALU.mult, ALU.mult, [Tt0, Tr1, Tp], [Tya])
                            dma("sp", YA_d[h * 128:(h + 1) * 128, q0:q0 + nq], ya[:, :nq], reads=[Tya], awrites=[TYA])
                S.barrier()
                if cfg.stop == "attn" and l == cfg.stop_layer:
                    break

                TOD.new_version()
                for d in range(2):
                    with ExitStack() as ges:
                        A_, B_, Ms, MiT = (UI, SL, SL, UI) if d == 0 else (LI, SU, SU, LI)
                        order = list(range(NCH)) if d == 0 else (list(range(NCC - 1, -1, -1)) + list(range(NCH - 1, NCC - 1, -1)))

                        def chain(h):
                            col = d * 8 + h
                            nm = f"g{d}{h}"
                            mk = lambda n_: (sbuf(ges, nm + n_, [128, 128]), T())
                            (St, TS), (qT, TqT), (kT, TkT), (vT, TvT) = mk("S"), mk("q"), mk("k"), mk("v")
                            (g2, Tg2), (e_, Te), (et_, Tet), (Ru, TRu), (Rw, TRw), (kd, Tkd) = mk("g2"), mk("e"), mk("et"), mk("Ru"), mk("Rw"), mk("kd")
                            (N0, TN0), (NT0, TNT0), (N1, TN1), (NT1, TNT1) = mk("N0"), mk("NT0"), mk("N1"), mk("NT1")
                            (P0, TP0), (P1, TP1), (qkT, Tqk) = mk("P0"), mk("P1"), mk("qk")
                            (D2, TD2), (E2, TE2), (Ysb, TY), (Zsb, TZ) = mk("D2"), mk("E2"), mk("Ysb"), mk("Zsb")
                            (u_, Tu), (wT, TwT), (vn, Tvn), (o1s, To1), (oo, Too) = mk("u"), mk("wT"), mk("vn"), mk("o1"), mk("oo")
                            S.op("pool", lambda e, o=St[:]: e.memset(o, 0.0), (), [TS])
                            Tb_ = PB[h][1]
                            Q = [PB[h][0][:, i_ * 128:(i_ + 1) * 128] for i_ in range(4)]
                            yield
                            for n in order:
                                c0 = n * 128
                                dma("sp", qT[:], GQKV_d[h * 128:(h + 1) * 128, c0:c0 + 128], reads=[TGQKV], writes=[TqT])
                                dma("sp", kT[:], GQKV_d[1024 + h * 128:1024 + (h + 1) * 128, c0:c0 + 128], reads=[TGQKV], writes=[TkT])
                                dma("sp", vT[:], GQKV_d[2048 + h * 128:2048 + (h + 1) * 128, c0:c0 + 128], reads=[TGQKV], writes=[TvT])
                                ts("pool", g2[:], B_, GG[:, n, col:col + 1], ALU.mult, [TG, Tc], [Tg2])
                                yield
                                pk, pv, pd, pdt = Q
                                tr(pk, kT[:], [TkT], [Tb_])
                                tr(pv, vT[:], [TvT], [Tb_])
                                mm(pd, A_, g2[:], [Tg2, Tc], [Tb_])
                                mm(pdt, g2[:], A_, [Tg2, Tc], [Tb_])
                                yield
                                act(e_[:], pd, AF.Exp, [Tb_], [Te])
                                act(et_[:], pdt, AF.Exp, [Tb_], [Tet])
                                act(kd[:], pk, AF.Identity, [Tb_, TG, Tc], [Tkd], scale=EG[:, n, 1, col:col + 1], bias=ZERO_AP)
                                yield
                                ts("dve", Ru[:], pv, BETA[:, n, col:col + 1], ALU.mult, [Tb_, TG], [TRu])
                                ts("dve", Rw[:], pk, BGC[:, n, col:col + 1], ALU.mult, [Tb_, TG], [TRw])
                                tt("pool", e_[:], e_[:], Ms, ALU.mult, [Te, Tc], [Te])
                                tt("pool", et_[:], et_[:], MiT, ALU.mult, [Tet, Tc], [Tet])
                                yield
                                pkk, pkq = Q[0], Q[1]
                                mm(pkk, kT[:], kT[:], [TkT], [Tb_])
                                mm(pkq, kT[:], qT[:], [TkT, TqT], [Tb_])
                                yield
                                stt(N0[:], pkk, NEGB[:, n, col:col + 1], e_[:], ALU.mult, ALU.mult, [Tb_, TG, Te], [TN0])
                                tt("dve", qkT[:], pkq, et_[:], ALU.mult, [Tb_, Tet], [Tqk])
                                yield
                                pn = Q[2]
                                tr(pn, N0[:], [TN0], [Tb_])
                                yield
                                cp("act", NT0[:], pn, [Tb_], [TNT0])
                                yield
                                OFFK = [consts[:, C_OFF + k_, :] for k_ in range(7)]
                                tt("pool", N1[:], N0[:], OFFK[0], ALU.mult, [TN0, Tc], [TN1])
                                tt("pool", NT1[:], NT0[:], OFFK[0], ALU.mult, [TNT0, Tc], [TNT1])
                                yield
                                tt("pool", P0[:], N1[:], IDENT, ALU.add, [TN1, Tc], [TP0])
                                tt("pool", P1[:], NT1[:], IDENT, ALU.add, [TNT1, Tc], [TP1])
                                Dc, Ec = (P0, TP0), (P1, TP1)
                                Dn, En = (D2, TD2), (E2, TE2)
                                for k in range(1, 7):
                                    last = (k == 6)
                                    tt("pool", N1[:], N0[:], OFFK[k], ALU.mult, [TN0, Tc], [TN1])
                                    if not last:
                                        tt("pool", NT1[:], NT0[:], OFFK[k], ALU.mult, [TNT0, Tc], [TNT1])
                                    yield
                                    mm(Q[0], N1[:], Ec[0][:], [TN1, Ec[1]], [Tb_])
                                    if not last:
                                        mm(Q[1], NT1[:], Dc[0][:], [TNT1, Dc[1]], [Tb_])
                                    yield
                                    cp("act", Ysb[:], Q[0], [Tb_], [TY])
                                    if not last:
                                        cp("dve", Zsb[:], Q[1], [Tb_], [TZ])
                                    yield
                                    mm(Q[2], Dc[0][:], Ysb[:], [Dc[1], TY], [Tb_])
                                    if not last:
                                        mm(Q[3], Ec[0][:], Zsb[:], [Ec[1], TZ], [Tb_])
                                    yield
                                    tt("dve", En[0][:], Q[2], Ec[0][:], ALU.add, [Tb_, Ec[1]], [En[1]])
                                    if not last:
                                        tt("dve", Dn[0][:], Q[3], Dc[0][:], ALU.add, [Tb_, Dc[1]], [Dn[1]])
                                    Dc, Dn = Dn, Dc
                                    Ec, En = En, Ec
                                    yield
                                Pc = Ec
                                PI, TPI = Pc
                                pu, pw_ = Q[0], Q[1]
                                mm(pu, PI[:], Ru[:], [TPI, TRu], [Tb_])
                                mm(pw_, Rw[:], PI[:], [TPI, TRw], [Tb_])
                                yield
                                cp("act", u_[:], pu, [Tb_], [Tu])
                                cp("act", wT[:], pw_, [Tb_], [TwT])
                                yield
                                pa, po = Q[2], Q[3]
                                mm(pa, wT[:], St[:], [TwT, TS], [Tb_])
                                mm(po, qT[:], St[:], [TqT, TS], [Tb_])
                                yield
                                tt("dve", vn[:], u_[:], pa, ALU.subtract, [Tu, Tb_], [Tvn])
                                ts("dve", o1s[:], po, EG[:, n, 0, col:col + 1], ALU.mult, [Tb_, TG], [To1])
                                yield
                                po2, ps_ = Q[0], Q[1]
                                mm(po2, qkT[:], vn[:], [Tqk, Tvn], [Tb_])
                                mm(ps_, kd[:], vn[:], [Tkd, Tvn], [Tb_])
                                yield
                                if cfg.debug and os.environ.get("K_GDBG") and d == 0 and h == 0 and n == order[0]:
                                    for i_, (arr_, T_) in enumerate(((e_, Te), (N0, TN0), (PI, TPI), (u_, Tu), (qkT, Tqk))):
                                        dma("sp", DBGF_d[:, i_ * 128:(i_ + 1) * 128], arr_[:], reads=[T_])
                                tt("dve", oo[:], po2, o1s[:], ALU.add, [Tb_, To1], [Too])
                                stt(St[:], St[:], EG[:, n, 2, col:col + 1], ps_, ALU.mult, ALU.add, [TS, TG, Tb_], [TS])
                                dma("sp", OD_d[d, c0:c0 + 128, h * 128:(h + 1) * 128], oo[:], reads=[Too], awrites=[TOD])
                                yield

                        interleave([chain(h) for h in range(B_HEADS)])
                    S.barrier()
                if cfg.stop == "gdn" and l == cfg.stop_layer:
                    break

                TYB.new_version()
                with ExitStack() as fes:
                    o0r = ring(fes, "fo0", 2, [128, D])
                    o1r = ring(fes, "fo1", 2, [128, D])
                    zsr = ring(fes, "fzs", 2, [128, D])
                    jkr = ring(fes, "fjk", 2, [128, 128])
                    ssr = ring(fes, "fss", 2, [128, 8])
                    ybr = ring(fes, "fyb", 3, [128, KC, 128], BF16)
                    for tk in range(NCH):
                        o0, To0 = o0r.next()
                        dma("sp", o0[:], OD_d[0, tk * 128:(tk + 1) * 128, :], reads=[TOD], writes=[To0])
                        o1, To1_ = o1r.next()
                        dma("sp", o1[:], OD_d[1, tk * 128:(tk + 1) * 128, :], reads=[TOD], writes=[To1_])
                        zs, Tzs = zsr.next()
                        dma("sp", zs[:], ZS_d[tk * 128:(tk + 1) * 128, :], reads=[TZS], writes=[Tzs])
                        tt("pool", o0[:], o0[:], o1[:], ALU.add, [To0, To1_], [To0])
                        ss, Tss = ssr.next()
                        Tss.new_version()
                        for h in range(B_HEADS):
                            jk, Tjk = jkr.next()
                            act(jk[:], o0[:, h * 128:(h + 1) * 128], AF.Square, [To0], [Tjk], aw=[Tss], accum_out=ss[:, h:h + 1])
                        act(ss[:], ss[:], AF.Sqrt, [Tss, Tc], [Tss], scale=1.0 / B_DIM, bias=EPS_AP)
                        recip(ss[:], ss[:], [Tss], [Tss])
                        o3 = o0[:].rearrange("p (h e) -> p h e", h=B_HEADS)
                        tt("dve", o3, o3, ss[:].unsqueeze(2).to_broadcast([128, B_HEADS, 128]), ALU.mult, [To0, Tss], [To0])
                        tt("dve", o3, o3, rvl[:, 32:160].unsqueeze(1).to_broadcast([128, B_HEADS, 128]), ALU.mult, [To0, Tp], [To0])
                        tt("pool", o0[:], o0[:], zs[:], ALU.mult, [To0, Tzs], [To0])
                        yb, Tyb = ybr.next()
                        Tyb.new_version()
                        for h in range(B_HEADS):
                            pq, Tpq = psQ.next()
                            tr(pq, o0[:, h * 128:(h + 1) * 128], [To0], [Tpq])
                            evac(yb[:, h, :], pq, [Tpq], (), aw=[Tyb])
                        dma("sp", YB_d.rearrange("(h p) t -> p h t", p=128)[:, :, tk * 128:(tk + 1) * 128], yb[:], reads=[Tyb], awrites=[TYB])
                S.barrier()
                if cfg.stop == "gdnfin" and l == cfg.stop_layer:
                    break

                TX.new_version()
                mtiles = []
                for (c0, n) in tiles:
                    for s0 in range(0, n, 256):
                        mtiles.append((c0 + s0, min(256, n - s0)))
                with ExitStack() as mes:
                    wbr = sbuf(mes, "wbr", [128, 3, KC, D], BF16)
                    wou = sbuf(mes, "wou", [128, KC, D], BF16)
                    Twm = T()
                    wstg = ring(mes, "wstg", 2, [128, KC, 256])
                    srcs = [(w_br_d[l, k].rearrange("(kc p) f -> p kc f", p=128), wbr[:, k]) for k in range(3)] + [(w_out_d[l].rearrange("(kc p) f -> p kc f", p=128), wou[:])]
                    for (sv, dv) in srcs:
                        for jb in range(4):
                            ws, Tws = wstg.next()
                            dma("sp", ws[:], sv[:, :, jb * 256:(jb + 1) * 256], writes=[Tws])
                            cp("pool", dv[:, :, jb * 256:(jb + 1) * 256], ws[:], [Tws], (), aw=[Twm])
                    ysr = [ring(mes, f"ys{k}", 2, [128, KC, 256], BF16) for k in range(3)]
                    sgr = ring(mes, "msg", 2, [128, KC, 256], BF16)
                    yacr = ring(mes, "yacc", 1, [128, KC, 256])
                    tmr = ring(mes, "mtm", 2, [128, 256])
                    ybfr = ring(mes, "ybf", 2, [128, KC, 256], BF16)
                    xtr = ring(mes, "mxt", 2, [128, KC, 256])
                    xor = ring(mes, "mxo", 2, [128, KC, 256])
                    YS_d = (YA_d, YB_d, YR_d)
                    TYS = (TYA, TYB, TYR)
                    SGv = SG_d.rearrange("(k c p) t -> p k c t", p=128, k=3)
                    for (c0, n) in mtiles:
                        if c0 < TC and not need_ctx:
                            continue
                        j = 1 if c0 < TC else 0
                        xt_, Txt = xtr.next()
                        dma("sp", xt_[:, :, :n], Xv[:, :, c0:c0 + n], reads=[TX], writes=[Txt])
                        yac, Tyac = yacr.next()
                        for k in range(3):
                            y_, Ty_ = ysr[k].next()
                            dma("sp", y_[:, :, :n], YS_d[k].rearrange("(kc p) t -> p kc t", p=128)[:, :, c0:c0 + n], reads=[TYS[k]], writes=[Ty_])
                            sg, Tsg = sgr.next()
                            dma("sp", sg[:, :, :n], SGv[:, k, :, c0:c0 + n], reads=[TSG], writes=[Tsg])
                            for fo in range(KC):
                                pp, Tpp = psAll.next()
                                for kc in range(KC):
                                    mm(pp[:, :n], wbr[:, k, kc, fo * 128:(fo + 1) * 128], y_[:, kc, :n], [Twm, Ty_], [Tpp], start=(kc == 0), stop=(kc == KC - 1))
                                if k == 0:
                                    tt("dve", yac[:, fo, :n], pp[:, :n], sg[:, fo, :n], ALU.mult, [Tpp, Tsg], (), aw=[Tyac])
                                else:
                                    tm, Ttm = tmr.next()
                                    tt("dve", tm[:, :n], pp[:, :n], sg[:, fo, :n], ALU.mult, [Tpp, Tsg], [Ttm])
                                    tt("pool", yac[:, fo, :n], yac[:, fo, :n], tm[:, :n], ALU.add, [Ttm, Tyac], (), aw=[Tyac])
                        ybf, Tybf = ybfr.next()
                        cp("act", ybf[:, :, :n], yac[:, :, :n], [Tyac], [Tybf])
                        Tyac.new_version()
                        xo, Txo = xor.next()
                        Txo.new_version()
                        for fo in range(KC):
                            pp, Tpp = psAll.next()
                            for kc in range(KC):
                                mm(pp[:, :n], wou[:, kc, fo * 128:(fo + 1) * 128], ybf[:, kc, :n], [Twm, Tybf], [Tpp], start=(kc == 0), stop=(kc == KC - 1))
                            stt(xo[:, fo, :n], pp[:, :n], gt1(j, fo), xt_[:, fo, :n], ALU.mult, ALU.add, [Tpp, Tp, Txt], (), aw=[Txo])
                        dma("sp", Xv[:, :, c0:c0 + n], xo[:, :, :n], reads=[Txo], awrites=[TX])
                S.barrier()
                if cfg.stop == "merge" and l == cfg.stop_layer:
                    break
                H2F_T = T()
                TH2.new_version()
                ptok0 = 0 if need_ctx else TC
                ptiles = [(c0, n) for (c0, n) in mtiles if c0 >= ptok0]
                H2F_d = GQKV_d[0:D, :]
                H2Fv = H2F_d.rearrange("(kc p) t -> p kc t", p=128)
                with ExitStack() as nes:
                    rings = (ring(nes, "n2x", 2, [128, KC, 512]), ring(nes, "n2sq", 2, [128, KC, 512], BF16),
                             ring(nes, "n2rt", 2, [128, 512]), ring(nes, "n2tm", 1, [128, KC, 512]))
                    h2r = ring(nes, "h2o", 2, [128, KC, 512])
                    htr = ring(nes, "h2t", 2, [128, D])

                    def out_h2(c0, n, j, tm_, Ttm, gs, shf):
                        if c0 + n <= ptok0:
                            return
                        h2, Th2 = h2r.next()
                        Th2.new_version()
                        for kc in range(KC):
                            act(h2[:, kc, :n], tm_[:, kc, :n], AF.Identity, [Ttm, Tp], (), aw=[Th2], scale=gs[:, j, kc:kc + 1], bias=shf(j, kc))
                        dma("sp", H2Fv[:, :, c0:c0 + n], h2[:, :, :n], reads=[Th2], awrites=[H2F_T])
                        for s0 in range(0, n, 128):
                            ht, Tht = htr.next()
                            Tht.new_version()
                            for kc in range(KC):
                                pq, Tpq = psQ.next()
                                tr(pq, h2[:, kc, s0:s0 + 128], [Th2], [Tpq])
                                evac(ht[:, kc * 128:(kc + 1) * 128], pq, [Tpq], (), aw=[Tht])
                            dma("sp", H2_d[c0 + s0:c0 + s0 + 128, :], ht[:], reads=[Tht], awrites=[TH2])
                    norm_mod(nes, "n2", gs2, sh2, out_h2, rings)
                S.barrier()
                if cfg.stop == "peernorm" and l == cfg.stop_layer:
                    break
                IDXT = sbuf(les, "IDXT", [128, TT], U32)
                GATET = sbuf(les, "GATET", [128, TT])
                TIG = T()
                with ExitStack() as qes:
                    skt = sbuf(qes, "skt", [128, 16, 128])
                    Tsk = T()
                    for g0_ in range(0, 16, 4):
                        dma("sp", skt[:, g0_:g0_ + 4, :], skT_d[l, g0_:g0_ + 4].rearrange("g d k -> d g k"), awrites=[Tsk])
                    wqv = wq_d[l].rearrange("(kc p) f -> p kc f", p=128)
                    wqr = ring(qes, "wq", 2, [128, KC, 128])
                    h2r = ring(qes, "h2i", 2, [128, KC, 256])
                    QN = sbuf(qes, "QN", [128, 16, 256])
                    TQN = T()
                    qfr = ring(qes, "pqf", 2, [128, 256])
                    sqr = ring(qes, "psq", 2, [128, 256])
                    rtr = ring(qes, "prt", 2, [128, 256])
                    SCr = ring(qes, "SC", 2, [128, 16, 128])
                    m1r = ring(qes, "m1", 2, [128, 16, 16])
                    ixr = ring(qes, "ix", 2, [128, 16, 16], U32)
                    wkr = ring(qes, "wk", 2, [128, 256])
                    ixf = sbuf(qes, "ixf", [128, 16, 16])
                    cand = sbuf(qes, "cand", [128, 8, 256])
                    b1 = sbuf(qes, "b1", [128, 8, 16])
                    pos = sbuf(qes, "pos", [128, 8, 16], U32)
                    pab = sbuf(qes, "pab", [128, 2, 8, 16], U32)
                    pabf = sbuf(qes, "pabf", [128, 2, 8, 16])
                    oh = sbuf(qes, "oh", [128, 8, 16, 16])
                    isel = sbuf(qes, "isel", [128, 2, 8, 16])
                    idxf = sbuf(qes, "idxf", [128, 128])
                    gate = sbuf(qes, "gate", [128, 8, 16])
                    gsm = sbuf(qes, "gsm", [128, 8])
                    Tk_ = T()
                    IOTA16 = consts[:, C_IOTA0, 0:16]
                    for (c0, n) in ptiles:
                        h2, Th2 = h2r.next()
                        dma("sp", h2[:, :, :n], H2Fv[:, :, c0:c0 + n], reads=[H2F_T], writes=[Th2])
                        TQN.new_version()
                        for gi in range(16):
                            wq, Twq = wqr.next()
                            dma("sp", wq[:], wqv[:, :, gi * 128:(gi + 1) * 128], writes=[Twq])
                            pp, Tpp = psA.next()
                            for kc in range(KC):
                                mm(pp[:, :n], wq[:, kc, :], h2[:, kc, :n], [Twq, Th2], [Tpp], start=(kc == 0), stop=(kc == KC - 1))
                            qf, Tqf = qfr.next()
                            cp("dve", qf[:, :n], pp[:, :n], [Tpp], [Tqf])
                            sq, Tsq = sqr.next()
                            act(sq[:, :n], qf[:, :n], AF.Square, [Tqf], [Tsq])
                            pb, Tpb = psB.next()
                            mm(pb[:, :n], ONES, sq[:, :n], [Tsq, Tc], [Tpb])
                            rt, Trt = rtr.next()
                            act(rt[:, :n], pb[:, :n], AF.Sqrt, [Tpb, Tc], [Trt], scale=1.0 / 128.0, bias=EPS_AP)
                            recip(rt[:, :n], rt[:, :n], [Trt], [Trt])
                            tt("dve", QN[:, gi, :n], qf[:, :n], rt[:, :n], ALU.mult, [Tqf, Trt], (), aw=[TQN])
                        for s0 in range(0, n, 128):
                            tk0 = c0 + s0
                            SC, TSC = SCr.next()
                            TSC.new_version()
                            for gq in range(4):
                                pp, Tpp = psA.next()
                                for g4 in range(4):
                                    gi = gq * 4 + g4
                                    mm(pp[:, g4 * 128:(g4 + 1) * 128], QN[:, gi, s0:s0 + 128], skt[:, gi, :], [TQN, Tsk], [Tpp])
                                evac(SC[:, gq * 4:(gq + 1) * 4, :], pp[:].rearrange("p (a b) -> p a b", a=4), [Tpp], (), aw=[TSC])
                            m1, Tm1 = m1r.next()
                            ix, Tix = ixr.next()
                            for gi in range(16):
                                wk, Twk = wkr.next()
                                S.op("dve", lambda e, o=m1[:, gi, 0:8], i=SC[:, gi, :]: e.max(out=o, in_=i), [TSC], [Tm1])
                                S.op("dve", lambda e, o=wk[:, 0:128], r=m1[:, gi, 0:8], i=SC[:, gi, :]: e.match_replace(out=o, in_to_replace=r, in_values=i, imm_value=-1e30), [TSC, Tm1], [Twk])
                                S.op("dve", lambda e, o=m1[:, gi, 8:16], i=wk[:, 0:128]: e.max(out=o, in_=i), [Twk], [Tm1])
                                S.op("dve", lambda e, o=ix[:, gi, 0:8], r=m1[:, gi, 0:8], i=SC[:, gi, :]: e.max_index(out=o, in_max=r, in_values=i), [TSC, Tm1], [Tix])
                                S.op("dve", lambda e, o=ix[:, gi, 8:16], r=m1[:, gi, 8:16], i=SC[:, gi, :]: e.max_index(out=o, in_max=r, in_values=i), [TSC, Tm1], [Tix])
                            cp("dve", ixf[:], ix[:], [Tix], [Tk_])
                            m1v = m1[:].rearrange("p (h c) k -> p h c k", c=2)
                            ixv = ixf[:].rearrange("p (h c) k -> p h c k", c=2)
                            cand4 = cand[:].rearrange("p h (a b) -> p h a b", a=16)
                            tt("dve", cand4, m1v[:, :, 0, :].unsqueeze(3).to_broadcast([128, 8, 16, 16]),
                               m1v[:, :, 1, :].unsqueeze(2).to_broadcast([128, 8, 16, 16]), ALU.add, [Tm1], [Tk_])
                            for h in range(P_HEADS):
                                wk, Twk = wkr.next()
                                S.op("dve", lambda e, o=b1[:, h, 0:8], i=cand[:, h, :]: e.max(out=o, in_=i), [Tk_], [Tk_])
                                S.op("dve", lambda e, o=wk[:], r=b1[:, h, 0:8], i=cand[:, h, :]: e.match_replace(out=o, in_to_replace=r, in_values=i, imm_value=-1e30), [Tk_], [Twk])
                                S.op("dve", lambda e, o=b1[:, h, 8:16], i=wk[:]: e.max(out=o, in_=i), [Twk], [Tk_])
                                S.op("dve", lambda e, o=pos[:, h, 0:8], r=b1[:, h, 0:8], i=cand[:, h, :]: e.max_index(out=o, in_max=r, in_values=i), [Tk_], [Tk_])
                                S.op("dve", lambda e, o=pos[:, h, 8:16], r=b1[:, h, 8:16], i=cand[:, h, :]: e.max_index(out=o, in_max=r, in_values=i), [Tk_], [Tk_])
                            ts("dve", pab[:, 0], pos[:], 4, ALU.logical_shift_right, [Tk_], [Tk_])
                            ts("dve", pab[:, 1], pos[:], 15, ALU.bitwise_and, [Tk_], [Tk_])
                            cp("dve", pabf[:], pab[:], [Tk_], [Tk_])
                            for c in range(2):
                                tt("dve", oh[:], IOTA16.unsqueeze(1).unsqueeze(1).to_broadcast([128, 8, 16, 16]),
                                   pabf[:, c].unsqueeze(3).to_broadcast([128, 8, 16, 16]), ALU.is_equal, [Tk_, Tc], [Tk_])
                                tt("dve", oh[:], oh[:], ixv[:, :, c, :].unsqueeze(2).to_broadcast([128, 8, 16, 16]), ALU.mult, [Tk_], [Tk_])
                                S.op("dve", lambda e, o=isel[:, c], i=oh[:]: e.tensor_reduce(out=o, in_=i, axis=AX.X, op=ALU.add), [Tk_], [Tk_])
                            stt(idxf[:].rearrange("p (h k) -> p h k", h=8), isel[:, 0], 128.0, isel[:, 1], ALU.mult, ALU.add, [Tk_], [Tk_])
                            tt("dve", gate[:], b1[:], b1[:, :, 0:1].to_broadcast([128, 8, 16]), ALU.subtract, [Tk_], [Tk_])
                            act(gate[:], gate[:], AF.Exp, [Tk_], [Tk_])
                            S.op("dve", lambda e, o=gsm[:], i=gate[:]: e.tensor_reduce(out=o, in_=i, axis=AX.X, op=ALU.add), [Tk_], [Tk_])
                            recip(gsm[:], gsm[:], [Tk_], [Tk_])
                            tt("dve", gate[:], gate[:], gsm[:].unsqueeze(2).to_broadcast([128, 8, 16]), ALU.mult, [Tk_], [Tk_])
                            pq, Tpq = psQ.next()
                            tr(pq, idxf[:], [Tk_], [Tpq])
                            cp("dve", IDXT[:, tk0:tk0 + 128], pq, [Tpq], (), aw=[TIG])
                            pq2, Tpq2 = psQ.next()
                            tr(pq2, gate[:].rearrange("p h k -> p (h k)"), [Tk_], [Tpq2])
                            cp("act", GATET[:, tk0:tk0 + 128], pq2, [Tpq2], (), aw=[TIG])
                S.barrier()
                if cfg.stop == "peertopk" and l == cfg.stop_layer:
                    if cfg.debug:
                        dma("sp", DBGF_d[:, ptok0:TT], GATET[:, ptok0:TT], reads=[TIG])
                        dma("sp", DBGI_d[:, ptok0:TT], IDXT[:, ptok0:TT], reads=[TIG])
                    break
                TX.new_version()
                with ExitStack() as ges:
                    ugr = ring(ges, "ug", 3, [128, D])
                    vgr = ring(ges, "vg", 3, [128, D])
                    hbr = ring(ges, "hb", 3, [128, D])
                    jkr2 = ring(ges, "pjunk", 2, [128, D])
                    dotr = ring(ges, "dots", 2, [128, 128])
                    ctr = ring(ges, "ct", 2, [128, 128])
                    xtr = ring(ges, "pxt", 2, [128, KC, 128])
                    xor = ring(ges, "pxo", 2, [128, KC, 128])
                    u_l, v_l = pu_d.rearrange("l e d -> (l e) d"), pv_d.rearrange("l e d -> (l e) d")
                    eoff = l * P_KEYS * P_KEYS * D
                    pbank = 0
                    for tk0 in range(ptok0, TT, 128):
                        j = 1 if tk0 < TC else 0
                        dots, Tdots = dotr.next()
                        Tdots.new_version()
                        for i in range(128):
                            n = tk0 + i
                            ug, Tug = ugr.next()
                            S.dma("pool", lambda e, o=ug[:], ix=IDXT[:, n:n + 1], u_l=u_l, eoff=eoff: e.indirect_dma_start(
                                out=o, out_offset=None, in_=u_l, in_offset=bass.IndirectOffsetOnAxis(ap=ix, axis=0), element_offset=eoff), [TIG], [Tug])
                            hb, Thb = hbr.next()
                            dma("sp", hb[:], H2_d[n:n + 1, :].to_broadcast([128, D]), reads=[TH2], writes=[Thb])
                            junk, Tjunk = jkr2.next()
                            stt(junk[:], ug[:], 1.0, hb[:], ALU.mult, ALU.mult, [Tug, Thb], [Tjunk], accum_out=dots[:, i:i + 1], aw=[Tdots])
                        ct, Tct = ctr.next()
                        act(ct[:], dots[:], AF.Gelu_apprx_tanh, [Tdots], [Tct])
                        tt("dve", ct[:], ct[:], GATET[:, tk0:tk0 + 128], ALU.mult, [Tct, TIG], [Tct])
                        (pA, TpA), (pB_, TpB) = PB[pbank], PB[pbank + 1]
                        pbank = (pbank + 2) % 8
                        TpA.new_version()
                        TpB.new_version()
                        for i in range(128):
                            n = tk0 + i
                            vg, Tvg = vgr.next()
                            S.dma("pool", lambda e, o=vg[:], ix=IDXT[:, n:n + 1], v_l=v_l, eoff=eoff: e.indirect_dma_start(
                                out=o, out_offset=None, in_=v_l, in_offset=bass.IndirectOffsetOnAxis(ap=ix, axis=0), element_offset=eoff), [TIG], [Tvg])
                            for kc in range(KC):
                                pt_, Tpt_ = (pA, TpA) if kc < 4 else (pB_, TpB)
                                S.op("pe", lambda e, o=pt_[:, (kc % 4) * 128 + i:(kc % 4) * 128 + i + 1], w=vg[:, kc * 128:(kc + 1) * 128], r=ct[:, i:i + 1]:
                                     e.matmul(o, w, r, start=True, stop=True), [Tvg, Tct], (), [Tpt_])
                        xt_, Txt = xtr.next()
                        dma("sp", xt_[:], Xv[:, :, tk0:tk0 + 128], reads=[TX], writes=[Txt])
                        xo, Txo = xor.next()
                        Txo.new_version()
                        for kc in range(KC):
                            pt_, Tpt_ = (pA, TpA) if kc < 4 else (pB_, TpB)
                            stt(xo[:, kc, :], pt_[:, (kc % 4) * 128:(kc % 4 + 1) * 128], gt2(j, kc), xt_[:, kc, :], ALU.mult, ALU.add, [Tpt_, Tp, Txt], (), aw=[Txo])
                        dma("sp", Xv[:, :, tk0:tk0 + 128], xo[:], reads=[Txo], awrites=[TX])
                S.barrier()
        else:
            dma("sp", out_d[:, :], X_d[:, TC:TT], reads=[TX])
        S.finish()
        S.emit()
    return nc


def _host_consts():
    c = np.zeros((NCONST, 128, 128), np.float32)
    c[C_ID] = np.eye(128, dtype=np.float32)
    c[C_ONES] = 1.0
    c[C_BD64, 0:64, 0:64] = 1.0
    c[C_BD64, 64:128, 64:128] = 1.0
    for p in range(128):
        dd = p % 64
        partner = p + 16 if (dd % 32) < 16 else p - 16
        c[C_PERM, partner, p] = 1.0
    ones = np.ones((128, 128), np.float32)
    c[C_UI] = np.triu(ones)
    c[C_LI] = np.tril(ones)
    c[C_SL] = np.tril(ones, -1)
    c[C_SU] = np.triu(ones, 1)
    c[C_IOTA0] = np.arange(128, dtype=np.float32)[None, :]
    c[C_IOTA1] = np.arange(128, dtype=np.float32)[None, :] + 128.0
    ii = np.arange(128)
    for k in range(7):
        s_ = 1 << k
        c[C_OFF + k] = ((ii[:, None] // (2 * s_) == ii[None, :] // (2 * s_)) & (ii[:, None] // s_ != ii[None, :] // s_)).astype(np.float32)
    return np.ascontiguousarray(c.transpose(1, 0, 2).reshape(128, NCONST * 128))


def _rope_tables(TL):
    t = np.arange(TL)
    row = (t // GRID_W).astype(np.float32)
    col = (t % GRID_W).astype(np.float32)
    nf = A_DIM // 4
    inv = (np.float32(ROPE_THETA) ** (-np.arange(nf, dtype=np.float32) / np.float32(nf))).astype(np.float32)
    out = np.zeros((2, 128, TL), np.float32)
    for p in range(128):
        dd = p % 64
        axis, half, f = dd // 32, (dd % 32) // 16, dd % 16
        ang = ((row if axis == 0 else col) * inv[f]).astype(np.float32)
        out[0, p] = np.cos(ang)
        out[1, p] = -np.sin(ang) if half == 0 else np.sin(ang)
    return out


def _chunkT(v, n):
    return np.ascontiguousarray(np.asarray(v, np.float32).reshape(n, 128).T)


def prepare_inputs(inp, cfg):
    L = cfg.L
    f = lambda a: np.asarray(a, np.float32)
    spar = np.zeros((L, 128, NSP), np.float32)
    rvec = np.zeros((L, NRV), np.float32)
    lruw = np.zeros((L, 2, 2, KC, 128, 128), np.float32)
    for l in range(L):
        s = spar[l]
        s[:, SP["b_ada"]:SP["b_ada"] + 48] = _chunkT(inp["b_ada"][l], 48)
        s[:, SP["n1g"]:SP["n1g"] + 8] = _chunkT(inp["norm1_g"][l], 8)
        s[:, SP["n2g"]:SP["n2g"] + 8] = _chunkT(inp["norm2_g"][l], 8)
        s[:, SP["gq"]] = np.tile(f(inp["attn_qn_g"][l]), 2)
        s[:, SP["gk"]] = np.tile(f(inp["attn_kn_g"][l]), 2)
        s[:, SP["subg"]] = f(inp["attn_sub_g"][l])
        for i, k in enumerate(("lam_q1", "lam_k1", "lam_q2", "lam_k2")):
            s[0:64, SP["lam"] + i] = f(inp[k][l])
        s[:, SP["gconv"]:SP["gconv"] + 120] = f(inp["gdn_conv_w"][l]).reshape(5, 24, 128).transpose(2, 1, 0).reshape(128, 120)
        s[:, SP["lconv"]:SP["lconv"] + 40] = f(inp["lru_conv_w"][l]).reshape(5, 8, 128).transpose(2, 1, 0).reshape(128, 40)
        s[:, SP["lconvb"]:SP["lconvb"] + 8] = _chunkT(inp["lru_conv_b"][l], 8)
        for nm, key in (("lbr", "lru_b_r"), ("lbi", "lru_b_i"), ("llam", "lru_lambda")):
            s[:, SP[nm]:SP[nm] + 16] = f(inp[key][l]).reshape(2, 8, 128).transpose(2, 0, 1).reshape(128, 16)
        rvec[l, 0:16] = f(inp["gdn_a_log"][l]).reshape(16)
        rvec[l, 16:32] = f(inp["gdn_dt_bias"][l]).reshape(16)
        rvec[l, 32:160] = f(inp["gdn_norm_g"][l])
        for d in range(2):
            for ri, key in enumerate(("lru_w_r", "lru_w_i")):
                w = f(inp[key][l, d])
                for c in range(KC):
                    lruw[l, d, ri, c, 0:64, 0:64] = w[2 * c]
                    lruw[l, d, ri, c, 64:128, 64:128] = w[2 * c + 1]
    skT = np.ascontiguousarray(f(inp["peer_subkeys"])[:L].reshape(L, 16, 128, 128).transpose(0, 1, 3, 2))
    shared = {
        "consts": _host_consts(), "rope": _rope_tables(cfg.TL), "spar": spar, "rvec": rvec,
        "w_ada": np.ascontiguousarray(f(inp["w_ada"])[:L]), "w_in": np.ascontiguousarray(f(inp["w_in"])[:L]),
        "w_branch": np.ascontiguousarray(f(inp["w_branch"])[:L]), "w_out": np.ascontiguousarray(f(inp["w_out"])[:L]),
        "lruw": lruw, "peer_wq": np.ascontiguousarray(f(inp["peer_wq"])[:L]), "skT": skT,
        "peer_u": np.ascontiguousarray(f(inp["peer_u"])[:L]), "peer_v": np.ascontiguousarray(f(inp["peer_v"])[:L]),
    }
    x, ctx, c, c_ctx = f(inp["x"]), f(inp["ctx"]), f(inp["c"]), f(inp["c_ctx"])
    maps = []
    for b in range(x.shape[0]):
        m = dict(shared)
        m["xT"] = np.ascontiguousarray(np.concatenate([ctx[b].T, x[b].T], axis=1))
        cond = np.stack([c[b].reshape(KC, 128).T, c_ctx.reshape(KC, 128).T], axis=2)
        m["cond"] = np.ascontiguousarray(cond.reshape(128, KC * 2))
        maps.append(m)
    return maps


_NC_CACHE = {}


def kernel(**inputs):
    x = np.asarray(inputs["x"])
    B, TL, _ = x.shape
    TC = np.asarray(inputs["ctx"]).shape[1]
    L = np.asarray(inputs["w_in"]).shape[0]
    cfg = Cfg(TC=TC, TL=TL, L=L)
    key = (TC, TL, L)
    if key not in _NC_CACHE:
        _NC_CACHE[key] = build(cfg)
    nc = _NC_CACHE[key]
    maps = prepare_inputs(inputs, cfg)
    res = run_bass_kernel_spmd(nc, maps, core_ids=list(range(B)))
    out = np.stack([np.ascontiguousarray(res.results[b]["outT"].T) for b in range(B)], axis=0)
    return out.astype(np.float32)
```

```python
import math
import os
from contextlib import ExitStack

import numpy as np
import concourse.bass as bass
import concourse.mybir as mybir
from concourse.bass_utils import run_bass_kernel_spmd

F32 = mybir.dt.float32
BF16 = mybir.dt.bfloat16
U32 = mybir.dt.uint32
AF = mybir.ActivationFunctionType
ALU = mybir.AluOpType
AX = mybir.AxisListType

D = 1024
KC = 8
N_MOD = 6
EPS = 1e-6
A_HEADS = 8
A_DIM = 64
ROPE_THETA = 10000.0
GRID_W = 64
B_HEADS = 8
B_DIM = 128
SHORT_CONV = 5
C_WIDTH = 1024
LRU_C = 8.0
P_HEADS = 8
P_KEYS = 128
P_TOPK = 16
A_QKV = 3072
B_QKV = 3072
OFF_AQ, OFF_AK, OFF_AV = 0, 1024, 2048
OFF_BQKV = 3072
OFF_Z = 6144
OFF_BETA = 7168
OFF_ALPHA = 7184
OFF_LX = 7200
OFF_LG = 8224
OFF_MG = 9248
IN_WIDTH = 12320

SP = {}
_o = 0
for _n, _w in (("b_ada", 48), ("n1g", 8), ("n2g", 8), ("gq", 1), ("gk", 1), ("subg", 1), ("lam", 4),
               ("gconv", 120), ("lconv", 40), ("lconvb", 8), ("lbr", 16), ("lbi", 16), ("llam", 16)):
    SP[_n] = _o
    _o += _w
NSP = _o
NRV = 160
C_ID, C_ONES, C_BD64, C_PERM, C_UI, C_LI, C_SL, C_SU, C_IOTA0, C_IOTA1 = range(10)
C_OFF = 10
NCONST = 17

EPOCH = 60000
DMA_ND = 8
DMA_GEN = 3500


class T:
    __slots__ = ("w", "r", "pw", "excl")

    def __init__(self, excl=False):
        self.w = {}
        self.r = {}
        self.pw = {}
        self.excl = excl

    def new_version(self):
        pw = dict(self.w)
        _merge(pw, self.r.values())
        _merge(pw, self.pw.values())
        self.pw = pw
        self.w = {}
        self.r = {}


def _merge(d, toks):
    for tok in toks:
        k = id(tok[0])
        if k not in d or d[k][1] < tok[1]:
            d[k] = tok


class Sched:
    ENGS = ("pe", "act", "dve", "pool", "sp")

    def __init__(self, nc, es):
        self.nc = nc
        self.es = es
        self.q = {e: [] for e in self.ENGS}
        self.cnt = {e: 0 for e in self.ENGS}
        self.sems = {e: [] for e in self.ENGS}
        self.seen = {e: {} for e in self.ENGS}
        self.dcnt = {e: 0 for e in self.ENGS}
        self.dsems = {e: [] for e in self.ENGS}
        self.nsem = 0

    def _newsem(self, name):
        self.nsem += 1
        return self.es.enter_context(self.nc.semaphore(f"{name}{self.nsem}"))

    def _wait(self, eng, deps, is_pe_op):
        for (sem, val, src) in deps:
            if is_pe_op and src == "pe":
                continue
            k = id(sem)
            if self.seen[eng].get(k, 0) >= val:
                continue
            self.seen[eng][k] = val
            self.q[eng].append(lambda e, s=sem, v=val: e.wait_ge(s, v))

    def _deps(self, reads, writes, awrites):
        deps = {}
        for t in reads:
            _merge(deps, t.w.values())
            _merge(deps, t.pw.values())
        for t in writes:
            _merge(deps, t.w.values())
            _merge(deps, t.r.values())
            _merge(deps, t.pw.values())
        for t in awrites:
            _merge(deps, t.pw.values())
        return deps

    def _record(self, tok, reads, writes, awrites):
        for t in reads:
            _merge(t.r, [tok])
        for t in writes:
            t.w = {id(tok[0]): tok}
            t.r = {}
            t.pw = {}
        for t in awrites:
            _merge(t.w, [tok])

    @staticmethod
    def _split(reads, writes):
        ex = [t for t in reads if t.excl]
        if not ex:
            return reads, writes
        return [t for t in reads if not t.excl], list(writes) + ex

    def op(self, eng, fn, reads=(), writes=(), awrites=()):
        reads, writes = self._split(reads, writes)
        deps = self._deps(reads, writes, awrites)
        self._wait(eng, deps.values(), eng == "pe")
        i = self.cnt[eng]
        self.cnt[eng] += 1
        ep = i // EPOCH
        while len(self.sems[eng]) <= ep:
            self.sems[eng].append(self._newsem(eng))
        sem = self.sems[eng][ep]
        self.q[eng].append(lambda e, f=fn, s=sem: f(e).then_inc(s, 1))
        tok = (sem, i % EPOCH + 1, eng)
        self._record(tok, reads, writes, awrites)
        return tok

    def dma(self, eng, fn, reads=(), writes=(), awrites=()):
        reads, writes = self._split(reads, writes)
        deps = self._deps(reads, writes, awrites)
        j = self.dcnt[eng]
        self.dcnt[eng] += 1
        gen, within = divmod(j, DMA_ND * DMA_GEN)
        slot = within % DMA_ND
        use = within // DMA_ND
        while len(self.dsems[eng]) <= gen:
            self.dsems[eng].append([self._newsem("d" + eng) for _ in range(DMA_ND)])
        sem = self.dsems[eng][gen][slot]
        if use > 0:
            _merge(deps, [(sem, 16 * use, "dma")])
        self._wait(eng, deps.values(), False)
        self.q[eng].append(lambda e, f=fn, s=sem: f(e).then_inc(s, 16))
        tok = (sem, 16 * (use + 1), "dma")
        self._record(tok, reads, writes, awrites)
        return tok

    def _all_tokens(self):
        toks = []
        for e in self.ENGS:
            if self.cnt[e] > 0:
                i = self.cnt[e] - 1
                toks.append((self.sems[e][i // EPOCH], i % EPOCH + 1, e))
            for gi, gen in enumerate(self.dsems[e]):
                n_in = min(max(self.dcnt[e] - gi * DMA_ND * DMA_GEN, 0), DMA_ND * DMA_GEN)
                for slot, sem in enumerate(gen):
                    uses = (n_in - slot + DMA_ND - 1) // DMA_ND if n_in > slot else 0
                    if uses > 0:
                        toks.append((sem, 16 * uses, "dma"))
        return toks

    def barrier(self):
        toks = self._all_tokens()
        for e in self.ENGS:
            self._wait(e, toks, False)

    def finish(self):
        self._wait("sp", self._all_tokens(), False)

    def emit(self):
        with self.nc.Block() as block:
            @block.sync
            def _(e):
                for f in self.q["sp"]:
                    f(e)

            @block.tensor
            def _(e):
                for f in self.q["pe"]:
                    f(e)

            @block.scalar
            def _(e):
                for f in self.q["act"]:
                    f(e)

            @block.vector
            def _(e):
                for f in self.q["dve"]:
                    f(e)

            @block.gpsimd
            def _(e):
                for f in self.q["pool"]:
                    f(e)


class Ring:
    def __init__(self, tiles):
        self.tiles = [(t, T()) for t in tiles]
        self.i = 0

    def next(self):
        r = self.tiles[self.i % len(self.tiles)]
        self.i += 1
        return r


def interleave(gens):
    gens = list(gens)
    while gens:
        nxt = []
        for g in gens:
            try:
                next(g)
                nxt.append(g)
            except StopIteration:
                pass
        gens = nxt


class Cfg:
    def __init__(self, TC=256, TL=4096, L=4, debug=False, stop=None, stop_layer=0):
        self.TC, self.TL, self.L, self.debug, self.stop, self.stop_layer = TC, TL, L, debug, stop, stop_layer


def build(cfg):
    TC, TL, L = cfg.TC, cfg.TL, cfg.L
    TT = TC + TL
    NCH = TT // 128
    NCC = TC // 128
    TP = TT + 6
    tiles = [(c0, min(512, TC - c0)) for c0 in range(0, TC, 512)] + [(c0, min(512, TT - c0)) for c0 in range(TC, TT, 512)]

    def ppos(c0):
        return c0 + 2 if c0 < TC else c0 + 4

    nc = bass.Bass("TRN2", target_bir_lowering=False)
    kind_dbg = "ExternalOutput" if cfg.debug else "Internal"

    def dram(name, shape, dt=F32, kind="ExternalInput"):
        return nc.dram_tensor(name, list(shape), dt, kind=kind).ap()

    xT_d = dram("xT", [D, TT])
    cond_d = dram("cond", [128, KC * 2])
    consts_d = dram("consts", [128, NCONST * 128])
    rope_d = dram("rope", [2, 128, TL])
    sp_d = dram("spar", [L, 128, NSP])
    rv_d = dram("rvec", [L, NRV])
    w_ada_d = dram("w_ada", [L, D, N_MOD * D])
    w_in_d = dram("w_in", [L, D, IN_WIDTH])
    w_br_d = dram("w_branch", [L, 3, D, D])
    w_out_d = dram("w_out", [L, D, D])
    lruw_d = dram("lruw", [L, 2, 2, KC, 128, 128])
    wq_d = dram("peer_wq", [L, D, 2048])
    skT_d = dram("skT", [L, 16, 128, 128])
    pu_d = dram("peer_u", [L, P_KEYS * P_KEYS, D])
    pv_d = dram("peer_v", [L, P_KEYS * P_KEYS, D])
    out_d = dram("outT", [D, TL], kind="ExternalOutput")
    X_d = dram("X", [D, TT], kind=kind_dbg)
    QKT_d = dram("QKT", [2048, TT], BF16, kind=kind_dbg)
    VTOK_d = dram("VTOK", [TT, D], BF16, kind=kind_dbg)
    GQKV_d = dram("GQKV", [3072, TT], kind=kind_dbg)
    ZS_d = dram("ZS", [TT, D], kind=kind_dbg)
    SG_d = dram("SG", [3072, TT], BF16, kind=kind_dbg)
    YA_d = dram("YA", [D, TT], BF16, kind=kind_dbg)
    YB_d = dram("YB", [D, TT], BF16, kind=kind_dbg)
    YR_d = dram("YR", [D, TT], BF16, kind=kind_dbg)
    OD_d = dram("OD", [2, TT, D], kind=kind_dbg)
    H2_d = dram("H2", [TT, D], kind=kind_dbg)
    HT_d = dram("HTdbg", [D, TT], BF16, kind=kind_dbg) if cfg.debug else None
    DBGF_d = dram("DBGF", [128, TT], kind="ExternalOutput") if cfg.debug else None
    DBGI_d = dram("DBGI", [128, TT], U32, kind="ExternalOutput") if cfg.debug else None
    UB_d = dram("UB", [L * P_KEYS * P_KEYS, D], BF16, kind="Internal")
    VB_d = dram("VB", [L * P_KEYS * P_KEYS, D], BF16, kind="Internal")
    TX, TQKT, TVTOK, TGQKV, TZS, TSG, TYA, TYB, TYR, TOD, TH2 = [T() for _ in range(11)]
    TUB = T()

    es = ExitStack()
    with es:
        S = Sched(nc, es)

        uniq = [0]

        def sbuf(stack, name, shape, dt=F32):
            uniq[0] += 1
            return stack.enter_context(nc.sbuf_tensor(f"{name}_{uniq[0]}", list(shape), dt))

        def ring(stack, name, n, shape, dt=F32):
            return Ring([sbuf(stack, f"{name}{i}", shape, dt) for i in range(n)])

        def act(out, in_, func, reads, writes, aw=(), **kw):
            return S.op("act", lambda e: e.activation(out=out, in_=in_, func=func, **kw), reads, writes, aw)

        def tt(eng, out, in0, in1, op, reads, writes, aw=()):
            return S.op(eng, lambda e: e.tensor_tensor(out=out, in0=in0, in1=in1, op=op), reads, writes, aw)

        def ts(eng, out, in0, s1, op0, reads, writes, s2=None, op1=None, aw=()):
            if op1 is None:
                return S.op(eng, lambda e: e.tensor_scalar(out=out, in0=in0, scalar1=s1, scalar2=None, op0=op0), reads, writes, aw)
            return S.op(eng, lambda e: e.tensor_scalar(out=out, in0=in0, scalar1=s1, scalar2=s2, op0=op0, op1=op1), reads, writes, aw)

        def stt(out, in0, scalar, in1, op0, op1, reads, writes, accum_out=None, aw=()):
            if accum_out is None:
                return S.op("dve", lambda e: e.scalar_tensor_tensor(out=out, in0=in0, scalar=scalar, in1=in1, op0=op0, op1=op1), reads, writes, aw)
            return S.op("dve", lambda e: e.scalar_tensor_tensor(out=out, in0=in0, scalar=scalar, in1=in1, op0=op0, op1=op1, accum_out=accum_out), reads, writes, aw)

        def cp(eng, out, in_, reads, writes, aw=()):
            if eng == "act":
                return act(out, in_, AF.Copy, reads, writes, aw)
            return S.op(eng, lambda e: e.tensor_copy(out=out, in_=in_), reads, writes, aw)

        def recip(out, in_, reads, writes):
            return S.op("dve", lambda e: e.reciprocal(out=out, in_=in_), reads, writes)

        def mm(out, lhsT, rhs, reads, writes, start=True, stop=True):
            return S.op("pe", lambda e: e.matmul(out, lhsT, rhs, start=start, stop=stop), reads, writes)

        def tr(out, in_, reads, writes):
            return S.op("pe", lambda e: e.transpose(out, in_, IDENT), list(reads) + [Tc], writes)

        def dma(q, out, in_, reads=(), writes=(), awrites=()):
            return S.dma(q, lambda e: e.dma_start(out=out, in_=in_), reads, writes, awrites)

        evc = [0]

        def evac(out, in_, reads, writes, aw=()):
            evc[0] += 1
            return cp("act" if evc[0] % 2 else "dve", out, in_, reads, writes, aw)

        PB = [(es.enter_context(nc.psum_tensor(f"pb{i}", [128, 512], F32)), T(excl=True)) for i in range(8)]
        psA = Ring([PB[i][0] for i in range(4)])
        psA.tiles = [PB[i] for i in range(4)]
        psB = Ring([PB[i][0] for i in range(4, 8)])
        psB.tiles = [PB[i] for i in range(4, 8)]
        psAll = Ring([PB[i][0] for i in range(8)])
        psAll.tiles = PB
        psQ = Ring([PB[0][0]])
        psQ.tiles = [(PB[k // 4][0][:, (k % 4) * 128:(k % 4 + 1) * 128], PB[k // 4][1]) for k in range(32)]

        consts = sbuf(es, "consts", [128, NCONST, 128])
        Tc = T()
        dma("sp", consts[:].rearrange("p c f -> p (c f)"), consts_d[:, :], writes=[Tc])
        IDENT, ONES, PERM = consts[:, C_ID, :], consts[:, C_ONES, :], consts[:, C_PERM, :]
        UI, LI, SL, SU = consts[:, C_UI, :], consts[:, C_LI, :], consts[:, C_SL, :], consts[:, C_SU, :]
        cbf = sbuf(es, "cbf", [128, 2, 128], BF16)
        cp("pool", cbf[:, 0, :], consts[:, C_ONES, :], [Tc], [Tc])
        cp("pool", cbf[:, 1, :], consts[:, C_BD64, :], [Tc], [Tc])
        ONES_BF, BD64_BF = cbf[:, 0, :], cbf[:, 1, :]
        cst = sbuf(es, "cst", [128, 4])
        S.op("pool", lambda e: e.memset(cst[:, 0:1], EPS), (), [Tc])
        S.op("pool", lambda e: e.memset(cst[:, 1:2], 1.0), (), [Tc])
        S.op("pool", lambda e: e.memset(cst[:, 2:3], 0.0), (), [Tc])
        EPS_AP, ONE_AP, ZERO_AP = cst[:, 0:1], cst[:, 1:2], cst[:, 2:3]
        scond = sbuf(es, "scond", [128, KC, 2])
        dma("sp", scond[:].rearrange("p k j -> p (k j)"), cond_d[:, :], writes=[Tc])
        act(scond[:], scond[:], AF.Silu, [Tc], [Tc])

        dma("sp", X_d[:, :], xT_d[:, :], awrites=[TX])
        with ExitStack() as ces:
            cfr = ring(ces, "cvf", 2, [128, 8, D])
            cbr = ring(ces, "cvb", 2, [128, 8, D], BF16)
            ci = 0
            for (src_, dst_) in ((pu_d, UB_d), (pv_d, VB_d)):
                sv_ = src_.rearrange("l (r p j) d -> (l r) p j d", p=128, j=8)
                dv_ = dst_.rearrange("(r p j) d -> r p j d", p=128, j=8)
                for r_ in range(L * 16):
                    f_, Tf_ = cfr.next()
                    dma("sp", f_[:], sv_[r_], writes=[Tf_])
                    b_, Tb__ = cbr.next()
                    cp(("act", "pool", "dve")[ci % 3], b_[:], f_[:], [Tf_], [Tb__])
                    ci += 1
                    dma("sp", dv_[r_], b_[:], reads=[Tb__], awrites=[TUB])
        S.barrier()

        Xv = X_d.rearrange("(kc p) t -> p kc t", p=128)

        for l in range(L):
            need_ctx = l < L - 1
            lam_init = 0.8 - 0.6 * math.exp(-0.3 * l)
            les = ExitStack()
            with les:
                Tp = T()
                spl = sbuf(les, "spl", [128, NSP])
                dma("sp", spl[:], sp_d[l], writes=[Tp])
                rvl = sbuf(les, "rvl", [128, NRV])
                dma("sp", rvl[:], rv_d[l:l + 1, :].to_broadcast([128, NRV]), writes=[Tp])
                mod = sbuf(les, "mod", [128, 2, 48])
                with ExitStack() as pes:
                    wring = ring(pes, "wada", 2, [128, KC, 512])
                    wav = w_ada_d[l].rearrange("(kc p) f -> p kc f", p=128)
                    pm, Tpm = PB[0]
                    for fg in range(12):
                        wt, Tw = wring.next()
                        dma("sp", wt[:], wav[:, :, fg * 512:(fg + 1) * 512], writes=[Tw])
                        for j in range(4):
                            fc = fg * 4 + j
                            for kc in range(KC):
                                mm(pm[:, 2 * fc:2 * fc + 2], wt[:, kc, j * 128:(j + 1) * 128], scond[:, kc, :], [Tw, Tc], [Tpm],
                                   start=(kc == 0), stop=(kc == KC - 1))
                    pmv = pm[:, 0:96].rearrange("p (f j) -> p j f", j=2)
                    for j in range(2):
                        tt("dve", mod[:, j, :], pmv[:, j, :], spl[:, SP["b_ada"]:SP["b_ada"] + 48], ALU.add, [Tpm, Tp], [Tp])
                S.barrier()
                gs1 = sbuf(les, "gs1", [128, 2, KC])
                gs2 = sbuf(les, "gs2", [128, 2, KC])
                for j in range(2):
                    stt(gs1[:, j, :], mod[:, j, 8:16], 1.0, spl[:, SP["n1g"]:SP["n1g"] + 8], ALU.add, ALU.mult, [Tp], [Tp])
                    stt(gs2[:, j, :], mod[:, j, 32:40], 1.0, spl[:, SP["n2g"]:SP["n2g"] + 8], ALU.add, ALU.mult, [Tp], [Tp])
                sh1 = lambda j, kc: mod[:, j, 0 + kc:1 + kc]
                gt1 = lambda j, kc: mod[:, j, 16 + kc:17 + kc]
                sh2 = lambda j, kc: mod[:, j, 24 + kc:25 + kc]
                gt2 = lambda j, kc: mod[:, j, 40 + kc:41 + kc]
                lamt = sbuf(les, "lamt", [128, 8])
                tt("dve", lamt[0:64, 0:1], spl[0:64, SP["lam"]:SP["lam"] + 1], spl[0:64, SP["lam"] + 1:SP["lam"] + 2], ALU.mult, [Tp], [Tp])
                tt("dve", lamt[0:64, 1:2], spl[0:64, SP["lam"] + 2:SP["lam"] + 3], spl[0:64, SP["lam"] + 3:SP["lam"] + 4], ALU.mult, [Tp], [Tp])
                pl_, Tpl = PB[1]
                mm(pl_[:, 0:2], consts[0:64, C_ONES, :], lamt[0:64, 0:2], [Tp, Tc], [Tpl])
                act(lamt[:, 2:4], pl_[:, 0:2], AF.Exp, [Tpl], [Tp])
                tt("dve", lamt[:, 4:5], lamt[:, 3:4], lamt[:, 2:3], ALU.subtract, [Tp], [Tp])
                ts("dve", lamt[:, 5:6], lamt[:, 4:5], -lam_init, ALU.add, [Tp], [Tp])
                NEGLAM = lamt[:, 5:6]
                ts("dve", lamt[:, 6:7], spl[:, SP["gq"]:SP["gq"] + 1], 0.125, ALU.mult, [Tp], [Tp])
                ts("dve", lamt[:, 7:8], spl[:, SP["subg"]:SP["subg"] + 1], 1.0 - lam_init, ALU.mult, [Tp], [Tp])
                GQ, GK, SUBG = lamt[:, 6:7], spl[:, SP["gk"]:SP["gk"] + 1], lamt[:, 7:8]

                BETA = sbuf(les, "BETA", [128, NCH, 16])
                NEGB = sbuf(les, "NEGB", [128, NCH, 16])
                GG = sbuf(les, "GG", [128, NCH, 16])
                EG = sbuf(les, "EG", [128, NCH, 3, 16])
                BGC = sbuf(les, "BGC", [128, NCH, 16])
                TG = T()

                def norm_mod(pes, tag, gs, shf, out_fn, rings):
                    xr, sqr, rtr, tmr = rings
                    for (c0, n) in tiles:
                        j = 1 if c0 < TC else 0
                        xt_, Tx_ = xr.next()
                        dma("sp", xt_[:, :, :n], Xv[:, :, c0:c0 + n], reads=[TX], writes=[Tx_])
                        sq_, Tsq = sqr.next()
                        act(sq_[:, :, :n], xt_[:, :, :n], AF.Square, [Tx_], [Tsq])
                        pn, Tpn = psB.next()
                        for kc in range(KC):
                            mm(pn[:, :n], ONES_BF, sq_[:, kc, :n], [Tsq, Tc], [Tpn], start=(kc == 0), stop=(kc == KC - 1))
                        rt_, Trt = rtr.next()
                        act(rt_[:, :n], pn[:, :n], AF.Sqrt, [Tpn, Tc], [Trt], scale=1.0 / D, bias=EPS_AP)
                        recip(rt_[:, :n], rt_[:, :n], [Trt], [Trt])
                        tm_, Ttm = tmr.next()
                        tt("dve", tm_[:, :, :n], xt_[:, :, :n], rt_[:, :n].unsqueeze(1).to_broadcast([128, KC, n]), ALU.mult, [Tx_, Trt], [Ttm])
                        out_fn(c0, n, j, tm_, Ttm, gs, shf)

                pes = ExitStack()
                with pes:
                    hT = sbuf(pes, "hT", [128, KC, TT], BF16)
                    ThT = T()
                    with ExitStack() as nes:
                        rings = (ring(nes, "nx", 2, [128, KC, 512]), ring(nes, "nsq", 2, [128, KC, 512], BF16),
                                 ring(nes, "nrt", 2, [128, 512]), ring(nes, "ntm", 2, [128, KC, 512]))

                        def out_h(c0, n, j, tm_, Ttm, gs, shf):
                            for kc in range(KC):
                                act(hT[:, kc, c0:c0 + n], tm_[:, kc, :n], AF.Identity, [Ttm, Tp], (), aw=[ThT],
                                    scale=gs[:, j, kc:kc + 1], bias=shf(j, kc))
                        norm_mod(nes, "n1", gs1, sh1, out_h, rings)
                    S.barrier()
                    if cfg.debug:
                        dma("sp", HT_d.rearrange("(kc p) t -> p kc t", p=128), hT[:], reads=[ThT])
                    if cfg.stop == "norm1" and l == cfg.stop_layer:
                        break

                    wiv = w_in_d[l].rearrange("(kc p) f -> p kc f", p=128)
                    wst = ring(pes, "wst", 2, [128, KC, 128])
                    wbf = ring(pes, "wbf", 2, [128, KC, 128], BF16)

                    def load_w(col0, ncols):
                        ws, Tws = wst.next()
                        dma("sp", ws[:, :, :ncols], wiv[:, :, col0:col0 + ncols], writes=[Tws])
                        wb, Twb = wbf.next()
                        cp("pool", wb[:, :, :ncols], ws[:, :, :ncols], [Tws], [Twb])
                        return wb, Twb

                    def proj_feat(wb, Twb, ncols, c0, n, pring=psA):
                        pp, Tpp = pring.next()
                        for kc in range(KC):
                            mm(pp[:ncols, :n], wb[:, kc, :ncols], hT[:, kc, c0:c0 + n], [Twb, ThT], [Tpp], start=(kc == 0), stop=(kc == KC - 1))
                        return pp, Tpp

                    TQKT.new_version()
                    with ExitStack() as aes:
                        qfr = ring(aes, "qf", 2, [128, 512])
                        sqr = ring(aes, "asq", 2, [128, 512], BF16)
                        rtr = ring(aes, "art", 2, [128, 512])
                        qnr = ring(aes, "qn", 2, [128, 512])
                        csr = ring(aes, "cs", 2, [128, 2, 512])
                        t1r = ring(aes, "t1", 2, [128, 512])
                        t2r = ring(aes, "t2", 2, [128, 512])
                        obr = ring(aes, "qob", 3, [128, 512], BF16)
                        CUT = int(os.environ.get("K_CUT", "99"))
                        for ch in range(16):
                            wb, Twb = load_w(ch * 128, 128)
                            gsc = GQ if ch < 8 else GK
                            for (c0, n) in tiles:
                                if CUT < 2:
                                    continue
                                pp, Tpp = proj_feat(wb, Twb, 128, c0, n)
                                if CUT < 3:
                                    continue
                                qf, Tqf = qfr.next()
                                cp("dve", qf[:, :n], pp[:, :n], [Tpp], [Tqf])
                                sq, Tsq = sqr.next()
                                act(sq[:, :n], qf[:, :n], AF.Square, [Tqf], [Tsq])
                                if CUT < 4:
                                    continue
                                pb, Tpb = psB.next()
                                mm(pb[:, :n], BD64_BF, sq[:, :n], [Tsq, Tc], [Tpb])
                                rt, Trt = rtr.next()
                                act(rt[:, :n], pb[:, :n], AF.Sqrt, [Tpb, Tc], [Trt], scale=1.0 / A_DIM, bias=EPS_AP)
                                recip(rt[:, :n], rt[:, :n], [Trt], [Trt])
                                if CUT < 5:
                                    continue
                                qn, Tqn = qnr.next()
                                stt(qn[:, :n], qf[:, :n], gsc, rt[:, :n], ALU.mult, ALU.mult, [Tqf, Trt, Tp], [Tqn])
                                if CUT < 6:
                                    continue
                                ob, Tob = obr.next()
                                if c0 >= TC and not os.environ.get('K_NOROPE'):
                                    cs, Tcs = csr.next()
                                    dma("sp", cs[:, :, :n], rope_d[:, :, c0 - TC:c0 - TC + n].rearrange("c p t -> p c t"), writes=[Tcs])
                                    pr, Tpr = psB.next()
                                    mm(pr[:, :n], PERM, qn[:, :n], [Tqn, Tc], [Tpr])
                                    t1, Tt1 = t1r.next()
                                    tt("dve", t1[:, :n], qn[:, :n], cs[:, 0, :n], ALU.mult, [Tqn, Tcs], [Tt1])
                                    t2, Tt2 = t2r.next()
                                    tt("dve", t2[:, :n], pr[:, :n], cs[:, 1, :n], ALU.mult, [Tpr, Tcs], [Tt2])
                                    tt("pool", ob[:, :n], t1[:, :n], t2[:, :n], ALU.add, [Tt1, Tt2], [Tob])
                                else:
                                    cp("act", ob[:, :n], qn[:, :n], [Tqn], [Tob])
                                dma("sp", QKT_d[ch * 128:(ch + 1) * 128, c0:c0 + n], ob[:, :n], reads=[Tob], awrites=[TQKT])
                    S.barrier()
                    if cfg.stop == "attnprep" and l == cfg.stop_layer:
                        break

                    tes = ExitStack()
                    wst5 = ring(tes, "wst5", 1, [128, KC, 512])
                    wbf5 = ring(tes, "wbf5", 2, [128, KC, 512], BF16)

                    def load_w5(col0, ncols):
                        ws, Tws = wst5.next()
                        dma("sp", ws[:, :, :ncols], wiv[:, :, col0:col0 + ncols], writes=[Tws])
                        wb, Twb = wbf5.next()
                        cp("pool", wb[:, :, :ncols], ws[:, :, :ncols], [Tws], [Twb])
                        return wb, Twb

                    def proj_tok(wb, Twb, ncols, tk):
                        pp, Tpp = psA.next()
                        for kc in range(KC):
                            mm(pp[:, :ncols], hT[:, kc, tk * 128:(tk + 1) * 128], wb[:, kc, :ncols], [Twb, ThT], [Tpp], start=(kc == 0), stop=(kc == KC - 1))
                        return pp, Tpp

                    TVTOK.new_version()
                    with ExitStack() as ves:
                        vob = ring(ves, "vob", 3, [128, 512], BF16)
                        for jb in range(2):
                            wb, Twb = load_w5(OFF_AV + jb * 512, 512)
                            for tk in range(NCH):
                                pp, Tpp = proj_tok(wb, Twb, 512, tk)
                                ob, Tob = vob.next()
                                evac(ob[:], pp[:], [Tpp], [Tob])
                                dma("sp", VTOK_d[tk * 128:(tk + 1) * 128, jb * 512:(jb + 1) * 512], ob[:], reads=[Tob], awrites=[TVTOK])
                    S.barrier()
                    TZS.new_version()
                    with ExitStack() as ves:
                        zob = ring(ves, "zob", 3, [128, 512])
                        for jb in range(2):
                            wb, Twb = load_w5(OFF_Z + jb * 512, 512)
                            for tk in range(NCH):
                                pp, Tpp = proj_tok(wb, Twb, 512, tk)
                                ob, Tob = zob.next()
                                act(ob[:], pp[:], AF.Silu, [Tpp], [Tob])
                                dma("sp", ZS_d[tk * 128:(tk + 1) * 128, jb * 512:(jb + 1) * 512], ob[:], reads=[Tob], awrites=[TZS])
                    S.barrier()
                    with ExitStack() as ves:
                        negea = sbuf(ves, "negea", [128, 16])
                        Tne = T()
                        act(negea[:], rvl[:, 0:16], AF.Exp, [Tp], [Tne])
                        ts("dve", negea[:], negea[:], -1.0, ALU.mult, [Tne], [Tne])
                        xar = ring(ves, "xa", 2, [128, 16])
                        wb, Twb = load_w5(OFF_BETA, 32)
                        for tk in range(NCH):
                            pp, Tpp = proj_tok(wb, Twb, 32, tk)
                            act(BETA[:, tk, :], pp[:, 0:16], AF.Sigmoid, [Tpp], (), aw=[TG])
                            xa, Txa = xar.next()
                            tt("dve", xa[:], pp[:, 16:32], rvl[:, 16:32], ALU.add, [Tpp, Tp], [Txa])
                            act(xa[:], xa[:], AF.Exp, [Txa], [Txa])
                            act(xa[:], xa[:], AF.Ln, [Txa, Tc], [Txa], bias=ONE_AP)
                            tt("dve", GG[:, tk, :], xa[:], negea[:], ALU.mult, [Txa, Tne], (), aw=[TG])
                            ts("dve", NEGB[:, tk, :], BETA[:, tk, :], -1.0, ALU.mult, [TG], (), aw=[TG])
                            pg, Tpg = psB.next()
                            pgv = pg[:, 0:48].rearrange("p (a b) -> p a b", a=3)
                            for d in range(2):
                                A_, R_ = (UI, SL) if d == 0 else (LI, SU)
                                gsl = GG[:, tk, d * 8:(d + 1) * 8]
                                mm(pgv[:, 0, d * 8:(d + 1) * 8], A_, gsl, [TG, Tc], [Tpg])
                                mm(pgv[:, 1, d * 8:(d + 1) * 8], R_, gsl, [TG, Tc], [Tpg])
                                mm(pgv[:, 2, d * 8:(d + 1) * 8], ONES, gsl, [TG, Tc], [Tpg])
                            act(EG[:, tk, :, :], pgv, AF.Exp, [Tpg], (), aw=[TG])
                            tt("dve", BGC[:, tk, :], BETA[:, tk, :], EG[:, tk, 0, :], ALU.mult, [TG], (), aw=[TG])
                    if cfg.debug and cfg.stop in ("prep1", "gdn", "gdnprep") and l == cfg.stop_layer:
                        o_ = 0
                        for arr, w_ in ((GG, NCH * 16), (BETA, NCH * 16), (BGC, NCH * 16), (NEGB, NCH * 16), (EG, NCH * 48)):
                            dma("sp", DBGF_d[:, o_:o_ + w_], arr[:].rearrange("p a b -> p (a b)") if arr is not EG else arr[:].rearrange("p a b c -> p (a b c)"), reads=[TG])
                            o_ += w_
                    tes.close()
                    S.barrier()
                    TSG.new_version()
                    with ExitStack() as ves:
                        sgo = ring(ves, "sgo", 3, [128, 512], BF16)
                        for ch in range(24):
                            wb, Twb = load_w(OFF_MG + ch * 128, 128)
                            for (c0, n) in tiles:
                                pp, Tpp = proj_feat(wb, Twb, 128, c0, n)
                                ob, Tob = sgo.next()
                                act(ob[:, :n], pp[:, :n], AF.Sigmoid, [Tpp], [Tob])
                                dma("sp", SG_d[ch * 128:(ch + 1) * 128, c0:c0 + n], ob[:, :n], reads=[Tob], awrites=[TSG])
                    S.barrier()
                    if cfg.stop == "prep1" and l == cfg.stop_layer:
                        break
                    raw = sbuf(pes, "raw", [128, TP])
                    acc = sbuf(pes, "acc", [128, TP])
                    Traw, Tacc = T(), T()
                    S.op("pool", lambda e, o=raw[:]: e.memset(o, 0.0), (), [Traw])
                    S.op("pool", lambda e, o=acc[:]: e.memset(o, 0.0), (), [Tacc])

                    def fill_raw(col0):
                        wb, Twb = load_w(col0, 128)
                        Traw.new_version()
                        for (c0, n) in tiles:
                            pp, Tpp = proj_feat(wb, Twb, 128, c0, n)
                            evac(raw[:, ppos(c0):ppos(c0) + n], pp[:, :n], [Tpp], (), aw=[Traw])

                    def conv(wcol, bias_ap):
                        a_out = acc[:, 2:TP - 2]
                        if bias_ap is None:
                            act(a_out, raw[:, 0:TP - 4], AF.Identity, [Traw, Tp], [Tacc], scale=spl[:, wcol:wcol + 1], bias=ZERO_AP)
                        else:
                            act(a_out, raw[:, 0:TP - 4], AF.Identity, [Traw, Tp], [Tacc], scale=spl[:, wcol:wcol + 1], bias=bias_ap)
                        for j in range(1, 5):
                            stt(a_out, raw[:, j:TP - 4 + j], spl[:, wcol + j:wcol + j + 1], a_out, ALU.mult, ALU.add, [Traw, Tp, Tacc], [Tacc])

                    TGQKV.new_version()
                    with ExitStack() as ves:
                        sqr = ring(ves, "gsq", 2, [128, 512], BF16)
                        rtr = ring(ves, "grt", 2, [128, 512])
                        gor = ring(ves, "gout", 3, [128, 512])
                        for ch in range(24):
                            typ = ch // 8
                            fill_raw(OFF_BQKV + ch * 128)
                            conv(SP["gconv"] + ch * 5, None)
                            act(acc[:, 2:TP - 2], acc[:, 2:TP - 2], AF.Silu, [Tacc], [Tacc])
                            for (c0, n) in tiles:
                                src = acc[:, ppos(c0):ppos(c0) + n]
                                if typ == 2:
                                    dma("sp", GQKV_d[ch * 128:(ch + 1) * 128, c0:c0 + n], src, reads=[Tacc], awrites=[TGQKV])
                                    continue
                                sq, Tsq = sqr.next()
                                act(sq[:, :n], src, AF.Square, [Tacc], [Tsq])
                                pb, Tpb = psB.next()
                                mm(pb[:, :n], ONES_BF, sq[:, :n], [Tsq, Tc], [Tpb])
                                rt, Trt = rtr.next()
                                act(rt[:, :n], pb[:, :n], AF.Sqrt, [Tpb, Tc], [Trt], scale=1.0, bias=EPS_AP)
                                recip(rt[:, :n], rt[:, :n], [Trt], [Trt])
                                go, Tgo = gor.next()
                                stt(go[:, :n], src, (B_DIM ** -0.5) if typ == 0 else 1.0, rt[:, :n], ALU.mult, ALU.mult, [Tacc, Trt], [Tgo])
                                dma("sp", GQKV_d[ch * 128:(ch + 1) * 128, c0:c0 + n], go[:, :n], reads=[Tgo], awrites=[TGQKV])
                    S.barrier()
                    if cfg.stop == "gdnprep" and l == cfg.stop_layer:
                        break
                    TYR.new_version()
                    with ExitStack() as ves:
                        hsum = sbuf(ves, "hsum", [128, TT])
                        Ths = [T() for _ in tiles]
                        cn = sbuf(ves, "cneg", [128, 3, 16])
                        Tcn = T()
                        act(cn[:, 0, :], spl[:, SP["llam"]:SP["llam"] + 16], AF.Exp, [Tp], [Tcn], scale=-1.0)
                        act(cn[:, 0, :], cn[:, 0, :], AF.Ln, [Tcn, Tc], [Tcn], bias=ONE_AP)
                        ts("dve", cn[:, 1, :], cn[:, 0, :], -2.0 * LRU_C, ALU.mult, [Tcn], [Tcn])
                        ts("dve", cn[:, 2, :], cn[:, 0, :], LRU_C, ALU.mult, [Tcn], [Tcn])
                        ts("dve", cn[:, 0, :], cn[:, 0, :], -LRU_C, ALU.mult, [Tcn], [Tcn])
                        lwr = ring(ves, "lw", 2, [128, 2, 2, 128])
                        rr = ring(ves, "lr", 2, [128, 512])
                        ir = ring(ves, "li", 2, [128, 512])
                        ar = ring(ves, "la", 2, [128, 512])
                        a2r = ring(ves, "la2", 2, [128, 512])
                        thr = ring(ves, "lth", 2, [128, 512])
                        hbr = ring(ves, "lhb", 2, [128, 512])
                        glr = ring(ves, "lgl", 2, [128, 512])
                        yor = ring(ves, "lyo", 3, [128, 512], BF16)
                        ctiles = [t_ for t_ in enumerate(tiles) if t_[1][0] < TC]
                        ltiles = [t_ for t_ in enumerate(tiles) if t_[1][0] >= TC]
                        for c in range(KC):
                            fill_raw(OFF_LX + c * 128)
                            conv(SP["lconv"] + c * 5, spl[:, SP["lconvb"] + c:SP["lconvb"] + c + 1])
                            lw, Tlw = lwr.next()
                            for d in range(2):
                                for ri in range(2):
                                    dma("sp", lw[:, d, ri, :], lruw_d[l, d, ri, c], writes=[Tlw] if (d == 0 and ri == 0) else (), awrites=() if (d == 0 and ri == 0) else [Tlw])
                            for d in range(2):
                                order = (ctiles + ltiles) if d == 0 else (ctiles[::-1] + ltiles[::-1])
                                carry, Tcar = None, None
                                col = d * 8 + c
                                for (ti, (c0, n)) in order:
                                    xs = acc[:, ppos(c0):ppos(c0) + n]
                                    pr, Tpr = psA.next()
                                    mm(pr[:, :n], lw[:, d, 0, :], xs, [Tlw, Tacc], [Tpr])
                                    pi, Tpi = psA.next()
                                    mm(pi[:, :n], lw[:, d, 1, :], xs, [Tlw, Tacc], [Tpi])
                                    r_, Tr = rr.next()
                                    act(r_[:, :n], pr[:, :n], AF.Sigmoid, [Tpr, Tp], [Tr], bias=spl[:, SP["lbr"] + col:SP["lbr"] + col + 1])
                                    i_, Ti = ir.next()
                                    act(i_[:, :n], pi[:, :n], AF.Sigmoid, [Tpi, Tp], [Ti], bias=spl[:, SP["lbi"] + col:SP["lbi"] + col + 1])
                                    a_, Ta = ar.next()
                                    act(a_[:, :n], r_[:, :n], AF.Exp, [Tr, Tcn], [Ta], scale=cn[:, 0, col:col + 1])
                                    a2, Ta2 = a2r.next()
                                    act(a2[:, :n], r_[:, :n], AF.Exp, [Tr, Tcn], [Ta2], scale=cn[:, 1, col:col + 1])
                                    th, Tth = thr.next()
                                    act(th[:, :n], r_[:, :n], AF.Tanh, [Tr, Tcn], [Tth], scale=cn[:, 2, col:col + 1])
                                    stt(a2[:, :n], a2[:, :n], 1.0, th[:, :n], ALU.add, ALU.mult, [Ta2, Tth], [Ta2])
                                    act(a2[:, :n], a2[:, :n], AF.Sqrt, [Ta2], [Ta2])
                                    tt("dve", i_[:, :n], i_[:, :n], a2[:, :n], ALU.mult, [Ti, Ta2], [Ti])
                                    tt("pool", i_[:, :n], i_[:, :n], xs, ALU.mult, [Ti, Tacc], [Ti])
                                    init = 0.0 if carry is None else carry
                                    rds = [Ta, Ti] + ([Tcar] if Tcar is not None else [])
                                    if d == 0:
                                        S.op("dve", lambda e, o=hsum[:, c0:c0 + n], d0=a_[:, :n], d1=i_[:, :n], ini=init:
                                             e.tensor_tensor_scan(out=o, data0=d0, data1=d1, initial=ini, op0=ALU.mult, op1=ALU.add), rds, [Ths[ti]])
                                        carry, Tcar = hsum[:, c0 + n - 1:c0 + n], Ths[ti]
                                    else:
                                        hb, Thb = hbr.next()
                                        S.op("dve", lambda e, o=hb[:, :n][:, ::-1], d0=a_[:, :n][:, ::-1], d1=i_[:, :n][:, ::-1], ini=init:
                                             e.tensor_tensor_scan(out=o, data0=d0, data1=d1, initial=ini, op0=ALU.mult, op1=ALU.add), rds, [Thb])
                                        carry, Tcar = hb[:, 0:1], Thb
                                        tt("pool", hsum[:, c0:c0 + n], hsum[:, c0:c0 + n], hb[:, :n], ALU.add, [Ths[ti], Thb], [Ths[ti]])
                            wb, Twb = load_w(OFF_LG + c * 128, 128)
                            for ti, (c0, n) in enumerate(tiles):
                                pp, Tpp = proj_feat(wb, Twb, 128, c0, n)
                                gl, Tgl = glr.next()
                                act(gl[:, :n], pp[:, :n], AF.Gelu_apprx_tanh, [Tpp], [Tgl])
                                yo, Tyo = yor.next()
                                tt("dve", yo[:, :n], gl[:, :n], hsum[:, c0:c0 + n], ALU.mult, [Tgl, Ths[ti]], [Tyo])
                                dma("sp", YR_d[c * 128:(c + 1) * 128, c0:c0 + n], yo[:, :n], reads=[Tyo], awrites=[TYR])
                S.barrier()
                if cfg.stop == "prep" and l == cfg.stop_layer:
                    break
                TYA.new_version()
                with ExitStack() as aes:
                    kTr = ring(aes, "kTh", 2, [128, TT], BF16)
                    Vr = ring(aes, "Vh", 2, [128, NCH, 128], BF16)
                    qbr = ring(aes, "qb", 2, [128, 512], BF16)
                    ptr = ring(aes, "pt", 4, [128, 512], BF16)
                    r0r = ring(aes, "ar0", 2, [128, 512])
                    t0r = ring(aes, "at0", 2, [128, 512])
                    t1r = ring(aes, "at1", 2, [128, 512])
                    sqr = ring(aes, "asq2", 2, [128, 512], BF16)
                    yar = ring(aes, "aya", 2, [128, 512], BF16)
                    VTv = VTOK_d.rearrange("(n p) e -> p n e", p=128)
                    for h in range(A_HEADS):
                        kT, TkT = kTr.next()
                        dma("sp", kT[:], QKT_d[1024 + h * 128:1024 + (h + 1) * 128, :], reads=[TQKT], writes=[TkT])
                        V, TV = Vr.next()
                        TV.new_version()
                        for k0_ in range(0, NCH, 8):
                            k1_ = min(NCH, k0_ + 8)
                            dma("sp", V[:, k0_:k1_, :], VTv[:, k0_:k1_, h * 128:(h + 1) * 128], reads=[TVTOK], awrites=[TV])
                        blocks = ([(0, TC, 0, NCC)] if need_ctx else []) + [(c0, n, 0, NCH) for (c0, n) in tiles if c0 >= TC]
                        for (q0, nq, k0, k1) in blocks:
                            qb, Tqb = qbr.next()
                            dma("sp", qb[:, :nq], QKT_d[h * 128:(h + 1) * 128, q0:q0 + nq], reads=[TQKT], writes=[Tqb])
                            (O0, TO0), (Z0, TZ0), (O1, TO1), (Z1, TZ1) = PB[0], PB[1], PB[2], PB[3]
                            OZ = ((O0, TO0, Z0, TZ0), (O1, TO1, Z1, TZ1))
                            for kt in range(k0, k1):
                                for c in range(2):
                                    st, Tst = psB.next()
                                    mm(st[:, :nq], kT[c * 64:(c + 1) * 64, kt * 128:(kt + 1) * 128], qb[c * 64:(c + 1) * 64, :nq], [TkT, Tqb], [Tst])
                                    pt, Tpt = ptr.next()
                                    act(pt[:, :nq], st[:, :nq], AF.Exp, [Tst], [Tpt])
                                    O_, TO_, Z_, TZ_ = OZ[c]
                                    mm(O_[:, :nq], V[:, kt, :], pt[:, :nq], [TV, Tpt], [TO_], start=(kt == k0), stop=(kt == k1 - 1))
                                    mm(Z_[:, :nq], ONES_BF, pt[:, :nq], [Tpt, Tc], [TZ_], start=(kt == k0), stop=(kt == k1 - 1))
                            r0, Tr0 = r0r.next()
                            recip(r0[:, :nq], Z0[:, :nq], [TZ0], [Tr0])
                            t0, Tt0 = t0r.next()
                            tt("dve", t0[:, :nq], O0[:, :nq], r0[:, :nq], ALU.mult, [TO0, Tr0], [Tt0])
                            r1, Tr1 = r0r.next()
                            recip(r1[:, :nq], Z1[:, :nq], [TZ1], [Tr1])
                            t1, Tt1 = t1r.next()
                            tt("dve", t1[:, :nq], O1[:, :nq], r1[:, :nq], ALU.mult, [TO1, Tr1], [Tt1])
                            stt(t0[:, :nq], t1[:, :nq], NEGLAM, t0[:, :nq], ALU.mult, ALU.add, [Tt1, Tt0, Tp], [Tt0])
                            sq, Tsq = sqr.next()
                            act(sq[:, :nq], t0[:, :nq], AF.Square, [Tt0], [Tsq])
                            pss, Tpss = psB.next()
                            mm(pss[:, :nq], ONES_BF, sq[:, :nq], [Tsq, Tc], [Tpss])
                            act(r1[:, :nq], pss[:, :nq], AF.Sqrt, [Tpss, Tc], [Tr1], scale=1.0 / 128.0, bias=EPS_AP)
                            recip(r1[:, :nq], r1[:, :nq], [Tr1], [Tr1])
                            ya, Tya = yar.next()
                            stt(ya[:, :nq], t0[:, :nq], SUBG, r1[:, :nq], ALU.mult, ALU.mult, [Tt0, Tr1, Tp], [Tya])
                            dma("sp", YA_d[h * 128:(h + 1) * 128, q0:q0 + nq], ya[:, :nq], reads=[Tya], awrites=[TYA])
                S.barrier()
                if cfg.stop == "attn" and l == cfg.stop_layer:
                    break

                TOD.new_version()
                for d in range(2):
                    with ExitStack() as ges:
                        A_, B_, Ms, MiT = (UI, SL, SL, UI) if d == 0 else (LI, SU, SU, LI)
                        order = list(range(NCH)) if d == 0 else (list(range(NCC - 1, -1, -1)) + list(range(NCH - 1, NCC - 1, -1)))

                        def chain(h):
                            col = d * 8 + h
                            nm = f"g{d}{h}"
                            mk = lambda n_: (sbuf(ges, nm + n_, [128, 128]), T())
                            (St, TS), (qT, TqT), (kT, TkT), (vT, TvT) = mk("S"), mk("q"), mk("k"), mk("v")
                            (g2, Tg2), (e_, Te), (et_, Tet), (Ru, TRu), (Rw, TRw), (kd, Tkd) = mk("g2"), mk("e"), mk("et"), mk("Ru"), mk("Rw"), mk("kd")
                            (N0, TN0), (NT0, TNT0), (N1, TN1), (NT1, TNT1) = mk("N0"), mk("NT0"), mk("N1"), mk("NT1")
                            (P0, TP0), (P1, TP1), (qkT, Tqk) = mk("P0"), mk("P1"), mk("qk")
                            (D2, TD2), (E2, TE2), (Ysb, TY), (Zsb, TZ) = mk("D2"), mk("E2"), mk("Ysb"), mk("Zsb")
                            (u_, Tu), (wT, TwT), (vn, Tvn), (o1s, To1), (oo, Too) = mk("u"), mk("wT"), mk("vn"), mk("o1"), mk("oo")
                            S.op("pool", lambda e, o=St[:]: e.memset(o, 0.0), (), [TS])
                            Tb_ = PB[h][1]
                            Q = [PB[h][0][:, i_ * 128:(i_ + 1) * 128] for i_ in range(4)]
                            yield
                            for n in order:
                                c0 = n * 128
                                dma("sp", qT[:], GQKV_d[h * 128:(h + 1) * 128, c0:c0 + 128], reads=[TGQKV], writes=[TqT])
                                dma("sp", kT[:], GQKV_d[1024 + h * 128:1024 + (h + 1) * 128, c0:c0 + 128], reads=[TGQKV], writes=[TkT])
                                dma("sp", vT[:], GQKV_d[2048 + h * 128:2048 + (h + 1) * 128, c0:c0 + 128], reads=[TGQKV], writes=[TvT])
                                ts("pool", g2[:], B_, GG[:, n, col:col + 1], ALU.mult, [TG, Tc], [Tg2])
                                yield
                                pk, pv, pd, pdt = Q
                                tr(pk, kT[:], [TkT], [Tb_])
                                tr(pv, vT[:], [TvT], [Tb_])
                                mm(pd, A_, g2[:], [Tg2, Tc], [Tb_])
                                mm(pdt, g2[:], A_, [Tg2, Tc], [Tb_])
                                yield
                                act(e_[:], pd, AF.Exp, [Tb_], [Te])
                                act(et_[:], pdt, AF.Exp, [Tb_], [Tet])
                                act(kd[:], pk, AF.Identity, [Tb_, TG, Tc], [Tkd], scale=EG[:, n, 1, col:col + 1], bias=ZERO_AP)
                                yield
                                ts("dve", Ru[:], pv, BETA[:, n, col:col + 1], ALU.mult, [Tb_, TG], [TRu])
                                ts("dve", Rw[:], pk, BGC[:, n, col:col + 1], ALU.mult, [Tb_, TG], [TRw])
                                tt("pool", e_[:], e_[:], Ms, ALU.mult, [Te, Tc], [Te])
                                tt("pool", et_[:], et_[:], MiT, ALU.mult, [Tet, Tc], [Tet])
                                yield
                                pkk, pkq = Q[0], Q[1]
                                mm(pkk, kT[:], kT[:], [TkT], [Tb_])
                                mm(pkq, kT[:], qT[:], [TkT, TqT], [Tb_])
                                yield
                                stt(N0[:], pkk, NEGB[:, n, col:col + 1], e_[:], ALU.mult, ALU.mult, [Tb_, TG, Te], [TN0])
                                tt("dve", qkT[:], pkq, et_[:], ALU.mult, [Tb_, Tet], [Tqk])
                                yield
                                pn = Q[2]
                                tr(pn, N0[:], [TN0], [Tb_])
                                yield
                                cp("act", NT0[:], pn, [Tb_], [TNT0])
                                yield
                                OFFK = [consts[:, C_OFF + k_, :] for k_ in range(7)]
                                tt("pool", N1[:], N0[:], OFFK[0], ALU.mult, [TN0, Tc], [TN1])
                                tt("pool", NT1[:], NT0[:], OFFK[0], ALU.mult, [TNT0, Tc], [TNT1])
                                yield
                                tt("pool", P0[:], N1[:], IDENT, ALU.add, [TN1, Tc], [TP0])
                                tt("pool", P1[:], NT1[:], IDENT, ALU.add, [TNT1, Tc], [TP1])
                                Dc, Ec = (P0, TP0), (P1, TP1)
                                Dn, En = (D2, TD2), (E2, TE2)
                                for k in range(1, 7):
                                    last = (k == 6)
                                    tt("pool", N1[:], N0[:], OFFK[k], ALU.mult, [TN0, Tc], [TN1])
                                    if not last:
                                        tt("pool", NT1[:], NT0[:], OFFK[k], ALU.mult, [TNT0, Tc], [TNT1])
                                    yield
                                    mm(Q[0], N1[:], Ec[0][:], [TN1, Ec[1]], [Tb_])
                                    if not last:
                                        mm(Q[1], NT1[:], Dc[0][:], [TNT1, Dc[1]], [Tb_])
                                    yield
                                    cp("act", Ysb[:], Q[0], [Tb_], [TY])
                                    if not last:
                                        cp("dve", Zsb[:], Q[1], [Tb_], [TZ])
                                    yield
                                    mm(Q[2], Dc[0][:], Ysb[:], [Dc[1], TY], [Tb_])
                                    if not last:
                                        mm(Q[3], Ec[0][:], Zsb[:], [Ec[1], TZ], [Tb_])
                                    yield
                                    tt("dve", En[0][:], Q[2], Ec[0][:], ALU.add, [Tb_, Ec[1]], [En[1]])
                                    if not last:
                                        tt("dve", Dn[0][:], Q[3], Dc[0][:], ALU.add, [Tb_, Dc[1]], [Dn[1]])
                                    Dc, Dn = Dn, Dc
                                    Ec, En = En, Ec
                                    yield
                                Pc = Ec
                                PI, TPI = Pc
                                pu, pw_ = Q[0], Q[1]
                                mm(pu, PI[:], Ru[:], [TPI, TRu], [Tb_])
                                mm(pw_, Rw[:], PI[:], [TPI, TRw], [Tb_])
                                yield
                                cp("act", u_[:], pu, [Tb_], [Tu])
                                cp("act", wT[:], pw_, [Tb_], [TwT])
                                yield
                                pa, po = Q[2], Q[3]
                                mm(pa, wT[:], St[:], [TwT, TS], [Tb_])
                                mm(po, qT[:], St[:], [TqT, TS], [Tb_])
                                yield
                                tt("dve", vn[:], u_[:], pa, ALU.subtract, [Tu, Tb_], [Tvn])
                                ts("dve", o1s[:], po, EG[:, n, 0, col:col + 1], ALU.mult, [Tb_, TG], [To1])
                                yield
                                po2, ps_ = Q[0], Q[1]
                                mm(po2, qkT[:], vn[:], [Tqk, Tvn], [Tb_])
                                mm(ps_, kd[:], vn[:], [Tkd, Tvn], [Tb_])
                                yield
                                if cfg.debug and os.environ.get("K_GDBG") and d == 0 and h == 0 and n == order[0]:
                                    for i_, (arr_, T_) in enumerate(((e_, Te), (N0, TN0), (PI, TPI), (u_, Tu), (qkT, Tqk))):
                                        dma("sp", DBGF_d[:, i_ * 128:(i_ + 1) * 128], arr_[:], reads=[T_])
                                tt("dve", oo[:], po2, o1s[:], ALU.add, [Tb_, To1], [Too])
                                stt(St[:], St[:], EG[:, n, 2, col:col + 1], ps_, ALU.mult, ALU.add, [TS, TG, Tb_], [TS])
                                dma("sp", OD_d[d, c0:c0 + 128, h * 128:(h + 1) * 128], oo[:], reads=[Too], awrites=[TOD])
                                yield

                        interleave([chain(h) for h in range(B_HEADS)])
                    S.barrier()
                if cfg.stop == "gdn" and l == cfg.stop_layer:
                    break

                TYB.new_version()
                with ExitStack() as fes:
                    o0r = ring(fes, "fo0", 2, [128, D])
                    o1r = ring(fes, "fo1", 2, [128, D])
                    zsr = ring(fes, "fzs", 2, [128, D])
                    jkr = ring(fes, "fjk", 2, [128, 128])
                    ssr = ring(fes, "fss", 2, [128, 8])
                    ybr = ring(fes, "fyb", 3, [128, KC, 128], BF16)
                    for tk in range(NCH):
                        o0, To0 = o0r.next()
                        dma("sp", o0[:], OD_d[0, tk * 128:(tk + 1) * 128, :], reads=[TOD], writes=[To0])
                        o1, To1_ = o1r.next()
                        dma("sp", o1[:], OD_d[1, tk * 128:(tk + 1) * 128, :], reads=[TOD], writes=[To1_])
                        zs, Tzs = zsr.next()
                        dma("sp", zs[:], ZS_d[tk * 128:(tk + 1) * 128, :], reads=[TZS], writes=[Tzs])
                        tt("pool", o0[:], o0[:], o1[:], ALU.add, [To0, To1_], [To0])
                        ss, Tss = ssr.next()
                        Tss.new_version()
                        for h in range(B_HEADS):
                            jk, Tjk = jkr.next()
                            act(jk[:], o0[:, h * 128:(h + 1) * 128], AF.Square, [To0], [Tjk], aw=[Tss], accum_out=ss[:, h:h + 1])
                        act(ss[:], ss[:], AF.Sqrt, [Tss, Tc], [Tss], scale=1.0 / B_DIM, bias=EPS_AP)
                        recip(ss[:], ss[:], [Tss], [Tss])
                        o3 = o0[:].rearrange("p (h e) -> p h e", h=B_HEADS)
                        tt("dve", o3, o3, ss[:].unsqueeze(2).to_broadcast([128, B_HEADS, 128]), ALU.mult, [To0, Tss], [To0])
                        tt("dve", o3, o3, rvl[:, 32:160].unsqueeze(1).to_broadcast([128, B_HEADS, 128]), ALU.mult, [To0, Tp], [To0])
                        tt("pool", o0[:], o0[:], zs[:], ALU.mult, [To0, Tzs], [To0])
                        yb, Tyb = ybr.next()
                        Tyb.new_version()
                        for h in range(B_HEADS):
                            pq, Tpq = psQ.next()
                            tr(pq, o0[:, h * 128:(h + 1) * 128], [To0], [Tpq])
                            evac(yb[:, h, :], pq, [Tpq], (), aw=[Tyb])
                        dma("sp", YB_d.rearrange("(h p) t -> p h t", p=128)[:, :, tk * 128:(tk + 1) * 128], yb[:], reads=[Tyb], awrites=[TYB])
                S.barrier()
                if cfg.stop == "gdnfin" and l == cfg.stop_layer:
                    break

                TX.new_version()
                mtiles = []
                for (c0, n) in tiles:
                    for s0 in range(0, n, 256):
                        mtiles.append((c0 + s0, min(256, n - s0)))
                with ExitStack() as mes:
                    wbr = sbuf(mes, "wbr", [128, 3, KC, D], BF16)
                    wou = sbuf(mes, "wou", [128, KC, D], BF16)
                    Twm = T()
                    wstg = ring(mes, "wstg", 2, [128, KC, 256])
                    srcs = [(w_br_d[l, k].rearrange("(kc p) f -> p kc f", p=128), wbr[:, k]) for k in range(3)] + [(w_out_d[l].rearrange("(kc p) f -> p kc f", p=128), wou[:])]
                    for (sv, dv) in srcs:
                        for jb in range(4):
                            ws, Tws = wstg.next()
                            dma("sp", ws[:], sv[:, :, jb * 256:(jb + 1) * 256], writes=[Tws])
                            cp("pool", dv[:, :, jb * 256:(jb + 1) * 256], ws[:], [Tws], (), aw=[Twm])
                    ysr = [ring(mes, f"ys{k}", 2, [128, KC, 256], BF16) for k in range(3)]
                    sgr = ring(mes, "msg", 2, [128, KC, 256], BF16)
                    yacr = ring(mes, "yacc", 1, [128, KC, 256])
                    tmr = ring(mes, "mtm", 2, [128, 256])
                    ybfr = ring(mes, "ybf", 2, [128, KC, 256], BF16)
                    xtr = ring(mes, "mxt", 2, [128, KC, 256])
                    xor = ring(mes, "mxo", 2, [128, KC, 256])
                    YS_d = (YA_d, YB_d, YR_d)
                    TYS = (TYA, TYB, TYR)
                    SGv = SG_d.rearrange("(k c p) t -> p k c t", p=128, k=3)
                    for (c0, n) in mtiles:
                        if c0 < TC and not need_ctx:
                            continue
                        j = 1 if c0 < TC else 0
                        xt_, Txt = xtr.next()
                        dma("sp", xt_[:, :, :n], Xv[:, :, c0:c0 + n], reads=[TX], writes=[Txt])
                        yac, Tyac = yacr.next()
                        for k in range(3):
                            y_, Ty_ = ysr[k].next()
                            dma("sp", y_[:, :, :n], YS_d[k].rearrange("(kc p) t -> p kc t", p=128)[:, :, c0:c0 + n], reads=[TYS[k]], writes=[Ty_])
                            sg, Tsg = sgr.next()
                            dma("sp", sg[:, :, :n], SGv[:, k, :, c0:c0 + n], reads=[TSG], writes=[Tsg])
                            for fo in range(KC):
                                pp, Tpp = psAll.next()
                                for kc in range(KC):
                                    mm(pp[:, :n], wbr[:, k, kc, fo * 128:(fo + 1) * 128], y_[:, kc, :n], [Twm, Ty_], [Tpp], start=(kc == 0), stop=(kc == KC - 1))
                                if k == 0:
                                    tt("dve", yac[:, fo, :n], pp[:, :n], sg[:, fo, :n], ALU.mult, [Tpp, Tsg], (), aw=[Tyac])
                                else:
                                    tm, Ttm = tmr.next()
                                    tt("dve", tm[:, :n], pp[:, :n], sg[:, fo, :n], ALU.mult, [Tpp, Tsg], [Ttm])
                                    tt("pool", yac[:, fo, :n], yac[:, fo, :n], tm[:, :n], ALU.add, [Ttm, Tyac], (), aw=[Tyac])
                        ybf, Tybf = ybfr.next()
                        cp("act", ybf[:, :, :n], yac[:, :, :n], [Tyac], [Tybf])
                        Tyac.new_version()
                        xo, Txo = xor.next()
                        Txo.new_version()
                        for fo in range(KC):
                            pp, Tpp = psAll.next()
                            for kc in range(KC):
                                mm(pp[:, :n], wou[:, kc, fo * 128:(fo + 1) * 128], ybf[:, kc, :n], [Twm, Tybf], [Tpp], start=(kc == 0), stop=(kc == KC - 1))
                            stt(xo[:, fo, :n], pp[:, :n], gt1(j, fo), xt_[:, fo, :n], ALU.mult, ALU.add, [Tpp, Tp, Txt], (), aw=[Txo])
                        dma("sp", Xv[:, :, c0:c0 + n], xo[:, :, :n], reads=[Txo], awrites=[TX])
                S.barrier()
                if cfg.stop == "merge" and l == cfg.stop_layer:
                    break
                H2F_T = T()
                TH2.new_version()
                ptok0 = 0 if need_ctx else TC
                ptiles = [(c0, n) for (c0, n) in mtiles if c0 >= ptok0]
                H2F_d = GQKV_d[0:D, :]
                H2Fv = H2F_d.rearrange("(kc p) t -> p kc t", p=128)
                with ExitStack() as nes:
                    rings = (ring(nes, "n2x", 2, [128, KC, 512]), ring(nes, "n2sq", 2, [128, KC, 512], BF16),
                             ring(nes, "n2rt", 2, [128, 512]), ring(nes, "n2tm", 1, [128, KC, 512]))
                    h2r = ring(nes, "h2o", 2, [128, KC, 512])
                    htr = ring(nes, "h2t", 2, [128, D])

                    def out_h2(c0, n, j, tm_, Ttm, gs, shf):
                        if c0 + n <= ptok0:
                            return
                        h2, Th2 = h2r.next()
                        Th2.new_version()
                        for kc in range(KC):
                            act(h2[:, kc, :n], tm_[:, kc, :n], AF.Identity, [Ttm, Tp], (), aw=[Th2], scale=gs[:, j, kc:kc + 1], bias=shf(j, kc))
                        dma("sp", H2Fv[:, :, c0:c0 + n], h2[:, :, :n], reads=[Th2], awrites=[H2F_T])
                        for s0 in range(0, n, 128):
                            ht, Tht = htr.next()
                            Tht.new_version()
                            for kc in range(KC):
                                pq, Tpq = psQ.next()
                                tr(pq, h2[:, kc, s0:s0 + 128], [Th2], [Tpq])
                                evac(ht[:, kc * 128:(kc + 1) * 128], pq, [Tpq], (), aw=[Tht])
                            dma("sp", H2_d[c0 + s0:c0 + s0 + 128, :], ht[:], reads=[Tht], awrites=[TH2])
                    norm_mod(nes, "n2", gs2, sh2, out_h2, rings)
                S.barrier()
                if cfg.stop == "peernorm" and l == cfg.stop_layer:
                    break
                IDXT = sbuf(les, "IDXT", [128, TT], U32)
                GATET = sbuf(les, "GATET", [128, TT])
                TIG = T()
                with ExitStack() as qes:
                    skt = sbuf(qes, "skt", [128, 16, 128])
                    Tsk = T()
                    for g0_ in range(0, 16, 4):
                        dma("sp", skt[:, g0_:g0_ + 4, :], skT_d[l, g0_:g0_ + 4].rearrange("g d k -> d g k"), awrites=[Tsk])
                    wqv = wq_d[l].rearrange("(kc p) f -> p kc f", p=128)
                    wqr = ring(qes, "wq", 2, [128, KC, 128])
                    h2r = ring(qes, "h2i", 2, [128, KC, 256])
                    QN = sbuf(qes, "QN", [128, 16, 256])
                    TQN = T()
                    qfr = ring(qes, "pqf", 2, [128, 256])
                    sqr = ring(qes, "psq", 2, [128, 256])
                    rtr = ring(qes, "prt", 2, [128, 256])
                    SCr = ring(qes, "SC", 2, [128, 16, 128])
                    m1r = ring(qes, "m1", 2, [128, 16, 16])
                    ixr = ring(qes, "ix", 2, [128, 16, 16], U32)
                    wkr = ring(qes, "wk", 2, [128, 256])
                    ixf = sbuf(qes, "ixf", [128, 16, 16])
                    cand = sbuf(qes, "cand", [128, 8, 256])
                    b1 = sbuf(qes, "b1", [128, 8, 16])
                    pos = sbuf(qes, "pos", [128, 8, 16], U32)
                    pab = sbuf(qes, "pab", [128, 2, 8, 16], U32)
                    pabf = sbuf(qes, "pabf", [128, 2, 8, 16])
                    oh = sbuf(qes, "oh", [128, 8, 16, 16])
                    isel = sbuf(qes, "isel", [128, 2, 8, 16])
                    idxf = sbuf(qes, "idxf", [128, 128])
                    gate = sbuf(qes, "gate", [128, 8, 16])
                    gsm = sbuf(qes, "gsm", [128, 8])
                    Tk_ = T()
                    IOTA16 = consts[:, C_IOTA0, 0:16]
                    for (c0, n) in ptiles:
                        h2, Th2 = h2r.next()
                        dma("sp", h2[:, :, :n], H2Fv[:, :, c0:c0 + n], reads=[H2F_T], writes=[Th2])
                        TQN.new_version()
                        for gi in range(16):
                            wq, Twq = wqr.next()
                            dma("sp", wq[:], wqv[:, :, gi * 128:(gi + 1) * 128], writes=[Twq])
                            pp, Tpp = psA.next()
                            for kc in range(KC):
                                mm(pp[:, :n], wq[:, kc, :], h2[:, kc, :n], [Twq, Th2], [Tpp], start=(kc == 0), stop=(kc == KC - 1))
                            qf, Tqf = qfr.next()
                            cp("dve", qf[:, :n], pp[:, :n], [Tpp], [Tqf])
                            sq, Tsq = sqr.next()
                            act(sq[:, :n], qf[:, :n], AF.Square, [Tqf], [Tsq])
                            pb, Tpb = psB.next()
                            mm(pb[:, :n], ONES, sq[:, :n], [Tsq, Tc], [Tpb])
                            rt, Trt = rtr.next()
                            act(rt[:, :n], pb[:, :n], AF.Sqrt, [Tpb, Tc], [Trt], scale=1.0 / 128.0, bias=EPS_AP)
                            recip(rt[:, :n], rt[:, :n], [Trt], [Trt])
                            tt("dve", QN[:, gi, :n], qf[:, :n], rt[:, :n], ALU.mult, [Tqf, Trt], (), aw=[TQN])
                        for s0 in range(0, n, 128):
                            tk0 = c0 + s0
                            SC, TSC = SCr.next()
                            TSC.new_version()
                            for gq in range(4):
                                pp, Tpp = psA.next()
                                for g4 in range(4):
                                    gi = gq * 4 + g4
                                    mm(pp[:, g4 * 128:(g4 + 1) * 128], QN[:, gi, s0:s0 + 128], skt[:, gi, :], [TQN, Tsk], [Tpp])
                                evac(SC[:, gq * 4:(gq + 1) * 4, :], pp[:].rearrange("p (a b) -> p a b", a=4), [Tpp], (), aw=[TSC])
                            m1, Tm1 = m1r.next()
                            ix, Tix = ixr.next()
                            for gi in range(16):
                                wk, Twk = wkr.next()
                                S.op("dve", lambda e, o=m1[:, gi, 0:8], i=SC[:, gi, :]: e.max(out=o, in_=i), [TSC], [Tm1])
                                S.op("dve", lambda e, o=wk[:, 0:128], r=m1[:, gi, 0:8], i=SC[:, gi, :]: e.match_replace(out=o, in_to_replace=r, in_values=i, imm_value=-1e30), [TSC, Tm1], [Twk])
                                S.op("dve", lambda e, o=m1[:, gi, 8:16], i=wk[:, 0:128]: e.max(out=o, in_=i), [Twk], [Tm1])
                                S.op("dve", lambda e, o=ix[:, gi, 0:8], r=m1[:, gi, 0:8], i=SC[:, gi, :]: e.max_index(out=o, in_max=r, in_values=i), [TSC, Tm1], [Tix])
                                S.op("dve", lambda e, o=ix[:, gi, 8:16], r=m1[:, gi, 8:16], i=SC[:, gi, :]: e.max_index(out=o, in_max=r, in_values=i), [TSC, Tm1], [Tix])
                            cp("dve", ixf[:], ix[:], [Tix], [Tk_])
                            m1v = m1[:].rearrange("p (h c) k -> p h c k", c=2)
                            ixv = ixf[:].rearrange("p (h c) k -> p h c k", c=2)
                            cand4 = cand[:].rearrange("p h (a b) -> p h a b", a=16)
                            tt("dve", cand4, m1v[:, :, 0, :].unsqueeze(3).to_broadcast([128, 8, 16, 16]),
                               m1v[:, :, 1, :].unsqueeze(2).to_broadcast([128, 8, 16, 16]), ALU.add, [Tm1], [Tk_])
                            for h in range(P_HEADS):
                                wk, Twk = wkr.next()
                                S.op("dve", lambda e, o=b1[:, h, 0:8], i=cand[:, h, :]: e.max(out=o, in_=i), [Tk_], [Tk_])
                                S.op("dve", lambda e, o=wk[:], r=b1[:, h, 0:8], i=cand[:, h, :]: e.match_replace(out=o, in_to_replace=r, in_values=i, imm_value=-1e30), [Tk_], [Twk])
                                S.op("dve", lambda e, o=b1[:, h, 8:16], i=wk[:]: e.max(out=o, in_=i), [Twk], [Tk_])
                                S.op("dve", lambda e, o=pos[:, h, 0:8], r=b1[:, h, 0:8], i=cand[:, h, :]: e.max_index(out=o, in_max=r, in_values=i), [Tk_], [Tk_])
                                S.op("dve", lambda e, o=pos[:, h, 8:16], r=b1[:, h, 8:16], i=cand[:, h, :]: e.max_index(out=o, in_max=r, in_values=i), [Tk_], [Tk_])
                            ts("dve", pab[:, 0], pos[:], 4, ALU.logical_shift_right, [Tk_], [Tk_])
                            ts("dve", pab[:, 1], pos[:], 15, ALU.bitwise_and, [Tk_], [Tk_])
                            cp("dve", pabf[:], pab[:], [Tk_], [Tk_])
                            for c in range(2):
                                tt("dve", oh[:], IOTA16.unsqueeze(1).unsqueeze(1).to_broadcast([128, 8, 16, 16]),
                                   pabf[:, c].unsqueeze(3).to_broadcast([128, 8, 16, 16]), ALU.is_equal, [Tk_, Tc], [Tk_])
                                tt("dve", oh[:], oh[:], ixv[:, :, c, :].unsqueeze(2).to_broadcast([128, 8, 16, 16]), ALU.mult, [Tk_], [Tk_])
                                S.op("dve", lambda e, o=isel[:, c], i=oh[:]: e.tensor_reduce(out=o, in_=i, axis=AX.X, op=ALU.add), [Tk_], [Tk_])
                            stt(idxf[:].rearrange("p (h k) -> p h k", h=8), isel[:, 0], 128.0, isel[:, 1], ALU.mult, ALU.add, [Tk_], [Tk_])
                            tt("dve", gate[:], b1[:], b1[:, :, 0:1].to_broadcast([128, 8, 16]), ALU.subtract, [Tk_], [Tk_])
                            act(gate[:], gate[:], AF.Exp, [Tk_], [Tk_])
                            S.op("dve", lambda e, o=gsm[:], i=gate[:]: e.tensor_reduce(out=o, in_=i, axis=AX.X, op=ALU.add), [Tk_], [Tk_])
                            recip(gsm[:], gsm[:], [Tk_], [Tk_])
                            tt("dve", gate[:], gate[:], gsm[:].unsqueeze(2).to_broadcast([128, 8, 16]), ALU.mult, [Tk_], [Tk_])
                            pq, Tpq = psQ.next()
                            tr(pq, idxf[:], [Tk_], [Tpq])
                            cp("dve", IDXT[:, tk0:tk0 + 128], pq, [Tpq], (), aw=[TIG])
                            pq2, Tpq2 = psQ.next()
                            tr(pq2, gate[:].rearrange("p h k -> p (h k)"), [Tk_], [Tpq2])
                            cp("act", GATET[:, tk0:tk0 + 128], pq2, [Tpq2], (), aw=[TIG])
                S.barrier()
                if cfg.stop == "peertopk" and l == cfg.stop_layer:
                    if cfg.debug:
                        dma("sp", DBGF_d[:, ptok0:TT], GATET[:, ptok0:TT], reads=[TIG])
                        dma("sp", DBGI_d[:, ptok0:TT], IDXT[:, ptok0:TT], reads=[TIG])
                    break
                TX.new_version()
                with ExitStack() as ges:
                    SEL = sbuf(ges, "SEL", [128, 128, 128], BF16)
                    Tsel = T()
                    cp("dve", SEL[:], IDENT.unsqueeze(2).to_broadcast([128, 128, 128]), [Tc], [Tsel])
                    ugr = ring(ges, "ug", 4, [128, D], BF16)
                    vgr = ring(ges, "vg", 4, [128, D], BF16)
                    h2fr = ring(ges, "h2f", 2, [128, D])
                    h2br = ring(ges, "h2b", 2, [128, D], BF16)
                    jkr2 = ring(ges, "pjunk", 2, [128, 512], BF16)
                    dotr = ring(ges, "dots", 2, [128, 128, 2])
                    ctr = ring(ges, "ct", 2, [128, 128])
                    ctbr = ring(ges, "ctb", 2, [128, 128], BF16)
                    xtr = ring(ges, "pxt", 2, [128, KC, 128])
                    xor = ring(ges, "pxo", 2, [128, KC, 128])
                    eoff = l * P_KEYS * P_KEYS * D
                    pbank = 4
                    hbank = 0
                    for tk0 in range(ptok0, TT, 128):
                        j = 1 if tk0 < TC else 0
                        h2f, Th2f = h2fr.next()
                        dma("sp", h2f[:], H2_d[tk0:tk0 + 128, :], reads=[TH2], writes=[Th2f])
                        h2b, Th2b = h2br.next()
                        cp("act", h2b[:], h2f[:], [Th2f], [Th2b])
                        dots, Tdots = dotr.next()
                        Tdots.new_version()
                        for i in range(128):
                            n = tk0 + i
                            ug, Tug = ugr.next()
                            S.dma("pool", lambda e, o=ug[:], ix=IDXT[:, n:n + 1], eoff=eoff: e.indirect_dma_start(
                                out=o, out_offset=None, in_=UB_d, in_offset=bass.IndirectOffsetOnAxis(ap=ix, axis=0), element_offset=eoff), [TIG, TUB], [Tug])
                            for hf in range(2):
                                hbk, Thbk = PB[hbank]
                                hbank = (hbank + 1) % 4
                                mm(hbk[:, :], SEL[:, i, :], h2b[:, hf * 512:(hf + 1) * 512], [Tsel, Th2b], [Thbk])
                                junk, Tjunk = jkr2.next()
                                stt(junk[:], ug[:, hf * 512:(hf + 1) * 512], 1.0, hbk[:, :], ALU.mult, ALU.mult, [Tug, Thbk], [Tjunk],
                                    accum_out=dots[:, i, hf:hf + 1], aw=[Tdots])
                        ct, Tct = ctr.next()
                        tt("dve", ct[:], dots[:, :, 0], dots[:, :, 1], ALU.add, [Tdots], [Tct])
                        act(ct[:], ct[:], AF.Gelu_apprx_tanh, [Tct], [Tct])
                        ctb, Tctb = ctbr.next()
                        tt("dve", ctb[:], ct[:], GATET[:, tk0:tk0 + 128], ALU.mult, [Tct, TIG], [Tctb])
                        (pA, TpA), (pB_, TpB) = PB[pbank], PB[pbank + 1]
                        pbank = 4 + (pbank - 4 + 2) % 4
                        for i in range(128):
                            n = tk0 + i
                            vg, Tvg = vgr.next()
                            S.dma("pool", lambda e, o=vg[:], ix=IDXT[:, n:n + 1], eoff=eoff: e.indirect_dma_start(
                                out=o, out_offset=None, in_=VB_d, in_offset=bass.IndirectOffsetOnAxis(ap=ix, axis=0), element_offset=eoff), [TIG, TUB], [Tvg])
                            for kc in range(KC):
                                pt_, Tpt_ = (pA, TpA) if kc < 4 else (pB_, TpB)
                                S.op("pe", lambda e, o=pt_[:, (kc % 4) * 128 + i:(kc % 4) * 128 + i + 1], w=vg[:, kc * 128:(kc + 1) * 128], r=ctb[:, i:i + 1]:
                                     e.matmul(o, w, r, start=True, stop=True), [Tvg, Tctb], [Tpt_])
                        xt_, Txt = xtr.next()
                        dma("sp", xt_[:], Xv[:, :, tk0:tk0 + 128], reads=[TX], writes=[Txt])
                        xo, Txo = xor.next()
                        Txo.new_version()
                        for kc in range(KC):
                            pt_, Tpt_ = (pA, TpA) if kc < 4 else (pB_, TpB)
                            stt(xo[:, kc, :], pt_[:, (kc % 4) * 128:(kc % 4 + 1) * 128], gt2(j, kc), xt_[:, kc, :], ALU.mult, ALU.add, [Tpt_, Tp, Txt], (), aw=[Txo])
                        dma("sp", Xv[:, :, tk0:tk0 + 128], xo[:], reads=[Txo], awrites=[TX])
                S.barrier()
        else:
            dma("sp", out_d[:, :], X_d[:, TC:TT], reads=[TX])
        S.finish()
        S.emit()
    return nc


def _host_consts():
    c = np.zeros((NCONST, 128, 128), np.float32)
    c[C_ID] = np.eye(128, dtype=np.float32)
    c[C_ONES] = 1.0
    c[C_BD64, 0:64, 0:64] = 1.0
    c[C_BD64, 64:128, 64:128] = 1.0
    for p in range(128):
        dd = p % 64
        partner = p + 16 if (dd % 32) < 16 else p - 16
        c[C_PERM, partner, p] = 1.0
    ones = np.ones((128, 128), np.float32)
    c[C_UI] = np.triu(ones)
    c[C_LI] = np.tril(ones)
    c[C_SL] = np.tril(ones, -1)
    c[C_SU] = np.triu(ones, 1)
    c[C_IOTA0] = np.arange(128, dtype=np.float32)[None, :]
    c[C_IOTA1] = np.arange(128, dtype=np.float32)[None, :] + 128.0
    ii = np.arange(128)
    for k in range(7):
        s_ = 1 << k
        c[C_OFF + k] = ((ii[:, None] // (2 * s_) == ii[None, :] // (2 * s_)) & (ii[:, None] // s_ != ii[None, :] // s_)).astype(np.float32)
    return np.ascontiguousarray(c.transpose(1, 0, 2).reshape(128, NCONST * 128))


def _rope_tables(TL):
    t = np.arange(TL)
    row = (t // GRID_W).astype(np.float32)
    col = (t % GRID_W).astype(np.float32)
    nf = A_DIM // 4
    inv = (np.float32(ROPE_THETA) ** (-np.arange(nf, dtype=np.float32) / np.float32(nf))).astype(np.float32)
    out = np.zeros((2, 128, TL), np.float32)
    for p in range(128):
        dd = p % 64
        axis, half, f = dd // 32, (dd % 32) // 16, dd % 16
        ang = ((row if axis == 0 else col) * inv[f]).astype(np.float32)
        out[0, p] = np.cos(ang)
        out[1, p] = -np.sin(ang) if half == 0 else np.sin(ang)
    return out


def _chunkT(v, n):
    return np.ascontiguousarray(np.asarray(v, np.float32).reshape(n, 128).T)


def prepare_inputs(inp, cfg):
    L = cfg.L
    f = lambda a: np.asarray(a, np.float32)
    spar = np.zeros((L, 128, NSP), np.float32)
    rvec = np.zeros((L, NRV), np.float32)
    lruw = np.zeros((L, 2, 2, KC, 128, 128), np.float32)
    for l in range(L):
        s = spar[l]
        s[:, SP["b_ada"]:SP["b_ada"] + 48] = _chunkT(inp["b_ada"][l], 48)
        s[:, SP["n1g"]:SP["n1g"] + 8] = _chunkT(inp["norm1_g"][l], 8)
        s[:, SP["n2g"]:SP["n2g"] + 8] = _chunkT(inp["norm2_g"][l], 8)
        s[:, SP["gq"]] = np.tile(f(inp["attn_qn_g"][l]), 2)
        s[:, SP["gk"]] = np.tile(f(inp["attn_kn_g"][l]), 2)
        s[:, SP["subg"]] = f(inp["attn_sub_g"][l])
        for i, k in enumerate(("lam_q1", "lam_k1", "lam_q2", "lam_k2")):
            s[0:64, SP["lam"] + i] = f(inp[k][l])
        s[:, SP["gconv"]:SP["gconv"] + 120] = f(inp["gdn_conv_w"][l]).reshape(5, 24, 128).transpose(2, 1, 0).reshape(128, 120)
        s[:, SP["lconv"]:SP["lconv"] + 40] = f(inp["lru_conv_w"][l]).reshape(5, 8, 128).transpose(2, 1, 0).reshape(128, 40)
        s[:, SP["lconvb"]:SP["lconvb"] + 8] = _chunkT(inp["lru_conv_b"][l], 8)
        for nm, key in (("lbr", "lru_b_r"), ("lbi", "lru_b_i"), ("llam", "lru_lambda")):
            s[:, SP[nm]:SP[nm] + 16] = f(inp[key][l]).reshape(2, 8, 128).transpose(2, 0, 1).reshape(128, 16)
        rvec[l, 0:16] = f(inp["gdn_a_log"][l]).reshape(16)
        rvec[l, 16:32] = f(inp["gdn_dt_bias"][l]).reshape(16)
        rvec[l, 32:160] = f(inp["gdn_norm_g"][l])
        for d in range(2):
            for ri, key in enumerate(("lru_w_r", "lru_w_i")):
                w = f(inp[key][l, d])
                for c in range(KC):
                    lruw[l, d, ri, c, 0:64, 0:64] = w[2 * c]
                    lruw[l, d, ri, c, 64:128, 64:128] = w[2 * c + 1]
    skT = np.ascontiguousarray(f(inp["peer_subkeys"])[:L].reshape(L, 16, 128, 128).transpose(0, 1, 3, 2))
    shared = {
        "consts": _host_consts(), "rope": _rope_tables(cfg.TL), "spar": spar, "rvec": rvec,
        "w_ada": np.ascontiguousarray(f(inp["w_ada"])[:L]), "w_in": np.ascontiguousarray(f(inp["w_in"])[:L]),
        "w_branch": np.ascontiguousarray(f(inp["w_branch"])[:L]), "w_out": np.ascontiguousarray(f(inp["w_out"])[:L]),
        "lruw": lruw, "peer_wq": np.ascontiguousarray(f(inp["peer_wq"])[:L]), "skT": skT,
        "peer_u": np.ascontiguousarray(f(inp["peer_u"])[:L]), "peer_v": np.ascontiguousarray(f(inp["peer_v"])[:L]),
    }
    x, ctx, c, c_ctx = f(inp["x"]), f(inp["ctx"]), f(inp["c"]), f(inp["c_ctx"])
    maps = []
    for b in range(x.shape[0]):
        m = dict(shared)
        m["xT"] = np.ascontiguousarray(np.concatenate([ctx[b].T, x[b].T], axis=1))
        cond = np.stack([c[b].reshape(KC, 128).T, c_ctx.reshape(KC, 128).T], axis=2)
        m["cond"] = np.ascontiguousarray(cond.reshape(128, KC * 2))
        maps.append(m)
    return maps


_NC_CACHE = {}


def kernel(**inputs):
    x = np.asarray(inputs["x"])
    B, TL, _ = x.shape
    TC = np.asarray(inputs["ctx"]).shape[1]
    L = np.asarray(inputs["w_in"]).shape[0]
    cfg = Cfg(TC=TC, TL=TL, L=L)
    key = (TC, TL, L)
    if key not in _NC_CACHE:
        _NC_CACHE[key] = build(cfg)
    nc = _NC_CACHE[key]
    maps = prepare_inputs(inputs, cfg)
    res = run_bass_kernel_spmd(nc, maps, core_ids=list(range(B)))
    out = np.stack([np.ascontiguousarray(res.results[b]["outT"].T) for b in range(B)], axis=0)
    return out.astype(np.float32)
```

```python
import math
import os
from contextlib import ExitStack

import numpy as np
import concourse.bass as bass
import concourse.mybir as mybir
from concourse.bass_utils import run_bass_kernel_spmd

F32 = mybir.dt.float32
BF16 = mybir.dt.bfloat16
U32 = mybir.dt.uint32
AF = mybir.ActivationFunctionType
ALU = mybir.AluOpType
AX = mybir.AxisListType

D = 1024
KC = 8
N_MOD = 6
EPS = 1e-6
A_HEADS = 8
A_DIM = 64
ROPE_THETA = 10000.0
GRID_W = 64
B_HEADS = 8
B_DIM = 128
SHORT_CONV = 5
C_WIDTH = 1024
LRU_C = 8.0
P_HEADS = 8
P_KEYS = 128
P_TOPK = 16
A_QKV = 3072
B_QKV = 3072
OFF_AQ, OFF_AK, OFF_AV = 0, 1024, 2048
OFF_BQKV = 3072
OFF_Z = 6144
OFF_BETA = 7168
OFF_ALPHA = 7184
OFF_LX = 7200
OFF_LG = 8224
OFF_MG = 9248
IN_WIDTH = 12320

SP = {}
_o = 0
for _n, _w in (("b_ada", 48), ("n1g", 8), ("n2g", 8), ("gq", 1), ("gk", 1), ("subg", 1), ("lam", 4),
               ("gconv", 120), ("lconv", 40), ("lconvb", 8), ("lbr", 16), ("lbi", 16), ("llam", 16)):
    SP[_n] = _o
    _o += _w
NSP = _o
NRV = 160
C_ID, C_ONES, C_BD64, C_PERM, C_UI, C_LI, C_SL, C_SU, C_IOTA0, C_IOTA1 = range(10)
C_OFF = 10
NCONST = 17

EPOCH = 60000
DMA_ND = 8
DMA_GEN = 3500


class T:
    __slots__ = ("w", "r", "pw", "excl")

    def __init__(self, excl=False):
        self.w = {}
        self.r = {}
        self.pw = {}
        self.excl = excl

    def new_version(self):
        pw = dict(self.w)
        _merge(pw, self.r.values())
        _merge(pw, self.pw.values())
        self.pw = pw
        self.w = {}
        self.r = {}


def _merge(d, toks):
    for tok in toks:
        k = id(tok[0])
        if k not in d or d[k][1] < tok[1]:
            d[k] = tok


class Sched:
    ENGS = ("pe", "act", "dve", "pool", "sp")

    def __init__(self, nc, es):
        self.nc = nc
        self.es = es
        self.q = {e: [] for e in self.ENGS}
        self.cnt = {e: 0 for e in self.ENGS}
        self.sems = {e: [] for e in self.ENGS}
        self.seen = {e: {} for e in self.ENGS}
        self.dcnt = {e: 0 for e in self.ENGS}
        self.dsems = {e: [] for e in self.ENGS}
        self.nsem = 0

    def _newsem(self, name):
        self.nsem += 1
        return self.es.enter_context(self.nc.semaphore(f"{name}{self.nsem}"))

    def _wait(self, eng, deps, is_pe_op):
        for (sem, val, src) in deps:
            if is_pe_op and src == "pe":
                continue
            k = id(sem)
            if self.seen[eng].get(k, 0) >= val:
                continue
            self.seen[eng][k] = val
            self.q[eng].append(lambda e, s=sem, v=val: e.wait_ge(s, v))

    def _deps(self, reads, writes, awrites):
        deps = {}
        for t in reads:
            _merge(deps, t.w.values())
            _merge(deps, t.pw.values())
        for t in writes:
            _merge(deps, t.w.values())
            _merge(deps, t.r.values())
            _merge(deps, t.pw.values())
        for t in awrites:
            _merge(deps, t.pw.values())
        return deps

    def _record(self, tok, reads, writes, awrites):
        for t in reads:
            _merge(t.r, [tok])
        for t in writes:
            t.w = {id(tok[0]): tok}
            t.r = {}
            t.pw = {}
        for t in awrites:
            _merge(t.w, [tok])

    @staticmethod
    def _split(reads, writes):
        ex = [t for t in reads if t.excl]
        if not ex:
            return reads, writes
        return [t for t in reads if not t.excl], list(writes) + ex

    def op(self, eng, fn, reads=(), writes=(), awrites=()):
        reads, writes = self._split(reads, writes)
        deps = self._deps(reads, writes, awrites)
        self._wait(eng, deps.values(), eng == "pe")
        i = self.cnt[eng]
        self.cnt[eng] += 1
        ep = i // EPOCH
        while len(self.sems[eng]) <= ep:
            self.sems[eng].append(self._newsem(eng))
        sem = self.sems[eng][ep]
        self.q[eng].append(lambda e, f=fn, s=sem: f(e).then_inc(s, 1))
        tok = (sem, i % EPOCH + 1, eng)
        self._record(tok, reads, writes, awrites)
        return tok

    def dma(self, eng, fn, reads=(), writes=(), awrites=()):
        reads, writes = self._split(reads, writes)
        deps = self._deps(reads, writes, awrites)
        j = self.dcnt[eng]
        self.dcnt[eng] += 1
        gen, within = divmod(j, DMA_ND * DMA_GEN)
        slot = within % DMA_ND
        use = within // DMA_ND
        while len(self.dsems[eng]) <= gen:
            self.dsems[eng].append([self._newsem("d" + eng) for _ in range(DMA_ND)])
        sem = self.dsems[eng][gen][slot]
        if use > 0:
            _merge(deps, [(sem, 16 * use, "dma")])
        self._wait(eng, deps.values(), False)
        self.q[eng].append(lambda e, f=fn, s=sem: f(e).then_inc(s, 16))
        tok = (sem, 16 * (use + 1), "dma")
        self._record(tok, reads, writes, awrites)
        return tok

    def _all_tokens(self):
        toks = []
        for e in self.ENGS:
            if self.cnt[e] > 0:
                i = self.cnt[e] - 1
                toks.append((self.sems[e][i // EPOCH], i % EPOCH + 1, e))
            for gi, gen in enumerate(self.dsems[e]):
                n_in = min(max(self.dcnt[e] - gi * DMA_ND * DMA_GEN, 0), DMA_ND * DMA_GEN)
                for slot, sem in enumerate(gen):
                    uses = (n_in - slot + DMA_ND - 1) // DMA_ND if n_in > slot else 0
                    if uses > 0:
                        toks.append((sem, 16 * uses, "dma"))
        return toks

    def barrier(self):
        toks = self._all_tokens()
        for e in self.ENGS:
            self._wait(e, toks, False)

    def finish(self):
        self._wait("sp", self._all_tokens(), False)

    def emit(self):
        with self.nc.Block() as block:
            @block.sync
            def _(e):
                for f in self.q["sp"]:
                    f(e)

            @block.tensor
            def _(e):
                for f in self.q["pe"]:
                    f(e)

            @block.scalar
            def _(e):
                for f in self.q["act"]:
                    f(e)

            @block.vector
            def _(e):
                for f in self.q["dve"]:
                    f(e)

            @block.gpsimd
            def _(e):
                for f in self.q["pool"]:
                    f(e)


class Ring:
    def __init__(self, tiles):
        self.tiles = [(t, T()) for t in tiles]
        self.i = 0

    def next(self):
        r = self.tiles[self.i % len(self.tiles)]
        self.i += 1
        return r


def interleave(gens):
    gens = list(gens)
    while gens:
        nxt = []
        for g in gens:
            try:
                next(g)
                nxt.append(g)
            except StopIteration:
                pass
        gens = nxt


class Cfg:
    def __init__(self, TC=256, TL=4096, L=4, debug=False, stop=None, stop_layer=0):
        self.TC, self.TL, self.L, self.debug, self.stop, self.stop_layer = TC, TL, L, debug, stop, stop_layer


def build(cfg):
    TC, TL, L = cfg.TC, cfg.TL, cfg.L
    TT = TC + TL
    NCH = TT // 128
    NCC = TC // 128
    TP = TT + 6
    tiles = [(c0, min(512, TC - c0)) for c0 in range(0, TC, 512)] + [(c0, min(512, TT - c0)) for c0 in range(TC, TT, 512)]

    def ppos(c0):
        return c0 + 2 if c0 < TC else c0 + 4

    nc = bass.Bass("TRN2", target_bir_lowering=False)
    kind_dbg = "ExternalOutput" if cfg.debug else "Internal"

    def dram(name, shape, dt=F32, kind="ExternalInput"):
        return nc.dram_tensor(name, list(shape), dt, kind=kind).ap()

    xT_d = dram("xT", [D, TT])
    cond_d = dram("cond", [128, KC * 2])
    consts_d = dram("consts", [128, NCONST * 128])
    rope_d = dram("rope", [2, 128, TL])
    sp_d = dram("spar", [L, 128, NSP])
    rv_d = dram("rvec", [L, NRV])
    w_ada_d = dram("w_ada", [L, D, N_MOD * D])
    w_in_d = dram("w_in", [L, D, IN_WIDTH])
    w_br_d = dram("w_branch", [L, 3, D, D])
    w_out_d = dram("w_out", [L, D, D])
    lruw_d = dram("lruw", [L, 2, 2, KC, 128, 128])
    wq_d = dram("peer_wq", [L, D, 2048])
    skT_d = dram("skT", [L, 16, 128, 128])
    pu_d = dram("peer_u", [L, P_KEYS * P_KEYS, D])
    pv_d = dram("peer_v", [L, P_KEYS * P_KEYS, D])
    out_d = dram("outT", [D, TL], kind="ExternalOutput")
    X_d = dram("X", [D, TT], kind=kind_dbg)
    QKT_d = dram("QKT", [2048, TT], BF16, kind=kind_dbg)
    VTOK_d = dram("VTOK", [TT, D], BF16, kind=kind_dbg)
    GQKV_d = dram("GQKV", [3072, TT], kind=kind_dbg)
    ZS_d = dram("ZS", [TT, D], kind=kind_dbg)
    SG_d = dram("SG", [3072, TT], BF16, kind=kind_dbg)
    YA_d = dram("YA", [D, TT], BF16, kind=kind_dbg)
    YB_d = dram("YB", [D, TT], BF16, kind=kind_dbg)
    YR_d = dram("YR", [D, TT], BF16, kind=kind_dbg)
    OD_d = dram("OD", [2, TT, D], kind=kind_dbg)
    H2_d = dram("H2", [TT, D], kind=kind_dbg)
    HT_d = dram("HTdbg", [D, TT], BF16, kind=kind_dbg) if cfg.debug else None
    DBGF_d = dram("DBGF", [128, TT], kind="ExternalOutput") if cfg.debug else None
    DBGI_d = dram("DBGI", [128, TT], U32, kind="ExternalOutput") if cfg.debug else None
    UV_d = dram("UV", [L * P_KEYS * P_KEYS, 2 * D], BF16, kind="Internal")
    TX, TQKT, TVTOK, TGQKV, TZS, TSG, TYA, TYB, TYR, TOD, TH2 = [T() for _ in range(11)]
    TUB = T()

    es = ExitStack()
    with es:
        S = Sched(nc, es)

        uniq = [0]

        def sbuf(stack, name, shape, dt=F32):
            uniq[0] += 1
            return stack.enter_context(nc.sbuf_tensor(f"{name}_{uniq[0]}", list(shape), dt))

        def ring(stack, name, n, shape, dt=F32):
            return Ring([sbuf(stack, f"{name}{i}", shape, dt) for i in range(n)])

        def act(out, in_, func, reads, writes, aw=(), **kw):
            return S.op("act", lambda e: e.activation(out=out, in_=in_, func=func, **kw), reads, writes, aw)

        def tt(eng, out, in0, in1, op, reads, writes, aw=()):
            return S.op(eng, lambda e: e.tensor_tensor(out=out, in0=in0, in1=in1, op=op), reads, writes, aw)

        def ts(eng, out, in0, s1, op0, reads, writes, s2=None, op1=None, aw=()):
            if op1 is None:
                return S.op(eng, lambda e: e.tensor_scalar(out=out, in0=in0, scalar1=s1, scalar2=None, op0=op0), reads, writes, aw)
            return S.op(eng, lambda e: e.tensor_scalar(out=out, in0=in0, scalar1=s1, scalar2=s2, op0=op0, op1=op1), reads, writes, aw)

        def stt(out, in0, scalar, in1, op0, op1, reads, writes, accum_out=None, aw=()):
            if accum_out is None:
                return S.op("dve", lambda e: e.scalar_tensor_tensor(out=out, in0=in0, scalar=scalar, in1=in1, op0=op0, op1=op1), reads, writes, aw)
            return S.op("dve", lambda e: e.scalar_tensor_tensor(out=out, in0=in0, scalar=scalar, in1=in1, op0=op0, op1=op1, accum_out=accum_out), reads, writes, aw)

        def cp(eng, out, in_, reads, writes, aw=()):
            if eng == "act":
                return act(out, in_, AF.Copy, reads, writes, aw)
            return S.op(eng, lambda e: e.tensor_copy(out=out, in_=in_), reads, writes, aw)

        def recip(out, in_, reads, writes):
            return S.op("dve", lambda e: e.reciprocal(out=out, in_=in_), reads, writes)

        def mm(out, lhsT, rhs, reads, writes, start=True, stop=True):
            return S.op("pe", lambda e: e.matmul(out, lhsT, rhs, start=start, stop=stop), reads, writes)

        def tr(out, in_, reads, writes):
            return S.op("pe", lambda e: e.transpose(out, in_, IDENT), list(reads) + [Tc], writes)

        def dma(q, out, in_, reads=(), writes=(), awrites=()):
            return S.dma(q, lambda e: e.dma_start(out=out, in_=in_), reads, writes, awrites)

        evc = [0]

        def evac(out, in_, reads, writes, aw=()):
            evc[0] += 1
            return cp("act" if evc[0] % 2 else "dve", out, in_, reads, writes, aw)

        PB = [(es.enter_context(nc.psum_tensor(f"pb{i}", [128, 512], F32)), T(excl=True)) for i in range(8)]
        psA = Ring([PB[i][0] for i in range(4)])
        psA.tiles = [PB[i] for i in range(4)]
        psB = Ring([PB[i][0] for i in range(4, 8)])
        psB.tiles = [PB[i] for i in range(4, 8)]
        psAll = Ring([PB[i][0] for i in range(8)])
        psAll.tiles = PB
        psQ = Ring([PB[0][0]])
        psQ.tiles = [(PB[k // 4][0][:, (k % 4) * 128:(k % 4 + 1) * 128], PB[k // 4][1]) for k in range(32)]

        consts = sbuf(es, "consts", [128, NCONST, 128])
        Tc = T()
        dma("sp", consts[:].rearrange("p c f -> p (c f)"), consts_d[:, :], writes=[Tc])
        IDENT, ONES, PERM = consts[:, C_ID, :], consts[:, C_ONES, :], consts[:, C_PERM, :]
        UI, LI, SL, SU = consts[:, C_UI, :], consts[:, C_LI, :], consts[:, C_SL, :], consts[:, C_SU, :]
        cbf = sbuf(es, "cbf", [128, 2, 128], BF16)
        cp("pool", cbf[:, 0, :], consts[:, C_ONES, :], [Tc], [Tc])
        cp("pool", cbf[:, 1, :], consts[:, C_BD64, :], [Tc], [Tc])
        ONES_BF, BD64_BF = cbf[:, 0, :], cbf[:, 1, :]
        cst = sbuf(es, "cst", [128, 4])
        S.op("pool", lambda e: e.memset(cst[:, 0:1], EPS), (), [Tc])
        S.op("pool", lambda e: e.memset(cst[:, 1:2], 1.0), (), [Tc])
        S.op("pool", lambda e: e.memset(cst[:, 2:3], 0.0), (), [Tc])
        EPS_AP, ONE_AP, ZERO_AP = cst[:, 0:1], cst[:, 1:2], cst[:, 2:3]
        scond = sbuf(es, "scond", [128, KC, 2])
        dma("sp", scond[:].rearrange("p k j -> p (k j)"), cond_d[:, :], writes=[Tc])
        act(scond[:], scond[:], AF.Silu, [Tc], [Tc])

        dma("sp", X_d[:, :], xT_d[:, :], awrites=[TX])
        with ExitStack() as ces:
            cfr = ring(ces, "cvf", 2, [128, 8, D])
            cbr = ring(ces, "cvb", 2, [128, 8, D], BF16)
            ci = 0
            dv_ = UV_d.rearrange("(r p j) (t d) -> r p j t d", p=128, j=8, t=2)
            for t_, src_ in enumerate((pu_d, pv_d)):
                sv_ = src_.rearrange("l (r p j) d -> (l r) p j d", p=128, j=8)
                for r_ in range(L * 16):
                    f_, Tf_ = cfr.next()
                    dma("sp", f_[:], sv_[r_], writes=[Tf_])
                    b_, Tb__ = cbr.next()
                    cp(("act", "pool", "dve")[ci % 3], b_[:], f_[:], [Tf_], [Tb__])
                    ci += 1
                    dma("sp", dv_[r_][:, :, t_, :], b_[:], reads=[Tb__], awrites=[TUB])
        S.barrier()

        Xv = X_d.rearrange("(kc p) t -> p kc t", p=128)

        for l in range(L):
            need_ctx = l < L - 1
            lam_init = 0.8 - 0.6 * math.exp(-0.3 * l)
            les = ExitStack()
            with les:
                Tp = T()
                spl = sbuf(les, "spl", [128, NSP])
                dma("sp", spl[:], sp_d[l], writes=[Tp])
                rvl = sbuf(les, "rvl", [128, NRV])
                dma("sp", rvl[:], rv_d[l:l + 1, :].to_broadcast([128, NRV]), writes=[Tp])
                mod = sbuf(les, "mod", [128, 2, 48])
                with ExitStack() as pes:
                    wring = ring(pes, "wada", 2, [128, KC, 512])
                    wav = w_ada_d[l].rearrange("(kc p) f -> p kc f", p=128)
                    pm, Tpm = PB[0]
                    for fg in range(12):
                        wt, Tw = wring.next()
                        dma("sp", wt[:], wav[:, :, fg * 512:(fg + 1) * 512], writes=[Tw])
                        for j in range(4):
                            fc = fg * 4 + j
                            for kc in range(KC):
                                mm(pm[:, 2 * fc:2 * fc + 2], wt[:, kc, j * 128:(j + 1) * 128], scond[:, kc, :], [Tw, Tc], [Tpm],
                                   start=(kc == 0), stop=(kc == KC - 1))
                    pmv = pm[:, 0:96].rearrange("p (f j) -> p j f", j=2)
                    for j in range(2):
                        tt("dve", mod[:, j, :], pmv[:, j, :], spl[:, SP["b_ada"]:SP["b_ada"] + 48], ALU.add, [Tpm, Tp], [Tp])
                S.barrier()
                gs1 = sbuf(les, "gs1", [128, 2, KC])
                gs2 = sbuf(les, "gs2", [128, 2, KC])
                for j in range(2):
                    stt(gs1[:, j, :], mod[:, j, 8:16], 1.0, spl[:, SP["n1g"]:SP["n1g"] + 8], ALU.add, ALU.mult, [Tp], [Tp])
                    stt(gs2[:, j, :], mod[:, j, 32:40], 1.0, spl[:, SP["n2g"]:SP["n2g"] + 8], ALU.add, ALU.mult, [Tp], [Tp])
                sh1 = lambda j, kc: mod[:, j, 0 + kc:1 + kc]
                gt1 = lambda j, kc: mod[:, j, 16 + kc:17 + kc]
                sh2 = lambda j, kc: mod[:, j, 24 + kc:25 + kc]
                gt2 = lambda j, kc: mod[:, j, 40 + kc:41 + kc]
                lamt = sbuf(les, "lamt", [128, 8])
                tt("dve", lamt[0:64, 0:1], spl[0:64, SP["lam"]:SP["lam"] + 1], spl[0:64, SP["lam"] + 1:SP["lam"] + 2], ALU.mult, [Tp], [Tp])
                tt("dve", lamt[0:64, 1:2], spl[0:64, SP["lam"] + 2:SP["lam"] + 3], spl[0:64, SP["lam"] + 3:SP["lam"] + 4], ALU.mult, [Tp], [Tp])
                pl_, Tpl = PB[1]
                mm(pl_[:, 0:2], consts[0:64, C_ONES, :], lamt[0:64, 0:2], [Tp, Tc], [Tpl])
                act(lamt[:, 2:4], pl_[:, 0:2], AF.Exp, [Tpl], [Tp])
                tt("dve", lamt[:, 4:5], lamt[:, 3:4], lamt[:, 2:3], ALU.subtract, [Tp], [Tp])
                ts("dve", lamt[:, 5:6], lamt[:, 4:5], -lam_init, ALU.add, [Tp], [Tp])
                NEGLAM = lamt[:, 5:6]
                ts("dve", lamt[:, 6:7], spl[:, SP["gq"]:SP["gq"] + 1], 0.125, ALU.mult, [Tp], [Tp])
                ts("dve", lamt[:, 7:8], spl[:, SP["subg"]:SP["subg"] + 1], 1.0 - lam_init, ALU.mult, [Tp], [Tp])
                GQ, GK, SUBG = lamt[:, 6:7], spl[:, SP["gk"]:SP["gk"] + 1], lamt[:, 7:8]

                BETA = sbuf(les, "BETA", [128, NCH, 16])
                NEGB = sbuf(les, "NEGB", [128, NCH, 16])
                GG = sbuf(les, "GG", [128, NCH, 16])
                EG = sbuf(les, "EG", [128, NCH, 3, 16])
                BGC = sbuf(les, "BGC", [128, NCH, 16])
                TG = T()

                def norm_mod(pes, tag, gs, shf, out_fn, rings):
                    xr, sqr, rtr, tmr = rings
                    for (c0, n) in tiles:
                        j = 1 if c0 < TC else 0
                        xt_, Tx_ = xr.next()
                        dma("sp", xt_[:, :, :n], Xv[:, :, c0:c0 + n], reads=[TX], writes=[Tx_])
                        sq_, Tsq = sqr.next()
                        act(sq_[:, :, :n], xt_[:, :, :n], AF.Square, [Tx_], [Tsq])
                        pn, Tpn = psB.next()
                        for kc in range(KC):
                            mm(pn[:, :n], ONES_BF, sq_[:, kc, :n], [Tsq, Tc], [Tpn], start=(kc == 0), stop=(kc == KC - 1))
                        rt_, Trt = rtr.next()
                        act(rt_[:, :n], pn[:, :n], AF.Sqrt, [Tpn, Tc], [Trt], scale=1.0 / D, bias=EPS_AP)
                        recip(rt_[:, :n], rt_[:, :n], [Trt], [Trt])
                        tm_, Ttm = tmr.next()
                        tt("dve", tm_[:, :, :n], xt_[:, :, :n], rt_[:, :n].unsqueeze(1).to_broadcast([128, KC, n]), ALU.mult, [Tx_, Trt], [Ttm])
                        out_fn(c0, n, j, tm_, Ttm, gs, shf)

                pes = ExitStack()
                with pes:
                    hT = sbuf(pes, "hT", [128, KC, TT], BF16)
                    ThT = T()
                    with ExitStack() as nes:
                        rings = (ring(nes, "nx", 2, [128, KC, 512]), ring(nes, "nsq", 2, [128, KC, 512], BF16),
                                 ring(nes, "nrt", 2, [128, 512]), ring(nes, "ntm", 2, [128, KC, 512]))

                        def out_h(c0, n, j, tm_, Ttm, gs, shf):
                            for kc in range(KC):
                                act(hT[:, kc, c0:c0 + n], tm_[:, kc, :n], AF.Identity, [Ttm, Tp], (), aw=[ThT],
                                    scale=gs[:, j, kc:kc + 1], bias=shf(j, kc))
                        norm_mod(nes, "n1", gs1, sh1, out_h, rings)
                    S.barrier()
                    if cfg.debug:
                        dma("sp", HT_d.rearrange("(kc p) t -> p kc t", p=128), hT[:], reads=[ThT])
                    if cfg.stop == "norm1" and l == cfg.stop_layer:
                        break

                    wiv = w_in_d[l].rearrange("(kc p) f -> p kc f", p=128)
                    wst = ring(pes, "wst", 2, [128, KC, 128])
                    wbf = ring(pes, "wbf", 2, [128, KC, 128], BF16)

                    def load_w(col0, ncols):
                        ws, Tws = wst.next()
                        dma("sp", ws[:, :, :ncols], wiv[:, :, col0:col0 + ncols], writes=[Tws])
                        wb, Twb = wbf.next()
                        cp("pool", wb[:, :, :ncols], ws[:, :, :ncols], [Tws], [Twb])
                        return wb, Twb

                    def proj_feat(wb, Twb, ncols, c0, n, pring=psA):
                        pp, Tpp = pring.next()
                        for kc in range(KC):
                            mm(pp[:ncols, :n], wb[:, kc, :ncols], hT[:, kc, c0:c0 + n], [Twb, ThT], [Tpp], start=(kc == 0), stop=(kc == KC - 1))
                        return pp, Tpp

                    TQKT.new_version()
                    with ExitStack() as aes:
                        qfr = ring(aes, "qf", 2, [128, 512])
                        sqr = ring(aes, "asq", 2, [128, 512], BF16)
                        rtr = ring(aes, "art", 2, [128, 512])
                        qnr = ring(aes, "qn", 2, [128, 512])
                        csr = ring(aes, "cs", 2, [128, 2, 512])
                        t1r = ring(aes, "t1", 2, [128, 512])
                        t2r = ring(aes, "t2", 2, [128, 512])
                        obr = ring(aes, "qob", 3, [128, 512], BF16)
                        CUT = int(os.environ.get("K_CUT", "99"))
                        for ch in range(16):
                            wb, Twb = load_w(ch * 128, 128)
                            gsc = GQ if ch < 8 else GK
                            for (c0, n) in tiles:
                                if CUT < 2:
                                    continue
                                pp, Tpp = proj_feat(wb, Twb, 128, c0, n)
                                if CUT < 3:
                                    continue
                                qf, Tqf = qfr.next()
                                cp("dve", qf[:, :n], pp[:, :n], [Tpp], [Tqf])
                                sq, Tsq = sqr.next()
                                act(sq[:, :n], qf[:, :n], AF.Square, [Tqf], [Tsq])
                                if CUT < 4:
                                    continue
                                pb, Tpb = psB.next()
                                mm(pb[:, :n], BD64_BF, sq[:, :n], [Tsq, Tc], [Tpb])
                                rt, Trt = rtr.next()
                                act(rt[:, :n], pb[:, :n], AF.Sqrt, [Tpb, Tc], [Trt], scale=1.0 / A_DIM, bias=EPS_AP)
                                recip(rt[:, :n], rt[:, :n], [Trt], [Trt])
                                if CUT < 5:
                                    continue
                                qn, Tqn = qnr.next()
                                stt(qn[:, :n], qf[:, :n], gsc, rt[:, :n], ALU.mult, ALU.mult, [Tqf, Trt, Tp], [Tqn])
                                if CUT < 6:
                                    continue
                                ob, Tob = obr.next()
                                if c0 >= TC and not os.environ.get('K_NOROPE'):
                                    cs, Tcs = csr.next()
                                    dma("sp", cs[:, :, :n], rope_d[:, :, c0 - TC:c0 - TC + n].rearrange("c p t -> p c t"), writes=[Tcs])
                                    pr, Tpr = psB.next()
                                    mm(pr[:, :n], PERM, qn[:, :n], [Tqn, Tc], [Tpr])
                                    t1, Tt1 = t1r.next()
                                    tt("dve", t1[:, :n], qn[:, :n], cs[:, 0, :n], ALU.mult, [Tqn, Tcs], [Tt1])
                                    t2, Tt2 = t2r.next()
                                    tt("dve", t2[:, :n], pr[:, :n], cs[:, 1, :n], ALU.mult, [Tpr, Tcs], [Tt2])
                                    tt("pool", ob[:, :n], t1[:, :n], t2[:, :n], ALU.add, [Tt1, Tt2], [Tob])
                                else:
                                    cp("act", ob[:, :n], qn[:, :n], [Tqn], [Tob])
                                dma("sp", QKT_d[ch * 128:(ch + 1) * 128, c0:c0 + n], ob[:, :n], reads=[Tob], awrites=[TQKT])
                    S.barrier()
                    if cfg.stop == "attnprep" and l == cfg.stop_layer:
                        break

                    tes = ExitStack()
                    wst5 = ring(tes, "wst5", 1, [128, KC, 512])
                    wbf5 = ring(tes, "wbf5", 2, [128, KC, 512], BF16)

                    def load_w5(col0, ncols):
                        ws, Tws = wst5.next()
                        dma("sp", ws[:, :, :ncols], wiv[:, :, col0:col0 + ncols], writes=[Tws])
                        wb, Twb = wbf5.next()
                        cp("pool", wb[:, :, :ncols], ws[:, :, :ncols], [Tws], [Twb])
                        return wb, Twb

                    def proj_tok(wb, Twb, ncols, tk):
                        pp, Tpp = psA.next()
                        for kc in range(KC):
                            mm(pp[:, :ncols], hT[:, kc, tk * 128:(tk + 1) * 128], wb[:, kc, :ncols], [Twb, ThT], [Tpp], start=(kc == 0), stop=(kc == KC - 1))
                        return pp, Tpp

                    TVTOK.new_version()
                    with ExitStack() as ves:
                        vob = ring(ves, "vob", 3, [128, 512], BF16)
                        for jb in range(2):
                            wb, Twb = load_w5(OFF_AV + jb * 512, 512)
                            for tk in range(NCH):
                                pp, Tpp = proj_tok(wb, Twb, 512, tk)
                                ob, Tob = vob.next()
                                evac(ob[:], pp[:], [Tpp], [Tob])
                                dma("sp", VTOK_d[tk * 128:(tk + 1) * 128, jb * 512:(jb + 1) * 512], ob[:], reads=[Tob], awrites=[TVTOK])
                    S.barrier()
                    TZS.new_version()
                    with ExitStack() as ves:
                        zob = ring(ves, "zob", 3, [128, 512])
                        for jb in range(2):
                            wb, Twb = load_w5(OFF_Z + jb * 512, 512)
                            for tk in range(NCH):
                                pp, Tpp = proj_tok(wb, Twb, 512, tk)
                                ob, Tob = zob.next()
                                act(ob[:], pp[:], AF.Silu, [Tpp], [Tob])
                                dma("sp", ZS_d[tk * 128:(tk + 1) * 128, jb * 512:(jb + 1) * 512], ob[:], reads=[Tob], awrites=[TZS])
                    S.barrier()
                    with ExitStack() as ves:
                        negea = sbuf(ves, "negea", [128, 16])
                        Tne = T()
                        act(negea[:], rvl[:, 0:16], AF.Exp, [Tp], [Tne])
                        ts("dve", negea[:], negea[:], -1.0, ALU.mult, [Tne], [Tne])
                        xar = ring(ves, "xa", 2, [128, 16])
                        wb, Twb = load_w5(OFF_BETA, 32)
                        for tk in range(NCH):
                            pp, Tpp = proj_tok(wb, Twb, 32, tk)
                            act(BETA[:, tk, :], pp[:, 0:16], AF.Sigmoid, [Tpp], (), aw=[TG])
                            xa, Txa = xar.next()
                            tt("dve", xa[:], pp[:, 16:32], rvl[:, 16:32], ALU.add, [Tpp, Tp], [Txa])
                            act(xa[:], xa[:], AF.Exp, [Txa], [Txa])
                            act(xa[:], xa[:], AF.Ln, [Txa, Tc], [Txa], bias=ONE_AP)
                            tt("dve", GG[:, tk, :], xa[:], negea[:], ALU.mult, [Txa, Tne], (), aw=[TG])
                            ts("dve", NEGB[:, tk, :], BETA[:, tk, :], -1.0, ALU.mult, [TG], (), aw=[TG])
                            pg, Tpg = psB.next()
                            pgv = pg[:, 0:48].rearrange("p (a b) -> p a b", a=3)
                            for d in range(2):
                                A_, R_ = (UI, SL) if d == 0 else (LI, SU)
                                gsl = GG[:, tk, d * 8:(d + 1) * 8]
                                mm(pgv[:, 0, d * 8:(d + 1) * 8], A_, gsl, [TG, Tc], [Tpg])
                                mm(pgv[:, 1, d * 8:(d + 1) * 8], R_, gsl, [TG, Tc], [Tpg])
                                mm(pgv[:, 2, d * 8:(d + 1) * 8], ONES, gsl, [TG, Tc], [Tpg])
                            act(EG[:, tk, :, :], pgv, AF.Exp, [Tpg], (), aw=[TG])
                            tt("dve", BGC[:, tk, :], BETA[:, tk, :], EG[:, tk, 0, :], ALU.mult, [TG], (), aw=[TG])
                    if cfg.debug and cfg.stop in ("prep1", "gdn", "gdnprep") and l == cfg.stop_layer:
                        o_ = 0
                        for arr, w_ in ((GG, NCH * 16), (BETA, NCH * 16), (BGC, NCH * 16), (NEGB, NCH * 16), (EG, NCH * 48)):
                            dma("sp", DBGF_d[:, o_:o_ + w_], arr[:].rearrange("p a b -> p (a b)") if arr is not EG else arr[:].rearrange("p a b c -> p (a b c)"), reads=[TG])
                            o_ += w_
                    tes.close()
                    S.barrier()
                    TSG.new_version()
                    with ExitStack() as ves:
                        sgo = ring(ves, "sgo", 3, [128, 512], BF16)
                        for ch in range(24):
                            wb, Twb = load_w(OFF_MG + ch * 128, 128)
                            for (c0, n) in tiles:
                                pp, Tpp = proj_feat(wb, Twb, 128, c0, n)
                                ob, Tob = sgo.next()
                                act(ob[:, :n], pp[:, :n], AF.Sigmoid, [Tpp], [Tob])
                                dma("sp", SG_d[ch * 128:(ch + 1) * 128, c0:c0 + n], ob[:, :n], reads=[Tob], awrites=[TSG])
                    S.barrier()
                    if cfg.stop == "prep1" and l == cfg.stop_layer:
                        break
                    raw = sbuf(pes, "raw", [128, TP])
                    acc = sbuf(pes, "acc", [128, TP])
                    Traw, Tacc = T(), T()
                    S.op("pool", lambda e, o=raw[:]: e.memset(o, 0.0), (), [Traw])
                    S.op("pool", lambda e, o=acc[:]: e.memset(o, 0.0), (), [Tacc])

                    def fill_raw(col0):
                        wb, Twb = load_w(col0, 128)
                        Traw.new_version()
                        for (c0, n) in tiles:
                            pp, Tpp = proj_feat(wb, Twb, 128, c0, n)
                            evac(raw[:, ppos(c0):ppos(c0) + n], pp[:, :n], [Tpp], (), aw=[Traw])

                    def conv(wcol, bias_ap):
                        a_out = acc[:, 2:TP - 2]
                        if bias_ap is None:
                            act(a_out, raw[:, 0:TP - 4], AF.Identity, [Traw, Tp], [Tacc], scale=spl[:, wcol:wcol + 1], bias=ZERO_AP)
                        else:
                            act(a_out, raw[:, 0:TP - 4], AF.Identity, [Traw, Tp], [Tacc], scale=spl[:, wcol:wcol + 1], bias=bias_ap)
                        for j in range(1, 5):
                            stt(a_out, raw[:, j:TP - 4 + j], spl[:, wcol + j:wcol + j + 1], a_out, ALU.mult, ALU.add, [Traw, Tp, Tacc], [Tacc])

                    TGQKV.new_version()
                    with ExitStack() as ves:
                        sqr = ring(ves, "gsq", 2, [128, 512], BF16)
                        rtr = ring(ves, "grt", 2, [128, 512])
                        gor = ring(ves, "gout", 3, [128, 512])
                        for ch in range(24):
                            typ = ch // 8
                            fill_raw(OFF_BQKV + ch * 128)
                            conv(SP["gconv"] + ch * 5, None)
                            act(acc[:, 2:TP - 2], acc[:, 2:TP - 2], AF.Silu, [Tacc], [Tacc])
                            for (c0, n) in tiles:
                                src = acc[:, ppos(c0):ppos(c0) + n]
                                if typ == 2:
                                    dma("sp", GQKV_d[ch * 128:(ch + 1) * 128, c0:c0 + n], src, reads=[Tacc], awrites=[TGQKV])
                                    continue
                                sq, Tsq = sqr.next()
                                act(sq[:, :n], src, AF.Square, [Tacc], [Tsq])
                                pb, Tpb = psB.next()
                                mm(pb[:, :n], ONES_BF, sq[:, :n], [Tsq, Tc], [Tpb])
                                rt, Trt = rtr.next()
                                act(rt[:, :n], pb[:, :n], AF.Sqrt, [Tpb, Tc], [Trt], scale=1.0, bias=EPS_AP)
                                recip(rt[:, :n], rt[:, :n], [Trt], [Trt])
                                go, Tgo = gor.next()
                                stt(go[:, :n], src, (B_DIM ** -0.5) if typ == 0 else 1.0, rt[:, :n], ALU.mult, ALU.mult, [Tacc, Trt], [Tgo])
                                dma("sp", GQKV_d[ch * 128:(ch + 1) * 128, c0:c0 + n], go[:, :n], reads=[Tgo], awrites=[TGQKV])
                    S.barrier()
                    if cfg.stop == "gdnprep" and l == cfg.stop_layer:
                        break
                    TYR.new_version()
                    with ExitStack() as ves:
                        hsum = sbuf(ves, "hsum", [128, TT])
                        Ths = [T() for _ in tiles]
                        cn = sbuf(ves, "cneg", [128, 3, 16])
                        Tcn = T()
                        act(cn[:, 0, :], spl[:, SP["llam"]:SP["llam"] + 16], AF.Exp, [Tp], [Tcn], scale=-1.0)
                        act(cn[:, 0, :], cn[:, 0, :], AF.Ln, [Tcn, Tc], [Tcn], bias=ONE_AP)
                        ts("dve", cn[:, 1, :], cn[:, 0, :], -2.0 * LRU_C, ALU.mult, [Tcn], [Tcn])
                        ts("dve", cn[:, 2, :], cn[:, 0, :], LRU_C, ALU.mult, [Tcn], [Tcn])
                        ts("dve", cn[:, 0, :], cn[:, 0, :], -LRU_C, ALU.mult, [Tcn], [Tcn])
                        lwr = ring(ves, "lw", 2, [128, 2, 2, 128])
                        rr = ring(ves, "lr", 2, [128, 512])
                        ir = ring(ves, "li", 2, [128, 512])
                        ar = ring(ves, "la", 2, [128, 512])
                        a2r = ring(ves, "la2", 2, [128, 512])
                        thr = ring(ves, "lth", 2, [128, 512])
                        hbr = ring(ves, "lhb", 2, [128, 512])
                        glr = ring(ves, "lgl", 2, [128, 512])
                        yor = ring(ves, "lyo", 3, [128, 512], BF16)
                        ctiles = [t_ for t_ in enumerate(tiles) if t_[1][0] < TC]
                        ltiles = [t_ for t_ in enumerate(tiles) if t_[1][0] >= TC]
                        for c in range(KC):
                            fill_raw(OFF_LX + c * 128)
                            conv(SP["lconv"] + c * 5, spl[:, SP["lconvb"] + c:SP["lconvb"] + c + 1])
                            lw, Tlw = lwr.next()
                            for d in range(2):
                                for ri in range(2):
                                    dma("sp", lw[:, d, ri, :], lruw_d[l, d, ri, c], writes=[Tlw] if (d == 0 and ri == 0) else (), awrites=() if (d == 0 and ri == 0) else [Tlw])
                            for d in range(2):
                                order = (ctiles + ltiles) if d == 0 else (ctiles[::-1] + ltiles[::-1])
                                carry, Tcar = None, None
                                col = d * 8 + c
                                for (ti, (c0, n)) in order:
                                    xs = acc[:, ppos(c0):ppos(c0) + n]
                                    pr, Tpr = psA.next()
                                    mm(pr[:, :n], lw[:, d, 0, :], xs, [Tlw, Tacc], [Tpr])
                                    pi, Tpi = psA.next()
                                    mm(pi[:, :n], lw[:, d, 1, :], xs, [Tlw, Tacc], [Tpi])
                                    r_, Tr = rr.next()
                                    act(r_[:, :n], pr[:, :n], AF.Sigmoid, [Tpr, Tp], [Tr], bias=spl[:, SP["lbr"] + col:SP["lbr"] + col + 1])
                                    i_, Ti = ir.next()
                                    act(i_[:, :n], pi[:, :n], AF.Sigmoid, [Tpi, Tp], [Ti], bias=spl[:, SP["lbi"] + col:SP["lbi"] + col + 1])
                                    a_, Ta = ar.next()
                                    act(a_[:, :n], r_[:, :n], AF.Exp, [Tr, Tcn], [Ta], scale=cn[:, 0, col:col + 1])
                                    a2, Ta2 = a2r.next()
                                    act(a2[:, :n], r_[:, :n], AF.Exp, [Tr, Tcn], [Ta2], scale=cn[:, 1, col:col + 1])
                                    th, Tth = thr.next()
                                    act(th[:, :n], r_[:, :n], AF.Tanh, [Tr, Tcn], [Tth], scale=cn[:, 2, col:col + 1])
                                    stt(a2[:, :n], a2[:, :n], 1.0, th[:, :n], ALU.add, ALU.mult, [Ta2, Tth], [Ta2])
                                    act(a2[:, :n], a2[:, :n], AF.Sqrt, [Ta2], [Ta2])
                                    tt("dve", i_[:, :n], i_[:, :n], a2[:, :n], ALU.mult, [Ti, Ta2], [Ti])
                                    tt("pool", i_[:, :n], i_[:, :n], xs, ALU.mult, [Ti, Tacc], [Ti])
                                    init = 0.0 if carry is None else carry
                                    rds = [Ta, Ti] + ([Tcar] if Tcar is not None else [])
                                    if d == 0:
                                        S.op("dve", lambda e, o=hsum[:, c0:c0 + n], d0=a_[:, :n], d1=i_[:, :n], ini=init:
                                             e.tensor_tensor_scan(out=o, data0=d0, data1=d1, initial=ini, op0=ALU.mult, op1=ALU.add), rds, [Ths[ti]])
                                        carry, Tcar = hsum[:, c0 + n - 1:c0 + n], Ths[ti]
                                    else:
                                        hb, Thb = hbr.next()
                                        S.op("dve", lambda e, o=hb[:, :n][:, ::-1], d0=a_[:, :n][:, ::-1], d1=i_[:, :n][:, ::-1], ini=init:
                                             e.tensor_tensor_scan(out=o, data0=d0, data1=d1, initial=ini, op0=ALU.mult, op1=ALU.add), rds, [Thb])
                                        carry, Tcar = hb[:, 0:1], Thb
                                        tt("pool", hsum[:, c0:c0 + n], hsum[:, c0:c0 + n], hb[:, :n], ALU.add, [Ths[ti], Thb], [Ths[ti]])
                            wb, Twb = load_w(OFF_LG + c * 128, 128)
                            for ti, (c0, n) in enumerate(tiles):
                                pp, Tpp = proj_feat(wb, Twb, 128, c0, n)
                                gl, Tgl = glr.next()
                                act(gl[:, :n], pp[:, :n], AF.Gelu_apprx_tanh, [Tpp], [Tgl])
                                yo, Tyo = yor.next()
                                tt("dve", yo[:, :n], gl[:, :n], hsum[:, c0:c0 + n], ALU.mult, [Tgl, Ths[ti]], [Tyo])
                                dma("sp", YR_d[c * 128:(c + 1) * 128, c0:c0 + n], yo[:, :n], reads=[Tyo], awrites=[TYR])
                S.barrier()
                if cfg.stop == "prep" and l == cfg.stop_layer:
                    break
                TYA.new_version()
                with ExitStack() as aes:
                    kTr = ring(aes, "kTh", 2, [128, TT], BF16)
                    Vr = ring(aes, "Vh", 2, [128, NCH, 128], BF16)
                    qbr = ring(aes, "qb", 2, [128, 512], BF16)
                    ptr = ring(aes, "pt", 4, [128, 512], BF16)
                    r0r = ring(aes, "ar0", 2, [128, 512])
                    t0r = ring(aes, "at0", 2, [128, 512])
                    t1r = ring(aes, "at1", 2, [128, 512])
                    sqr = ring(aes, "asq2", 2, [128, 512], BF16)
                    yar = ring(aes, "aya", 2, [128, 512], BF16)
                    VTv = VTOK_d.rearrange("(n p) e -> p n e", p=128)
                    for h in range(A_HEADS):
                        kT, TkT = kTr.next()
                        dma("sp", kT[:], QKT_d[1024 + h * 128:1024 + (h + 1) * 128, :], reads=[TQKT], writes=[TkT])
                        V, TV = Vr.next()
                        TV.new_version()
                        for k0_ in range(0, NCH, 8):
                            k1_ = min(NCH, k0_ + 8)
                            dma("sp", V[:, k0_:k1_, :], VTv[:, k0_:k1_, h * 128:(h + 1) * 128], reads=[TVTOK], awrites=[TV])
                        blocks = ([(0, TC, 0, NCC)] if need_ctx else []) + [(c0, n, 0, NCH) for (c0, n) in tiles if c0 >= TC]
                        for (q0, nq, k0, k1) in blocks:
                            qb, Tqb = qbr.next()
                            dma("sp", qb[:, :nq], QKT_d[h * 128:(h + 1) * 128, q0:q0 + nq], reads=[TQKT], writes=[Tqb])
                            (O0, TO0), (Z0, TZ0), (O1, TO1), (Z1, TZ1) = PB[0], PB[1], PB[2], PB[3]
                            OZ = ((O0, TO0, Z0, TZ0), (O1, TO1, Z1, TZ1))
                            pend = []

                            def consume(item):
                                st, Tst, kt, c = item
                                pt, Tpt = ptr.next()
                                act(pt[:, :nq], st[:, :nq], AF.Exp, [Tst], [Tpt])
                                O_, TO_, Z_, TZ_ = OZ[c]
                                mm(O_[:, :nq], V[:, kt, :], pt[:, :nq], [TV, Tpt], [TO_], start=(kt == k0), stop=(kt == k1 - 1))
                                mm(Z_[:, :nq], ONES_BF, pt[:, :nq], [Tpt, Tc], [TZ_], start=(kt == k0), stop=(kt == k1 - 1))

                            for kt in range(k0, k1):
                                for c in range(2):
                                    st, Tst = psB.next()
                                    mm(st[:, :nq], kT[c * 64:(c + 1) * 64, kt * 128:(kt + 1) * 128], qb[c * 64:(c + 1) * 64, :nq], [TkT, Tqb], [Tst])
                                    pend.append((st, Tst, kt, c))
                                    if len(pend) > 2:
                                        consume(pend.pop(0))
                            while pend:
                                consume(pend.pop(0))
                            r0, Tr0 = r0r.next()
                            recip(r0[:, :nq], Z0[:, :nq], [TZ0], [Tr0])
                            t0, Tt0 = t0r.next()
                            tt("dve", t0[:, :nq], O0[:, :nq], r0[:, :nq], ALU.mult, [TO0, Tr0], [Tt0])
                            r1, Tr1 = r0r.next()
                            recip(r1[:, :nq], Z1[:, :nq], [TZ1], [Tr1])
                            t1, Tt1 = t1r.next()
                            tt("dve", t1[:, :nq], O1[:, :nq], r1[:, :nq], ALU.mult, [TO1, Tr1], [Tt1])
                            stt(t0[:, :nq], t1[:, :nq], NEGLAM, t0[:, :nq], ALU.mult, ALU.add, [Tt1, Tt0, Tp], [Tt0])
                            sq, Tsq = sqr.next()
                            act(sq[:, :nq], t0[:, :nq], AF.Square, [Tt0], [Tsq])
                            pss, Tpss = psB.next()
                            mm(pss[:, :nq], ONES_BF, sq[:, :nq], [Tsq, Tc], [Tpss])
                            act(r1[:, :nq], pss[:, :nq], AF.Sqrt, [Tpss, Tc], [Tr1], scale=1.0 / 128.0, bias=EPS_AP)
                            recip(r1[:, :nq], r1[:, :nq], [Tr1], [Tr1])
                            ya, Tya = yar.next()
                            stt(ya[:, :nq], t0[:, :nq], SUBG, r1[:, :nq], ALU.mult, ALU.mult, [Tt0, Tr1, Tp], [Tya])
                            dma("sp", YA_d[h * 128:(h + 1) * 128, q0:q0 + nq], ya[:, :nq], reads=[Tya], awrites=[TYA])
                S.barrier()
                if cfg.stop == "attn" and l == cfg.stop_layer:
                    break

                TOD.new_version()
                for d in range(2):
                    with ExitStack() as ges:
                        A_, B_, Ms, MiT = (UI, SL, SL, UI) if d == 0 else (LI, SU, SU, LI)
                        order = list(range(NCH)) if d == 0 else (list(range(NCC - 1, -1, -1)) + list(range(NCH - 1, NCC - 1, -1)))

                        def chain(h):
                            col = d * 8 + h
                            nm = f"g{d}{h}"
                            mk = lambda n_: (sbuf(ges, nm + n_, [128, 128]), T())
                            (St, TS), (qT, TqT), (kT, TkT), (vT, TvT) = mk("S"), mk("q"), mk("k"), mk("v")
                            (g2, Tg2), (e_, Te), (et_, Tet), (Ru, TRu), (Rw, TRw), (kd, Tkd) = mk("g2"), mk("e"), mk("et"), mk("Ru"), mk("Rw"), mk("kd")
                            (N0, TN0), (NT0, TNT0), (N1, TN1), (NT1, TNT1) = mk("N0"), mk("NT0"), mk("N1"), mk("NT1")
                            (P0, TP0), (P1, TP1), (qkT, Tqk) = mk("P0"), mk("P1"), mk("qk")
                            (D2, TD2), (E2, TE2), (Ysb, TY), (Zsb, TZ) = mk("D2"), mk("E2"), mk("Ysb"), mk("Zsb")
                            (u_, Tu), (wT, TwT), (vn, Tvn), (o1s, To1), (oo, Too) = mk("u"), mk("wT"), mk("vn"), mk("o1"), mk("oo")
                            S.op("pool", lambda e, o=St[:]: e.memset(o, 0.0), (), [TS])
                            Tb_ = PB[h][1]
                            Q = [PB[h][0][:, i_ * 128:(i_ + 1) * 128] for i_ in range(4)]
                            yield
                            for n in order:
                                c0 = n * 128
                                dma("sp", qT[:], GQKV_d[h * 128:(h + 1) * 128, c0:c0 + 128], reads=[TGQKV], writes=[TqT])
                                dma("sp", kT[:], GQKV_d[1024 + h * 128:1024 + (h + 1) * 128, c0:c0 + 128], reads=[TGQKV], writes=[TkT])
                                dma("sp", vT[:], GQKV_d[2048 + h * 128:2048 + (h + 1) * 128, c0:c0 + 128], reads=[TGQKV], writes=[TvT])
                                ts("pool", g2[:], B_, GG[:, n, col:col + 1], ALU.mult, [TG, Tc], [Tg2])
                                yield
                                pk, pv, pd, pdt = Q
                                tr(pk, kT[:], [TkT], [Tb_])
                                tr(pv, vT[:], [TvT], [Tb_])
                                mm(pd, A_, g2[:], [Tg2, Tc], [Tb_])
                                mm(pdt, g2[:], A_, [Tg2, Tc], [Tb_])
                                yield
                                act(e_[:], pd, AF.Exp, [Tb_], [Te])
                                act(et_[:], pdt, AF.Exp, [Tb_], [Tet])
                                act(kd[:], pk, AF.Identity, [Tb_, TG, Tc], [Tkd], scale=EG[:, n, 1, col:col + 1], bias=ZERO_AP)
                                yield
                                ts("dve", Ru[:], pv, BETA[:, n, col:col + 1], ALU.mult, [Tb_, TG], [TRu])
                                ts("dve", Rw[:], pk, BGC[:, n, col:col + 1], ALU.mult, [Tb_, TG], [TRw])
                                tt("pool", e_[:], e_[:], Ms, ALU.mult, [Te, Tc], [Te])
                                tt("pool", et_[:], et_[:], MiT, ALU.mult, [Tet, Tc], [Tet])
                                yield
                                pkk, pkq = Q[0], Q[1]
                                mm(pkk, kT[:], kT[:], [TkT], [Tb_])
                                mm(pkq, kT[:], qT[:], [TkT, TqT], [Tb_])
                                yield
                                stt(N0[:], pkk, NEGB[:, n, col:col + 1], e_[:], ALU.mult, ALU.mult, [Tb_, TG, Te], [TN0])
                                tt("dve", qkT[:], pkq, et_[:], ALU.mult, [Tb_, Tet], [Tqk])
                                yield
                                pn = Q[2]
                                tr(pn, N0[:], [TN0], [Tb_])
                                yield
                                cp("act", NT0[:], pn, [Tb_], [TNT0])
                                yield
                                OFFK = [consts[:, C_OFF + k_, :] for k_ in range(7)]
                                tt("pool", N1[:], N0[:], OFFK[0], ALU.mult, [TN0, Tc], [TN1])
                                tt("pool", NT1[:], NT0[:], OFFK[0], ALU.mult, [TNT0, Tc], [TNT1])
                                yield
                                tt("pool", P0[:], N1[:], IDENT, ALU.add, [TN1, Tc], [TP0])
                                tt("pool", P1[:], NT1[:], IDENT, ALU.add, [TNT1, Tc], [TP1])
                                Dc, Ec = (P0, TP0), (P1, TP1)
                                Dn, En = (D2, TD2), (E2, TE2)
                                for k in range(1, 7):
                                    last = (k == 6)
                                    tt("pool", N1[:], N0[:], OFFK[k], ALU.mult, [TN0, Tc], [TN1])
                                    if not last:
                                        tt("pool", NT1[:], NT0[:], OFFK[k], ALU.mult, [TNT0, Tc], [TNT1])
                                    yield
                                    mm(Q[0], N1[:], Ec[0][:], [TN1, Ec[1]], [Tb_])
                                    if not last:
                                        mm(Q[1], NT1[:], Dc[0][:], [TNT1, Dc[1]], [Tb_])
                                    yield
                                    cp("act", Ysb[:], Q[0], [Tb_], [TY])
                                    if not last:
                                        cp("dve", Zsb[:], Q[1], [Tb_], [TZ])
                                    yield
                                    mm(Q[2], Dc[0][:], Ysb[:], [Dc[1], TY], [Tb_])
                                    if not last:
                                        mm(Q[3], Ec[0][:], Zsb[:], [Ec[1], TZ], [Tb_])
                                    yield
                                    tt("dve", En[0][:], Q[2], Ec[0][:], ALU.add, [Tb_, Ec[1]], [En[1]])
                                    if not last:
                                        tt("dve", Dn[0][:], Q[3], Dc[0][:], ALU.add, [Tb_, Dc[1]], [Dn[1]])
                                    Dc, Dn = Dn, Dc
                                    Ec, En = En, Ec
                                    yield
                                Pc = Ec
                                PI, TPI = Pc
                                pu, pw_ = Q[0], Q[1]
                                mm(pu, PI[:], Ru[:], [TPI, TRu], [Tb_])
                                mm(pw_, Rw[:], PI[:], [TPI, TRw], [Tb_])
                                yield
                                cp("act", u_[:], pu, [Tb_], [Tu])
                                cp("act", wT[:], pw_, [Tb_], [TwT])
                                yield
                                pa, po = Q[2], Q[3]
                                mm(pa, wT[:], St[:], [TwT, TS], [Tb_])
                                mm(po, qT[:], St[:], [TqT, TS], [Tb_])
                                yield
                                tt("dve", vn[:], u_[:], pa, ALU.subtract, [Tu, Tb_], [Tvn])
                                ts("dve", o1s[:], po, EG[:, n, 0, col:col + 1], ALU.mult, [Tb_, TG], [To1])
                                yield
                                po2, ps_ = Q[0], Q[1]
                                mm(po2, qkT[:], vn[:], [Tqk, Tvn], [Tb_])
                                mm(ps_, kd[:], vn[:], [Tkd, Tvn], [Tb_])
                                yield
                                if cfg.debug and os.environ.get("K_GDBG") and d == 0 and h == 0 and n == order[0]:
                                    for i_, (arr_, T_) in enumerate(((e_, Te), (N0, TN0), (PI, TPI), (u_, Tu), (qkT, Tqk))):
                                        dma("sp", DBGF_d[:, i_ * 128:(i_ + 1) * 128], arr_[:], reads=[T_])
                                tt("dve", oo[:], po2, o1s[:], ALU.add, [Tb_, To1], [Too])
                                stt(St[:], St[:], EG[:, n, 2, col:col + 1], ps_, ALU.mult, ALU.add, [TS, TG, Tb_], [TS])
                                dma("sp", OD_d[d, c0:c0 + 128, h * 128:(h + 1) * 128], oo[:], reads=[Too], awrites=[TOD])
                                yield

                        interleave([chain(h) for h in range(B_HEADS)])
                    S.barrier()
                if cfg.stop == "gdn" and l == cfg.stop_layer:
                    break

                TYB.new_version()
                with ExitStack() as fes:
                    o0r = ring(fes, "fo0", 2, [128, D])
                    o1r = ring(fes, "fo1", 2, [128, D])
                    zsr = ring(fes, "fzs", 2, [128, D])
                    jkr = ring(fes, "fjk", 2, [128, 128])
                    ssr = ring(fes, "fss", 2, [128, 8])
                    ybr = ring(fes, "fyb", 3, [128, KC, 128], BF16)
                    for tk in range(NCH):
                        o0, To0 = o0r.next()
                        dma("sp", o0[:], OD_d[0, tk * 128:(tk + 1) * 128, :], reads=[TOD], writes=[To0])
                        o1, To1_ = o1r.next()
                        dma("sp", o1[:], OD_d[1, tk * 128:(tk + 1) * 128, :], reads=[TOD], writes=[To1_])
                        zs, Tzs = zsr.next()
                        dma("sp", zs[:], ZS_d[tk * 128:(tk + 1) * 128, :], reads=[TZS], writes=[Tzs])
                        tt("pool", o0[:], o0[:], o1[:], ALU.add, [To0, To1_], [To0])
                        ss, Tss = ssr.next()
                        Tss.new_version()
                        for h in range(B_HEADS):
                            jk, Tjk = jkr.next()
                            act(jk[:], o0[:, h * 128:(h + 1) * 128], AF.Square, [To0], [Tjk], aw=[Tss], accum_out=ss[:, h:h + 1])
                        act(ss[:], ss[:], AF.Sqrt, [Tss, Tc], [Tss], scale=1.0 / B_DIM, bias=EPS_AP)
                        recip(ss[:], ss[:], [Tss], [Tss])
                        o3 = o0[:].rearrange("p (h e) -> p h e", h=B_HEADS)
                        tt("dve", o3, o3, ss[:].unsqueeze(2).to_broadcast([128, B_HEADS, 128]), ALU.mult, [To0, Tss], [To0])
                        tt("dve", o3, o3, rvl[:, 32:160].unsqueeze(1).to_broadcast([128, B_HEADS, 128]), ALU.mult, [To0, Tp], [To0])
                        tt("pool", o0[:], o0[:], zs[:], ALU.mult, [To0, Tzs], [To0])
                        yb, Tyb = ybr.next()
                        Tyb.new_version()
                        for h in range(B_HEADS):
                            pq, Tpq = psQ.next()
                            tr(pq, o0[:, h * 128:(h + 1) * 128], [To0], [Tpq])
                            evac(yb[:, h, :], pq, [Tpq], (), aw=[Tyb])
                        dma("sp", YB_d.rearrange("(h p) t -> p h t", p=128)[:, :, tk * 128:(tk + 1) * 128], yb[:], reads=[Tyb], awrites=[TYB])
                S.barrier()
                if cfg.stop == "gdnfin" and l == cfg.stop_layer:
                    break

                TX.new_version()
                mtiles = []
                for (c0, n) in tiles:
                    for s0 in range(0, n, 256):
                        mtiles.append((c0 + s0, min(256, n - s0)))
                with ExitStack() as mes:
                    wbr = sbuf(mes, "wbr", [128, 3, KC, D], BF16)
                    wou = sbuf(mes, "wou", [128, KC, D], BF16)
                    Twm = T()
                    wstg = ring(mes, "wstg", 2, [128, KC, 256])
                    srcs = [(w_br_d[l, k].rearrange("(kc p) f -> p kc f", p=128), wbr[:, k]) for k in range(3)] + [(w_out_d[l].rearrange("(kc p) f -> p kc f", p=128), wou[:])]
                    for (sv, dv) in srcs:
                        for jb in range(4):
                            ws, Tws = wstg.next()
                            dma("sp", ws[:], sv[:, :, jb * 256:(jb + 1) * 256], writes=[Tws])
                            cp("pool", dv[:, :, jb * 256:(jb + 1) * 256], ws[:], [Tws], (), aw=[Twm])
                    ysr = [ring(mes, f"ys{k}", 2, [128, KC, 256], BF16) for k in range(3)]
                    sgr = ring(mes, "msg", 2, [128, KC, 256], BF16)
                    yacr = ring(mes, "yacc", 1, [128, KC, 256])
                    tmr = ring(mes, "mtm", 2, [128, 256])
                    ybfr = ring(mes, "ybf", 2, [128, KC, 256], BF16)
                    xtr = ring(mes, "mxt", 2, [128, KC, 256])
                    xor = ring(mes, "mxo", 2, [128, KC, 256])
                    YS_d = (YA_d, YB_d, YR_d)
                    TYS = (TYA, TYB, TYR)
                    SGv = SG_d.rearrange("(k c p) t -> p k c t", p=128, k=3)
                    for (c0, n) in mtiles:
                        if c0 < TC and not need_ctx:
                            continue
                        j = 1 if c0 < TC else 0
                        xt_, Txt = xtr.next()
                        dma("sp", xt_[:, :, :n], Xv[:, :, c0:c0 + n], reads=[TX], writes=[Txt])
                        yac, Tyac = yacr.next()
                        for k in range(3):
                            y_, Ty_ = ysr[k].next()
                            dma("sp", y_[:, :, :n], YS_d[k].rearrange("(kc p) t -> p kc t", p=128)[:, :, c0:c0 + n], reads=[TYS[k]], writes=[Ty_])
                            sg, Tsg = sgr.next()
                            dma("sp", sg[:, :, :n], SGv[:, k, :, c0:c0 + n], reads=[TSG], writes=[Tsg])
                            for fo in range(KC):
                                pp, Tpp = psAll.next()
                                for kc in range(KC):
                                    mm(pp[:, :n], wbr[:, k, kc, fo * 128:(fo + 1) * 128], y_[:, kc, :n], [Twm, Ty_], [Tpp], start=(kc == 0), stop=(kc == KC - 1))
                                if k == 0:
                                    tt("dve", yac[:, fo, :n], pp[:, :n], sg[:, fo, :n], ALU.mult, [Tpp, Tsg], (), aw=[Tyac])
                                else:
                                    tm, Ttm = tmr.next()
                                    tt("dve", tm[:, :n], pp[:, :n], sg[:, fo, :n], ALU.mult, [Tpp, Tsg], [Ttm])
                                    tt("pool", yac[:, fo, :n], yac[:, fo, :n], tm[:, :n], ALU.add, [Ttm, Tyac], (), aw=[Tyac])
                        ybf, Tybf = ybfr.next()
                        cp("act", ybf[:, :, :n], yac[:, :, :n], [Tyac], [Tybf])
                        Tyac.new_version()
                        xo, Txo = xor.next()
                        Txo.new_version()
                        for fo in range(KC):
                            pp, Tpp = psAll.next()
                            for kc in range(KC):
                                mm(pp[:, :n], wou[:, kc, fo * 128:(fo + 1) * 128], ybf[:, kc, :n], [Twm, Tybf], [Tpp], start=(kc == 0), stop=(kc == KC - 1))
                            stt(xo[:, fo, :n], pp[:, :n], gt1(j, fo), xt_[:, fo, :n], ALU.mult, ALU.add, [Tpp, Tp, Txt], (), aw=[Txo])
                        dma("sp", Xv[:, :, c0:c0 + n], xo[:, :, :n], reads=[Txo], awrites=[TX])
                S.barrier()
                if cfg.stop == "merge" and l == cfg.stop_layer:
                    break
                H2F_T = T()
                TH2.new_version()
                ptok0 = 0 if need_ctx else TC
                ptiles = [(c0, n) for (c0, n) in mtiles if c0 >= ptok0]
                H2F_d = GQKV_d[0:D, :]
                H2Fv = H2F_d.rearrange("(kc p) t -> p kc t", p=128)
                with ExitStack() as nes:
                    rings = (ring(nes, "n2x", 2, [128, KC, 512]), ring(nes, "n2sq", 2, [128, KC, 512], BF16),
                             ring(nes, "n2rt", 2, [128, 512]), ring(nes, "n2tm", 1, [128, KC, 512]))
                    h2r = ring(nes, "h2o", 2, [128, KC, 512])
                    htr = ring(nes, "h2t", 2, [128, D])

                    def out_h2(c0, n, j, tm_, Ttm, gs, shf):
                        if c0 + n <= ptok0:
                            return
                        h2, Th2 = h2r.next()
                        Th2.new_version()
                        for kc in range(KC):
                            act(h2[:, kc, :n], tm_[:, kc, :n], AF.Identity, [Ttm, Tp], (), aw=[Th2], scale=gs[:, j, kc:kc + 1], bias=shf(j, kc))
                        dma("sp", H2Fv[:, :, c0:c0 + n], h2[:, :, :n], reads=[Th2], awrites=[H2F_T])
                        for s0 in range(0, n, 128):
                            ht, Tht = htr.next()
                            Tht.new_version()
                            for kc in range(KC):
                                pq, Tpq = psQ.next()
                                tr(pq, h2[:, kc, s0:s0 + 128], [Th2], [Tpq])
                                evac(ht[:, kc * 128:(kc + 1) * 128], pq, [Tpq], (), aw=[Tht])
                            dma("sp", H2_d[c0 + s0:c0 + s0 + 128, :], ht[:], reads=[Tht], awrites=[TH2])
                    norm_mod(nes, "n2", gs2, sh2, out_h2, rings)
                S.barrier()
                if cfg.stop == "peernorm" and l == cfg.stop_layer:
                    break
                IDXT = sbuf(les, "IDXT", [128, TT], U32)
                GATET = sbuf(les, "GATET", [128, TT])
                TIG = T()
                with ExitStack() as qes:
                    skt = sbuf(qes, "skt", [128, 16, 128])
                    Tsk = T()
                    for g0_ in range(0, 16, 4):
                        dma("sp", skt[:, g0_:g0_ + 4, :], skT_d[l, g0_:g0_ + 4].rearrange("g d k -> d g k"), awrites=[Tsk])
                    wqv = wq_d[l].rearrange("(kc p) f -> p kc f", p=128)
                    wqr = ring(qes, "wq", 2, [128, KC, 128])
                    h2r = ring(qes, "h2i", 2, [128, KC, 256])
                    QN = sbuf(qes, "QN", [128, 16, 256])
                    TQN = T()
                    qfr = ring(qes, "pqf", 2, [128, 256])
                    sqr = ring(qes, "psq", 2, [128, 256])
                    rtr = ring(qes, "prt", 2, [128, 256])
                    SCr = ring(qes, "SC", 2, [128, 16, 128])
                    m1r = ring(qes, "m1", 2, [128, 16, 16])
                    ixr = ring(qes, "ix", 2, [128, 16, 16], U32)
                    wkr = ring(qes, "wk", 2, [128, 256])
                    ixf = sbuf(qes, "ixf", [128, 16, 16])
                    cand = sbuf(qes, "cand", [128, 8, 256])
                    b1 = sbuf(qes, "b1", [128, 8, 16])
                    pos = sbuf(qes, "pos", [128, 8, 16], U32)
                    pab = sbuf(qes, "pab", [128, 2, 8, 16], U32)
                    pabf = sbuf(qes, "pabf", [128, 2, 8, 16])
                    oh = sbuf(qes, "oh", [128, 8, 16, 16])
                    isel = sbuf(qes, "isel", [128, 2, 8, 16])
                    idxf = sbuf(qes, "idxf", [128, 128])
                    gate = sbuf(qes, "gate", [128, 8, 16])
                    gsm = sbuf(qes, "gsm", [128, 8])
                    Tk_ = T()
                    IOTA16 = consts[:, C_IOTA0, 0:16]
                    for (c0, n) in ptiles:
                        h2, Th2 = h2r.next()
                        dma("sp", h2[:, :, :n], H2Fv[:, :, c0:c0 + n], reads=[H2F_T], writes=[Th2])
                        TQN.new_version()
                        for gi in range(16):
                            wq, Twq = wqr.next()
                            dma("sp", wq[:], wqv[:, :, gi * 128:(gi + 1) * 128], writes=[Twq])
                            pp, Tpp = psA.next()
                            for kc in range(KC):
                                mm(pp[:, :n], wq[:, kc, :], h2[:, kc, :n], [Twq, Th2], [Tpp], start=(kc == 0), stop=(kc == KC - 1))
                            qf, Tqf = qfr.next()
                            cp("dve", qf[:, :n], pp[:, :n], [Tpp], [Tqf])
                            sq, Tsq = sqr.next()
                            act(sq[:, :n], qf[:, :n], AF.Square, [Tqf], [Tsq])
                            pb, Tpb = psB.next()
                            mm(pb[:, :n], ONES, sq[:, :n], [Tsq, Tc], [Tpb])
                            rt, Trt = rtr.next()
                            act(rt[:, :n], pb[:, :n], AF.Sqrt, [Tpb, Tc], [Trt], scale=1.0 / 128.0, bias=EPS_AP)
                            recip(rt[:, :n], rt[:, :n], [Trt], [Trt])
                            tt("dve", QN[:, gi, :n], qf[:, :n], rt[:, :n], ALU.mult, [Tqf, Trt], (), aw=[TQN])
                        for s0 in range(0, n, 128):
                            tk0 = c0 + s0
                            SC, TSC = SCr.next()
                            TSC.new_version()
                            for gq in range(4):
                                pp, Tpp = psA.next()
                                for g4 in range(4):
                                    gi = gq * 4 + g4
                                    mm(pp[:, g4 * 128:(g4 + 1) * 128], QN[:, gi, s0:s0 + 128], skt[:, gi, :], [TQN, Tsk], [Tpp])
                                evac(SC[:, gq * 4:(gq + 1) * 4, :], pp[:].rearrange("p (a b) -> p a b", a=4), [Tpp], (), aw=[TSC])
                            m1, Tm1 = m1r.next()
                            ix, Tix = ixr.next()
                            for gi in range(16):
                                wk, Twk = wkr.next()
                                S.op("dve", lambda e, o=m1[:, gi, 0:8], i=SC[:, gi, :]: e.max(out=o, in_=i), [TSC], [Tm1])
                                S.op("dve", lambda e, o=wk[:, 0:128], r=m1[:, gi, 0:8], i=SC[:, gi, :]: e.match_replace(out=o, in_to_replace=r, in_values=i, imm_value=-1e30), [TSC, Tm1], [Twk])
                                S.op("dve", lambda e, o=m1[:, gi, 8:16], i=wk[:, 0:128]: e.max(out=o, in_=i), [Twk], [Tm1])
                                S.op("dve", lambda e, o=ix[:, gi, 0:8], r=m1[:, gi, 0:8], i=SC[:, gi, :]: e.max_index(out=o, in_max=r, in_values=i), [TSC, Tm1], [Tix])
                                S.op("dve", lambda e, o=ix[:, gi, 8:16], r=m1[:, gi, 8:16], i=SC[:, gi, :]: e.max_index(out=o, in_max=r, in_values=i), [TSC, Tm1], [Tix])
                            cp("dve", ixf[:], ix[:], [Tix], [Tk_])
                            m1v = m1[:].rearrange("p (h c) k -> p h c k", c=2)
                            ixv = ixf[:].rearrange("p (h c) k -> p h c k", c=2)
                            cand4 = cand[:].rearrange("p h (a b) -> p h a b", a=16)
                            tt("dve", cand4, m1v[:, :, 0, :].unsqueeze(3).to_broadcast([128, 8, 16, 16]),
                               m1v[:, :, 1, :].unsqueeze(2).to_broadcast([128, 8, 16, 16]), ALU.add, [Tm1], [Tk_])
                            for h in range(P_HEADS):
                                wk, Twk = wkr.next()
                                S.op("dve", lambda e, o=b1[:, h, 0:8], i=cand[:, h, :]: e.max(out=o, in_=i), [Tk_], [Tk_])
                                S.op("dve", lambda e, o=wk[:], r=b1[:, h, 0:8], i=cand[:, h, :]: e.match_replace(out=o, in_to_replace=r, in_values=i, imm_value=-1e30), [Tk_], [Twk])
                                S.op("dve", lambda e, o=b1[:, h, 8:16], i=wk[:]: e.max(out=o, in_=i), [Twk], [Tk_])
                                S.op("dve", lambda e, o=pos[:, h, 0:8], r=b1[:, h, 0:8], i=cand[:, h, :]: e.max_index(out=o, in_max=r, in_values=i), [Tk_], [Tk_])
                                S.op("dve", lambda e, o=pos[:, h, 8:16], r=b1[:, h, 8:16], i=cand[:, h, :]: e.max_index(out=o, in_max=r, in_values=i), [Tk_], [Tk_])
                            ts("dve", pab[:, 0], pos[:], 4, ALU.logical_shift_right, [Tk_], [Tk_])
                            ts("dve", pab[:, 1], pos[:], 15, ALU.bitwise_and, [Tk_], [Tk_])
                            cp("dve", pabf[:], pab[:], [Tk_], [Tk_])
                            for c in range(2):
                                tt("dve", oh[:], IOTA16.unsqueeze(1).unsqueeze(1).to_broadcast([128, 8, 16, 16]),
                                   pabf[:, c].unsqueeze(3).to_broadcast([128, 8, 16, 16]), ALU.is_equal, [Tk_, Tc], [Tk_])
                                tt("dve", oh[:], oh[:], ixv[:, :, c, :].unsqueeze(2).to_broadcast([128, 8, 16, 16]), ALU.mult, [Tk_], [Tk_])
                                S.op("dve", lambda e, o=isel[:, c], i=oh[:]: e.tensor_reduce(out=o, in_=i, axis=AX.X, op=ALU.add), [Tk_], [Tk_])
                            stt(idxf[:].rearrange("p (h k) -> p h k", h=8), isel[:, 0], 128.0, isel[:, 1], ALU.mult, ALU.add, [Tk_], [Tk_])
                            tt("dve", gate[:], b1[:], b1[:, :, 0:1].to_broadcast([128, 8, 16]), ALU.subtract, [Tk_], [Tk_])
                            act(gate[:], gate[:], AF.Exp, [Tk_], [Tk_])
                            S.op("dve", lambda e, o=gsm[:], i=gate[:]: e.tensor_reduce(out=o, in_=i, axis=AX.X, op=ALU.add), [Tk_], [Tk_])
                            recip(gsm[:], gsm[:], [Tk_], [Tk_])
                            tt("dve", gate[:], gate[:], gsm[:].unsqueeze(2).to_broadcast([128, 8, 16]), ALU.mult, [Tk_], [Tk_])
                            pq, Tpq = psQ.next()
                            tr(pq, idxf[:], [Tk_], [Tpq])
                            cp("dve", IDXT[:, tk0:tk0 + 128], pq, [Tpq], (), aw=[TIG])
                            pq2, Tpq2 = psQ.next()
                            tr(pq2, gate[:].rearrange("p h k -> p (h k)"), [Tk_], [Tpq2])
                            cp("act", GATET[:, tk0:tk0 + 128], pq2, [Tpq2], (), aw=[TIG])
                S.barrier()
                if cfg.stop == "peertopk" and l == cfg.stop_layer:
                    if cfg.debug:
                        dma("sp", DBGF_d[:, ptok0:TT], GATET[:, ptok0:TT], reads=[TIG])
                        dma("sp", DBGI_d[:, ptok0:TT], IDXT[:, ptok0:TT], reads=[TIG])
                    break
                TX.new_version()
                with ExitStack() as ges:
                    SEL = sbuf(ges, "SEL", [128, 128, 128], BF16)
                    Tsel = T()
                    cp("dve", SEL[:], IDENT.unsqueeze(2).to_broadcast([128, 128, 128]), [Tc], [Tsel])
                    uvr = ring(ges, "uvg", 6, [128, 2 * D], BF16)
                    h2fr = ring(ges, "h2f", 2, [128, D])
                    h2br = ring(ges, "h2b", 2, [128, D], BF16)
                    jkr2 = ring(ges, "pjunk", 2, [128, 512], BF16)
                    dotr = ring(ges, "dots", 2, [128, 128, 2])
                    ctr = ring(ges, "ct", 2, [128, 128])
                    ctbr = ring(ges, "ctb", 2, [128, 128], BF16)
                    xtr = ring(ges, "pxt", 2, [128, KC, 128])
                    xor = ring(ges, "pxo", 2, [128, KC, 128])
                    eoff = l * P_KEYS * P_KEYS * 2 * D
                    pbank = 4
                    hbank = 0
                    for tk0 in range(ptok0, TT, 128):
                        j = 1 if tk0 < TC else 0
                        h2f, Th2f = h2fr.next()
                        dma("sp", h2f[:], H2_d[tk0:tk0 + 128, :], reads=[TH2], writes=[Th2f])
                        h2b, Th2b = h2br.next()
                        cp("act", h2b[:], h2f[:], [Th2f], [Th2b])
                        dots, Tdots = dotr.next()
                        Tdots.new_version()
                        ct, Tct = ctr.next()
                        Tct.new_version()
                        ctb, Tctb = ctbr.next()
                        Tctb.new_version()
                        (pA, TpA), (pB_, TpB) = PB[pbank], PB[pbank + 1]
                        pbank = 4 + (pbank - 4 + 2) % 4
                        for i in range(128):
                            n = tk0 + i
                            uv, Tuv = uvr.next()
                            S.dma("pool", lambda e, o=uv[:], ix=IDXT[:, n:n + 1], eoff=eoff: e.indirect_dma_start(
                                out=o, out_offset=None, in_=UV_d, in_offset=bass.IndirectOffsetOnAxis(ap=ix, axis=0), element_offset=eoff), [TIG, TUB], [Tuv])
                            for hf in range(2):
                                hbk, Thbk = PB[hbank]
                                hbank = (hbank + 1) % 4
                                mm(hbk[:, :], SEL[:, i, :], h2b[:, hf * 512:(hf + 1) * 512], [Tsel, Th2b], [Thbk])
                                junk, Tjunk = jkr2.next()
                                stt(junk[:], uv[:, hf * 512:(hf + 1) * 512], 1.0, hbk[:, :], ALU.mult, ALU.mult, [Tuv, Thbk], [Tjunk],
                                    accum_out=dots[:, i, hf:hf + 1], aw=[Tdots])
                            act(ct[:, i:i + 1], dots[:, i, 0:1], AF.Gelu_apprx_tanh, [Tdots], (), aw=[Tct], bias=dots[:, i, 1:2])
                            tt("dve", ctb[:, i:i + 1], ct[:, i:i + 1], GATET[:, n:n + 1], ALU.mult, [Tct, TIG], (), aw=[Tctb])
                            for kc in range(KC):
                                pt_, Tpt_ = (pA, TpA) if kc < 4 else (pB_, TpB)
                                S.op("pe", lambda e, o=pt_[:, (kc % 4) * 128 + i:(kc % 4) * 128 + i + 1], w=uv[:, D + kc * 128:D + (kc + 1) * 128], r=ctb[:, i:i + 1]:
                                     e.matmul(o, w, r, start=True, stop=True), [Tuv, Tctb], [Tpt_])
                        xt_, Txt = xtr.next()
                        dma("sp", xt_[:], Xv[:, :, tk0:tk0 + 128], reads=[TX], writes=[Txt])
                        xo, Txo = xor.next()
                        Txo.new_version()
                        for kc in range(KC):
                            pt_, Tpt_ = (pA, TpA) if kc < 4 else (pB_, TpB)
                            stt(xo[:, kc, :], pt_[:, (kc % 4) * 128:(kc % 4 + 1) * 128], gt2(j, kc), xt_[:, kc, :], ALU.mult, ALU.add, [Tpt_, Tp, Txt], (), aw=[Txo])
                        dma("sp", Xv[:, :, tk0:tk0 + 128], xo[:], reads=[Txo], awrites=[TX])
                S.barrier()
        else:
            dma("sp", out_d[:, :], X_d[:, TC:TT], reads=[TX])
        S.finish()
        S.emit()
    return nc


def _host_consts():
    c = np.zeros((NCONST, 128, 128), np.float32)
    c[C_ID] = np.eye(128, dtype=np.float32)
    c[C_ONES] = 1.0
    c[C_BD64, 0:64, 0:64] = 1.0
    c[C_BD64, 64:128, 64:128] = 1.0
    for p in range(128):
        dd = p % 64
        partner = p + 16 if (dd % 32) < 16 else p - 16
        c[C_PERM, partner, p] = 1.0
    ones = np.ones((128, 128), np.float32)
    c[C_UI] = np.triu(ones)
    c[C_LI] = np.tril(ones)
    c[C_SL] = np.tril(ones, -1)
    c[C_SU] = np.triu(ones, 1)
    c[C_IOTA0] = np.arange(128, dtype=np.float32)[None, :]
    c[C_IOTA1] = np.arange(128, dtype=np.float32)[None, :] + 128.0
    ii = np.arange(128)
    for k in range(7):
        s_ = 1 << k
        c[C_OFF + k] = ((ii[:, None] // (2 * s_) == ii[None, :] // (2 * s_)) & (ii[:, None] // s_ != ii[None, :] // s_)).astype(np.float32)
    return np.ascontiguousarray(c.transpose(1, 0, 2).reshape(128, NCONST * 128))


def _rope_tables(TL):
    t = np.arange(TL)
    row = (t // GRID_W).astype(np.float32)
    col = (t % GRID_W).astype(np.float32)
    nf = A_DIM // 4
    inv = (np.float32(ROPE_THETA) ** (-np.arange(nf, dtype=np.float32) / np.float32(nf))).astype(np.float32)
    out = np.zeros((2, 128, TL), np.float32)
    for p in range(128):
        dd = p % 64
        axis, half, f = dd // 32, (dd % 32) // 16, dd % 16
        ang = ((row if axis == 0 else col) * inv[f]).astype(np.float32)
        out[0, p] = np.cos(ang)
        out[1, p] = -np.sin(ang) if half == 0 else np.sin(ang)
    return out


def _chunkT(v, n):
    return np.ascontiguousarray(np.asarray(v, np.float32).reshape(n, 128).T)


def prepare_inputs(inp, cfg):
    L = cfg.L
    f = lambda a: np.asarray(a, np.float32)
    spar = np.zeros((L, 128, NSP), np.float32)
    rvec = np.zeros((L, NRV), np.float32)
    lruw = np.zeros((L, 2, 2, KC, 128, 128), np.float32)
    for l in range(L):
        s = spar[l]
        s[:, SP["b_ada"]:SP["b_ada"] + 48] = _chunkT(inp["b_ada"][l], 48)
        s[:, SP["n1g"]:SP["n1g"] + 8] = _chunkT(inp["norm1_g"][l], 8)
        s[:, SP["n2g"]:SP["n2g"] + 8] = _chunkT(inp["norm2_g"][l], 8)
        s[:, SP["gq"]] = np.tile(f(inp["attn_qn_g"][l]), 2)
        s[:, SP["gk"]] = np.tile(f(inp["attn_kn_g"][l]), 2)
        s[:, SP["subg"]] = f(inp["attn_sub_g"][l])
        for i, k in enumerate(("lam_q1", "lam_k1", "lam_q2", "lam_k2")):
            s[0:64, SP["lam"] + i] = f(inp[k][l])
        s[:, SP["gconv"]:SP["gconv"] + 120] = f(inp["gdn_conv_w"][l]).reshape(5, 24, 128).transpose(2, 1, 0).reshape(128, 120)
        s[:, SP["lconv"]:SP["lconv"] + 40] = f(inp["lru_conv_w"][l]).reshape(5, 8, 128).transpose(2, 1, 0).reshape(128, 40)
        s[:, SP["lconvb"]:SP["lconvb"] + 8] = _chunkT(inp["lru_conv_b"][l], 8)
        for nm, key in (("lbr", "lru_b_r"), ("lbi", "lru_b_i"), ("llam", "lru_lambda")):
            s[:, SP[nm]:SP[nm] + 16] = f(inp[key][l]).reshape(2, 8, 128).transpose(2, 0, 1).reshape(128, 16)
        rvec[l, 0:16] = f(inp["gdn_a_log"][l]).reshape(16)
        rvec[l, 16:32] = f(inp["gdn_dt_bias"][l]).reshape(16)
        rvec[l, 32:160] = f(inp["gdn_norm_g"][l])
        for d in range(2):
            for ri, key in enumerate(("lru_w_r", "lru_w_i")):
                w = f(inp[key][l, d])
                for c in range(KC):
                    lruw[l, d, ri, c, 0:64, 0:64] = w[2 * c]
                    lruw[l, d, ri, c, 64:128, 64:128] = w[2 * c + 1]
    skT = np.ascontiguousarray(f(inp["peer_subkeys"])[:L].reshape(L, 16, 128, 128).transpose(0, 1, 3, 2))
    shared = {
        "consts": _host_consts(), "rope": _rope_tables(cfg.TL), "spar": spar, "rvec": rvec,
        "w_ada": np.ascontiguousarray(f(inp["w_ada"])[:L]), "w_in": np.ascontiguousarray(f(inp["w_in"])[:L]),
        "w_branch": np.ascontiguousarray(f(inp["w_branch"])[:L]), "w_out": np.ascontiguousarray(f(inp["w_out"])[:L]),
        "lruw": lruw, "peer_wq": np.ascontiguousarray(f(inp["peer_wq"])[:L]), "skT": skT,
        "peer_u": np.ascontiguousarray(f(inp["peer_u"])[:L]), "peer_v": np.ascontiguousarray(f(inp["peer_v"])[:L]),
    }
    x, ctx, c, c_ctx = f(inp["x"]), f(inp["ctx"]), f(inp["c"]), f(inp["c_ctx"])
    maps = []
    for b in range(x.shape[0]):
        m = dict(shared)
        m["xT"] = np.ascontiguousarray(np.concatenate([ctx[b].T, x[b].T], axis=1))
        cond = np.stack([c[b].reshape(KC, 128).T, c_ctx.reshape(KC, 128).T], axis=2)
        m["cond"] = np.ascontiguousarray(cond.reshape(128, KC * 2))
        maps.append(m)
    return maps


_NC_CACHE = {}


def kernel(**inputs):
    x = np.asarray(inputs["x"])
    B, TL, _ = x.shape
    TC = np.asarray(inputs["ctx"]).shape[1]
    L = np.asarray(inputs["w_in"]).shape[0]
    cfg = Cfg(TC=TC, TL=TL, L=L)
    key = (TC, TL, L)
    if key not in _NC_CACHE:
        _NC_CACHE[key] = build(cfg)
    nc = _NC_CACHE[key]
    maps = prepare_inputs(inputs, cfg)
    res = run_bass_kernel_spmd(nc, maps, core_ids=list(range(B)))
    out = np.stack([np.ascontiguousarray(res.results[b]["outT"].T) for b in range(B)], axis=0)
    return out.astype(np.float32)
```

```python
import math
import os
from contextlib import ExitStack

import numpy as np
import concourse.bass as bass
import concourse.mybir as mybir
from concourse.bass_utils import run_bass_kernel_spmd

F32 = mybir.dt.float32
BF16 = mybir.dt.bfloat16
U32 = mybir.dt.uint32
AF = mybir.ActivationFunctionType
ALU = mybir.AluOpType
AX = mybir.AxisListType

D = 1024
KC = 8
N_MOD = 6
EPS = 1e-6
A_HEADS = 8
A_DIM = 64
ROPE_THETA = 10000.0
GRID_W = 64
B_HEADS = 8
B_DIM = 128
SHORT_CONV = 5
C_WIDTH = 1024
LRU_C = 8.0
P_HEADS = 8
P_KEYS = 128
P_TOPK = 16
A_QKV = 3072
B_QKV = 3072
OFF_AQ, OFF_AK, OFF_AV = 0, 1024, 2048
OFF_BQKV = 3072
OFF_Z = 6144
OFF_BETA = 7168
OFF_ALPHA = 7184
OFF_LX = 7200
OFF_LG = 8224
OFF_MG = 9248
IN_WIDTH = 12320

SP = {}
_o = 0
for _n, _w in (("b_ada", 48), ("n1g", 8), ("n2g", 8), ("gq", 1), ("gk", 1), ("subg", 1), ("lam", 4),
               ("gconv", 120), ("lconv", 40), ("lconvb", 8), ("lbr", 16), ("lbi", 16), ("llam", 16)):
    SP[_n] = _o
    _o += _w
NSP = _o
NRV = 160
C_ID, C_ONES, C_BD64, C_PERM, C_UI, C_LI, C_SL, C_SU, C_IOTA0, C_IOTA1 = range(10)
C_OFF = 10
NCONST = 17

EPOCH = 60000
DMA_ND = 8
DMA_GEN = 3500


class T:
    __slots__ = ("w", "r", "pw", "excl")

    def __init__(self, excl=False):
        self.w = {}
        self.r = {}
        self.pw = {}
        self.excl = excl

    def new_version(self):
        pw = dict(self.w)
        _merge(pw, self.r.values())
        _merge(pw, self.pw.values())
        self.pw = pw
        self.w = {}
        self.r = {}


def _merge(d, toks):
    for tok in toks:
        k = id(tok[0])
        if k not in d or d[k][1] < tok[1]:
            d[k] = tok


class Sched:
    ENGS = ("pe", "act", "dve", "pool", "sp")

    def __init__(self, nc, es):
        self.nc = nc
        self.es = es
        self.q = {e: [] for e in self.ENGS}
        self.cnt = {e: 0 for e in self.ENGS}
        self.sems = {e: [] for e in self.ENGS}
        self.seen = {e: {} for e in self.ENGS}
        self.dcnt = {e: 0 for e in self.ENGS}
        self.dsems = {e: [] for e in self.ENGS}
        self.nsem = 0

    @staticmethod
    def nd(eng):
        return 16 if eng == "pool" else DMA_ND

    def _newsem(self, name):
        self.nsem += 1
        return self.es.enter_context(self.nc.semaphore(f"{name}{self.nsem}"))

    def _wait(self, eng, deps, is_pe_op):
        for (sem, val, src) in deps:
            if is_pe_op and src == "pe":
                continue
            k = id(sem)
            if self.seen[eng].get(k, 0) >= val:
                continue
            self.seen[eng][k] = val
            self.q[eng].append(lambda e, s=sem, v=val: e.wait_ge(s, v))

    def _deps(self, reads, writes, awrites):
        deps = {}
        for t in reads:
            _merge(deps, t.w.values())
            _merge(deps, t.pw.values())
        for t in writes:
            _merge(deps, t.w.values())
            _merge(deps, t.r.values())
            _merge(deps, t.pw.values())
        for t in awrites:
            _merge(deps, t.pw.values())
        return deps

    def _record(self, tok, reads, writes, awrites):
        for t in reads:
            _merge(t.r, [tok])
        for t in writes:
            t.w = {id(tok[0]): tok}
            t.r = {}
            t.pw = {}
        for t in awrites:
            _merge(t.w, [tok])

    @staticmethod
    def _split(reads, writes):
        ex = [t for t in reads if t.excl]
        if not ex:
            return reads, writes
        return [t for t in reads if not t.excl], list(writes) + ex

    def op(self, eng, fn, reads=(), writes=(), awrites=()):
        reads, writes = self._split(reads, writes)
        deps = self._deps(reads, writes, awrites)
        self._wait(eng, deps.values(), eng == "pe")
        i = self.cnt[eng]
        self.cnt[eng] += 1
        ep = i // EPOCH
        while len(self.sems[eng]) <= ep:
            self.sems[eng].append(self._newsem(eng))
        sem = self.sems[eng][ep]
        self.q[eng].append(lambda e, f=fn, s=sem: f(e).then_inc(s, 1))
        tok = (sem, i % EPOCH + 1, eng)
        self._record(tok, reads, writes, awrites)
        return tok

    def dma(self, eng, fn, reads=(), writes=(), awrites=()):
        reads, writes = self._split(reads, writes)
        deps = self._deps(reads, writes, awrites)
        j = self.dcnt[eng]
        self.dcnt[eng] += 1
        nd = self.nd(eng)
        gen, within = divmod(j, nd * DMA_GEN)
        slot = within % nd
        use = within // nd
        while len(self.dsems[eng]) <= gen:
            self.dsems[eng].append([self._newsem("d" + eng) for _ in range(nd)])
        sem = self.dsems[eng][gen][slot]
        if use > 0:
            _merge(deps, [(sem, 16 * use, "dma")])
        self._wait(eng, deps.values(), False)
        self.q[eng].append(lambda e, f=fn, s=sem: f(e).then_inc(s, 16))
        tok = (sem, 16 * (use + 1), "dma")
        self._record(tok, reads, writes, awrites)
        return tok

    def _all_tokens(self):
        toks = []
        for e in self.ENGS:
            if self.cnt[e] > 0:
                i = self.cnt[e] - 1
                toks.append((self.sems[e][i // EPOCH], i % EPOCH + 1, e))
            nd = self.nd(e)
            for gi, gen in enumerate(self.dsems[e]):
                n_in = min(max(self.dcnt[e] - gi * nd * DMA_GEN, 0), nd * DMA_GEN)
                for slot, sem in enumerate(gen):
                    uses = (n_in - slot + nd - 1) // nd if n_in > slot else 0
                    if uses > 0:
                        toks.append((sem, 16 * uses, "dma"))
        return toks

    def barrier(self):
        toks = self._all_tokens()
        for e in self.ENGS:
            self._wait(e, toks, False)

    def finish(self):
        self._wait("sp", self._all_tokens(), False)

    def emit(self):
        with self.nc.Block() as block:
            @block.sync
            def _(e):
                for f in self.q["sp"]:
                    f(e)

            @block.tensor
            def _(e):
                for f in self.q["pe"]:
                    f(e)

            @block.scalar
            def _(e):
                for f in self.q["act"]:
                    f(e)

            @block.vector
            def _(e):
                for f in self.q["dve"]:
                    f(e)

            @block.gpsimd
            def _(e):
                for f in self.q["pool"]:
                    f(e)


class Ring:
    def __init__(self, tiles):
        self.tiles = [(t, T()) for t in tiles]
        self.i = 0

    def next(self):
        r = self.tiles[self.i % len(self.tiles)]
        self.i += 1
        return r


def interleave(gens):
    gens = list(gens)
    while gens:
        nxt = []
        for g in gens:
            try:
                next(g)
                nxt.append(g)
            except StopIteration:
                pass
        gens = nxt


class Cfg:
    def __init__(self, TC=256, TL=4096, L=4, debug=False, stop=None, stop_layer=0):
        self.TC, self.TL, self.L, self.debug, self.stop, self.stop_layer = TC, TL, L, debug, stop, stop_layer


def build(cfg):
    TC, TL, L = cfg.TC, cfg.TL, cfg.L
    TT = TC + TL
    NCH = TT // 128
    NCC = TC // 128
    TP = TT + 6
    tiles = [(c0, min(512, TC - c0)) for c0 in range(0, TC, 512)] + [(c0, min(512, TT - c0)) for c0 in range(TC, TT, 512)]

    def ppos(c0):
        return c0 + 2 if c0 < TC else c0 + 4

    nc = bass.Bass("TRN2", target_bir_lowering=False)
    kind_dbg = "ExternalOutput" if cfg.debug else "Internal"

    def dram(name, shape, dt=F32, kind="ExternalInput"):
        return nc.dram_tensor(name, list(shape), dt, kind=kind).ap()

    xT_d = dram("xT", [D, TT])
    cond_d = dram("cond", [128, KC * 2])
    consts_d = dram("consts", [128, NCONST * 128])
    rope_d = dram("rope", [2, 128, TL])
    sp_d = dram("spar", [L, 128, NSP])
    rv_d = dram("rvec", [L, NRV])
    w_ada_d = dram("w_ada", [L, D, N_MOD * D])
    w_in_d = dram("w_in", [L, D, IN_WIDTH])
    w_br_d = dram("w_branch", [L, 3, D, D])
    w_out_d = dram("w_out", [L, D, D])
    lruw_d = dram("lruw", [L, 2, 2, KC, 128, 128])
    wq_d = dram("peer_wq", [L, D, 2048])
    skT_d = dram("skT", [L, 16, 128, 128])
    pu_d = dram("peer_u", [L, P_KEYS * P_KEYS, D])
    pv_d = dram("peer_v", [L, P_KEYS * P_KEYS, D])
    out_d = dram("outT", [D, TL], kind="ExternalOutput")
    X_d = dram("X", [D, TT], kind=kind_dbg)
    QKT_d = dram("QKT", [2048, TT], BF16, kind=kind_dbg)
    VTOK_d = dram("VTOK", [TT, D], BF16, kind=kind_dbg)
    GQKV_d = dram("GQKV", [3072, TT], kind=kind_dbg)
    ZS_d = dram("ZS", [TT, D], kind=kind_dbg)
    SG_d = dram("SG", [3072, TT], BF16, kind=kind_dbg)
    YA_d = dram("YA", [D, TT], BF16, kind=kind_dbg)
    YB_d = dram("YB", [D, TT], BF16, kind=kind_dbg)
    YR_d = dram("YR", [D, TT], BF16, kind=kind_dbg)
    OD_d = dram("OD", [2, TT, D], kind=kind_dbg)
    H2_d = dram("H2", [TT, D], kind=kind_dbg)
    HT_d = dram("HTdbg", [D, TT], BF16, kind=kind_dbg) if cfg.debug else None
    DBGF_d = dram("DBGF", [128, TT], kind="ExternalOutput") if cfg.debug else None
    DBGI_d = dram("DBGI", [128, TT], U32, kind="ExternalOutput") if cfg.debug else None
    UV_d = dram("UV", [L * P_KEYS * P_KEYS, 2 * D], BF16, kind="Internal")
    TX, TQKT, TVTOK, TGQKV, TZS, TSG, TYA, TYB, TYR, TOD, TH2 = [T() for _ in range(11)]
    TUB = T()

    es = ExitStack()
    with es:
        S = Sched(nc, es)

        uniq = [0]

        def sbuf(stack, name, shape, dt=F32):
            uniq[0] += 1
            return stack.enter_context(nc.sbuf_tensor(f"{name}_{uniq[0]}", list(shape), dt))

        def ring(stack, name, n, shape, dt=F32):
            return Ring([sbuf(stack, f"{name}{i}", shape, dt) for i in range(n)])

        def act(out, in_, func, reads, writes, aw=(), **kw):
            return S.op("act", lambda e: e.activation(out=out, in_=in_, func=func, **kw), reads, writes, aw)

        def tt(eng, out, in0, in1, op, reads, writes, aw=()):
            return S.op(eng, lambda e: e.tensor_tensor(out=out, in0=in0, in1=in1, op=op), reads, writes, aw)

        def ts(eng, out, in0, s1, op0, reads, writes, s2=None, op1=None, aw=()):
            if op1 is None:
                return S.op(eng, lambda e: e.tensor_scalar(out=out, in0=in0, scalar1=s1, scalar2=None, op0=op0), reads, writes, aw)
            return S.op(eng, lambda e: e.tensor_scalar(out=out, in0=in0, scalar1=s1, scalar2=s2, op0=op0, op1=op1), reads, writes, aw)

        def stt(out, in0, scalar, in1, op0, op1, reads, writes, accum_out=None, aw=()):
            if accum_out is None:
                return S.op("dve", lambda e: e.scalar_tensor_tensor(out=out, in0=in0, scalar=scalar, in1=in1, op0=op0, op1=op1), reads, writes, aw)
            return S.op("dve", lambda e: e.scalar_tensor_tensor(out=out, in0=in0, scalar=scalar, in1=in1, op0=op0, op1=op1, accum_out=accum_out), reads, writes, aw)

        def cp(eng, out, in_, reads, writes, aw=()):
            if eng == "act":
                return act(out, in_, AF.Copy, reads, writes, aw)
            return S.op(eng, lambda e: e.tensor_copy(out=out, in_=in_), reads, writes, aw)

        def recip(out, in_, reads, writes):
            return S.op("dve", lambda e: e.reciprocal(out=out, in_=in_), reads, writes)

        def mm(out, lhsT, rhs, reads, writes, start=True, stop=True):
            return S.op("pe", lambda e: e.matmul(out, lhsT, rhs, start=start, stop=stop), reads, writes)

        def tr(out, in_, reads, writes):
            return S.op("pe", lambda e: e.transpose(out, in_, IDENT), list(reads) + [Tc], writes)

        def dma(q, out, in_, reads=(), writes=(), awrites=()):
            return S.dma(q, lambda e: e.dma_start(out=out, in_=in_), reads, writes, awrites)

        evc = [0]

        def evac(out, in_, reads, writes, aw=()):
            evc[0] += 1
            return cp("act" if evc[0] % 2 else "dve", out, in_, reads, writes, aw)

        PB = [(es.enter_context(nc.psum_tensor(f"pb{i}", [128, 512], F32)), T(excl=True)) for i in range(8)]
        psA = Ring([PB[i][0] for i in range(4)])
        psA.tiles = [PB[i] for i in range(4)]
        psB = Ring([PB[i][0] for i in range(4, 8)])
        psB.tiles = [PB[i] for i in range(4, 8)]
        psAll = Ring([PB[i][0] for i in range(8)])
        psAll.tiles = PB
        psQ = Ring([PB[0][0]])
        psQ.tiles = [(PB[k // 4][0][:, (k % 4) * 128:(k % 4 + 1) * 128], PB[k // 4][1]) for k in range(32)]

        consts = sbuf(es, "consts", [128, NCONST, 128])
        Tc = T()
        dma("sp", consts[:].rearrange("p c f -> p (c f)"), consts_d[:, :], writes=[Tc])
        IDENT, ONES, PERM = consts[:, C_ID, :], consts[:, C_ONES, :], consts[:, C_PERM, :]
        UI, LI, SL, SU = consts[:, C_UI, :], consts[:, C_LI, :], consts[:, C_SL, :], consts[:, C_SU, :]
        cbf = sbuf(es, "cbf", [128, 2, 128], BF16)
        cp("pool", cbf[:, 0, :], consts[:, C_ONES, :], [Tc], [Tc])
        cp("pool", cbf[:, 1, :], consts[:, C_BD64, :], [Tc], [Tc])
        ONES_BF, BD64_BF = cbf[:, 0, :], cbf[:, 1, :]
        cst = sbuf(es, "cst", [128, 4])
        S.op("pool", lambda e: e.memset(cst[:, 0:1], EPS), (), [Tc])
        S.op("pool", lambda e: e.memset(cst[:, 1:2], 1.0), (), [Tc])
        S.op("pool", lambda e: e.memset(cst[:, 2:3], 0.0), (), [Tc])
        EPS_AP, ONE_AP, ZERO_AP = cst[:, 0:1], cst[:, 1:2], cst[:, 2:3]
        scond = sbuf(es, "scond", [128, KC, 2])
        dma("sp", scond[:].rearrange("p k j -> p (k j)"), cond_d[:, :], writes=[Tc])
        act(scond[:], scond[:], AF.Silu, [Tc], [Tc])

        dma("sp", X_d[:, :], xT_d[:, :], awrites=[TX])
        with ExitStack() as ces:
            cfr = ring(ces, "cvf", 2, [128, 8, D])
            cbr = ring(ces, "cvb", 2, [128, 8, D], BF16)
            ci = 0
            dv_ = UV_d.rearrange("(r p j) (t d) -> r p j t d", p=128, j=8, t=2)
            for t_, src_ in enumerate((pu_d, pv_d)):
                sv_ = src_.rearrange("l (r p j) d -> (l r) p j d", p=128, j=8)
                for r_ in range(L * 16):
                    f_, Tf_ = cfr.next()
                    dma("sp", f_[:], sv_[r_], writes=[Tf_])
                    b_, Tb__ = cbr.next()
                    cp(("act", "pool", "dve")[ci % 3], b_[:], f_[:], [Tf_], [Tb__])
                    ci += 1
                    dma("sp", dv_[r_][:, :, t_, :], b_[:], reads=[Tb__], awrites=[TUB])
        S.barrier()

        Xv = X_d.rearrange("(kc p) t -> p kc t", p=128)

        for l in range(L):
            need_ctx = l < L - 1
            lam_init = 0.8 - 0.6 * math.exp(-0.3 * l)
            les = ExitStack()
            with les:
                Tp = T()
                spl = sbuf(les, "spl", [128, NSP])
                dma("sp", spl[:], sp_d[l], writes=[Tp])
                rvl = sbuf(les, "rvl", [128, NRV])
                dma("sp", rvl[:], rv_d[l:l + 1, :].to_broadcast([128, NRV]), writes=[Tp])
                mod = sbuf(les, "mod", [128, 2, 48])
                with ExitStack() as pes:
                    wring = ring(pes, "wada", 2, [128, KC, 512])
                    wav = w_ada_d[l].rearrange("(kc p) f -> p kc f", p=128)
                    pm, Tpm = PB[0]
                    for fg in range(12):
                        wt, Tw = wring.next()
                        dma("sp", wt[:], wav[:, :, fg * 512:(fg + 1) * 512], writes=[Tw])
                        for j in range(4):
                            fc = fg * 4 + j
                            for kc in range(KC):
                                mm(pm[:, 2 * fc:2 * fc + 2], wt[:, kc, j * 128:(j + 1) * 128], scond[:, kc, :], [Tw, Tc], [Tpm],
                                   start=(kc == 0), stop=(kc == KC - 1))
                    pmv = pm[:, 0:96].rearrange("p (f j) -> p j f", j=2)
                    for j in range(2):
                        tt("dve", mod[:, j, :], pmv[:, j, :], spl[:, SP["b_ada"]:SP["b_ada"] + 48], ALU.add, [Tpm, Tp], [Tp])
                S.barrier()
                gs1 = sbuf(les, "gs1", [128, 2, KC])
                gs2 = sbuf(les, "gs2", [128, 2, KC])
                for j in range(2):
                    stt(gs1[:, j, :], mod[:, j, 8:16], 1.0, spl[:, SP["n1g"]:SP["n1g"] + 8], ALU.add, ALU.mult, [Tp], [Tp])
                    stt(gs2[:, j, :], mod[:, j, 32:40], 1.0, spl[:, SP["n2g"]:SP["n2g"] + 8], ALU.add, ALU.mult, [Tp], [Tp])
                sh1 = lambda j, kc: mod[:, j, 0 + kc:1 + kc]
                gt1 = lambda j, kc: mod[:, j, 16 + kc:17 + kc]
                sh2 = lambda j, kc: mod[:, j, 24 + kc:25 + kc]
                gt2 = lambda j, kc: mod[:, j, 40 + kc:41 + kc]
                lamt = sbuf(les, "lamt", [128, 8])
                tt("dve", lamt[0:64, 0:1], spl[0:64, SP["lam"]:SP["lam"] + 1], spl[0:64, SP["lam"] + 1:SP["lam"] + 2], ALU.mult, [Tp], [Tp])
                tt("dve", lamt[0:64, 1:2], spl[0:64, SP["lam"] + 2:SP["lam"] + 3], spl[0:64, SP["lam"] + 3:SP["lam"] + 4], ALU.mult, [Tp], [Tp])
                pl_, Tpl = PB[1]
                mm(pl_[:, 0:2], consts[0:64, C_ONES, :], lamt[0:64, 0:2], [Tp, Tc], [Tpl])
                act(lamt[:, 2:4], pl_[:, 0:2], AF.Exp, [Tpl], [Tp])
                tt("dve", lamt[:, 4:5], lamt[:, 3:4], lamt[:, 2:3], ALU.subtract, [Tp], [Tp])
                ts("dve", lamt[:, 5:6], lamt[:, 4:5], -lam_init, ALU.add, [Tp], [Tp])
                NEGLAM = lamt[:, 5:6]
                ts("dve", lamt[:, 6:7], spl[:, SP["gq"]:SP["gq"] + 1], 0.125, ALU.mult, [Tp], [Tp])
                ts("dve", lamt[:, 7:8], spl[:, SP["subg"]:SP["subg"] + 1], 1.0 - lam_init, ALU.mult, [Tp], [Tp])
                GQ, GK, SUBG = lamt[:, 6:7], spl[:, SP["gk"]:SP["gk"] + 1], lamt[:, 7:8]

                BETA = sbuf(les, "BETA", [128, NCH, 16])
                NEGB = sbuf(les, "NEGB", [128, NCH, 16])
                GG = sbuf(les, "GG", [128, NCH, 16])
                EG = sbuf(les, "EG", [128, NCH, 3, 16])
                BGC = sbuf(les, "BGC", [128, NCH, 16])
                TG = T()

                def norm_mod(pes, tag, gs, shf, out_fn, rings):
                    xr, sqr, rtr, tmr = rings
                    for (c0, n) in tiles:
                        j = 1 if c0 < TC else 0
                        xt_, Tx_ = xr.next()
                        dma("sp", xt_[:, :, :n], Xv[:, :, c0:c0 + n], reads=[TX], writes=[Tx_])
                        sq_, Tsq = sqr.next()
                        act(sq_[:, :, :n], xt_[:, :, :n], AF.Square, [Tx_], [Tsq])
                        pn, Tpn = psB.next()
                        for kc in range(KC):
                            mm(pn[:, :n], ONES_BF, sq_[:, kc, :n], [Tsq, Tc], [Tpn], start=(kc == 0), stop=(kc == KC - 1))
                        rt_, Trt = rtr.next()
                        act(rt_[:, :n], pn[:, :n], AF.Sqrt, [Tpn, Tc], [Trt], scale=1.0 / D, bias=EPS_AP)
                        recip(rt_[:, :n], rt_[:, :n], [Trt], [Trt])
                        tm_, Ttm = tmr.next()
                        tt("dve", tm_[:, :, :n], xt_[:, :, :n], rt_[:, :n].unsqueeze(1).to_broadcast([128, KC, n]), ALU.mult, [Tx_, Trt], [Ttm])
                        out_fn(c0, n, j, tm_, Ttm, gs, shf)

                pes = ExitStack()
                with pes:
                    hT = sbuf(pes, "hT", [128, KC, TT], BF16)
                    ThT = T()
                    with ExitStack() as nes:
                        rings = (ring(nes, "nx", 2, [128, KC, 512]), ring(nes, "nsq", 2, [128, KC, 512], BF16),
                                 ring(nes, "nrt", 2, [128, 512]), ring(nes, "ntm", 2, [128, KC, 512]))

                        def out_h(c0, n, j, tm_, Ttm, gs, shf):
                            for kc in range(KC):
                                act(hT[:, kc, c0:c0 + n], tm_[:, kc, :n], AF.Identity, [Ttm, Tp], (), aw=[ThT],
                                    scale=gs[:, j, kc:kc + 1], bias=shf(j, kc))
                        norm_mod(nes, "n1", gs1, sh1, out_h, rings)
                    S.barrier()
                    if cfg.debug:
                        dma("sp", HT_d.rearrange("(kc p) t -> p kc t", p=128), hT[:], reads=[ThT])
                    if cfg.stop == "norm1" and l == cfg.stop_layer:
                        break

                    wiv = w_in_d[l].rearrange("(kc p) f -> p kc f", p=128)
                    wst = ring(pes, "wst", 2, [128, KC, 128])
                    wbf = ring(pes, "wbf", 2, [128, KC, 128], BF16)

                    def load_w(col0, ncols):
                        ws, Tws = wst.next()
                        dma("sp", ws[:, :, :ncols], wiv[:, :, col0:col0 + ncols], writes=[Tws])
                        wb, Twb = wbf.next()
                        cp("pool", wb[:, :, :ncols], ws[:, :, :ncols], [Tws], [Twb])
                        return wb, Twb

                    def proj_feat(wb, Twb, ncols, c0, n, pring=psA):
                        pp, Tpp = pring.next()
                        for kc in range(KC):
                            mm(pp[:ncols, :n], wb[:, kc, :ncols], hT[:, kc, c0:c0 + n], [Twb, ThT], [Tpp], start=(kc == 0), stop=(kc == KC - 1))
                        return pp, Tpp

                    TQKT.new_version()
                    with ExitStack() as aes:
                        qfr = ring(aes, "qf", 2, [128, 512])
                        sqr = ring(aes, "asq", 2, [128, 512], BF16)
                        rtr = ring(aes, "art", 2, [128, 512])
                        qnr = ring(aes, "qn", 2, [128, 512])
                        csr = ring(aes, "cs", 2, [128, 2, 512])
                        t1r = ring(aes, "t1", 2, [128, 512])
                        t2r = ring(aes, "t2", 2, [128, 512])
                        obr = ring(aes, "qob", 3, [128, 512], BF16)
                        CUT = int(os.environ.get("K_CUT", "99"))
                        for ch in range(16):
                            wb, Twb = load_w(ch * 128, 128)
                            gsc = GQ if ch < 8 else GK
                            for (c0, n) in tiles:
                                if CUT < 2:
                                    continue
                                pp, Tpp = proj_feat(wb, Twb, 128, c0, n)
                                if CUT < 3:
                                    continue
                                qf, Tqf = qfr.next()
                                cp("dve", qf[:, :n], pp[:, :n], [Tpp], [Tqf])
                                sq, Tsq = sqr.next()
                                act(sq[:, :n], qf[:, :n], AF.Square, [Tqf], [Tsq])
                                if CUT < 4:
                                    continue
                                pb, Tpb = psB.next()
                                mm(pb[:, :n], BD64_BF, sq[:, :n], [Tsq, Tc], [Tpb])
                                rt, Trt = rtr.next()
                                act(rt[:, :n], pb[:, :n], AF.Sqrt, [Tpb, Tc], [Trt], scale=1.0 / A_DIM, bias=EPS_AP)
                                recip(rt[:, :n], rt[:, :n], [Trt], [Trt])
                                if CUT < 5:
                                    continue
                                qn, Tqn = qnr.next()
                                stt(qn[:, :n], qf[:, :n], gsc, rt[:, :n], ALU.mult, ALU.mult, [Tqf, Trt, Tp], [Tqn])
                                if CUT < 6:
                                    continue
                                ob, Tob = obr.next()
                                if c0 >= TC and not os.environ.get('K_NOROPE'):
                                    cs, Tcs = csr.next()
                                    dma("sp", cs[:, :, :n], rope_d[:, :, c0 - TC:c0 - TC + n].rearrange("c p t -> p c t"), writes=[Tcs])
                                    pr, Tpr = psB.next()
                                    mm(pr[:, :n], PERM, qn[:, :n], [Tqn, Tc], [Tpr])
                                    t1, Tt1 = t1r.next()
                                    tt("dve", t1[:, :n], qn[:, :n], cs[:, 0, :n], ALU.mult, [Tqn, Tcs], [Tt1])
                                    t2, Tt2 = t2r.next()
                                    tt("dve", t2[:, :n], pr[:, :n], cs[:, 1, :n], ALU.mult, [Tpr, Tcs], [Tt2])
                                    tt("pool", ob[:, :n], t1[:, :n], t2[:, :n], ALU.add, [Tt1, Tt2], [Tob])
                                else:
                                    cp("act", ob[:, :n], qn[:, :n], [Tqn], [Tob])
                                dma("sp", QKT_d[ch * 128:(ch + 1) * 128, c0:c0 + n], ob[:, :n], reads=[Tob], awrites=[TQKT])
                    S.barrier()
                    if cfg.stop == "attnprep" and l == cfg.stop_layer:
                        break

                    tes = ExitStack()
                    wst5 = ring(tes, "wst5", 1, [128, KC, 512])
                    wbf5 = ring(tes, "wbf5", 2, [128, KC, 512], BF16)

                    def load_w5(col0, ncols):
                        ws, Tws = wst5.next()
                        dma("sp", ws[:, :, :ncols], wiv[:, :, col0:col0 + ncols], writes=[Tws])
                        wb, Twb = wbf5.next()
                        cp("pool", wb[:, :, :ncols], ws[:, :, :ncols], [Tws], [Twb])
                        return wb, Twb

                    def proj_tok(wb, Twb, ncols, tk):
                        pp, Tpp = psA.next()
                        for kc in range(KC):
                            mm(pp[:, :ncols], hT[:, kc, tk * 128:(tk + 1) * 128], wb[:, kc, :ncols], [Twb, ThT], [Tpp], start=(kc == 0), stop=(kc == KC - 1))
                        return pp, Tpp

                    TVTOK.new_version()
                    with ExitStack() as ves:
                        vob = ring(ves, "vob", 3, [128, 512], BF16)
                        for jb in range(2):
                            wb, Twb = load_w5(OFF_AV + jb * 512, 512)
                            for tk in range(NCH):
                                pp, Tpp = proj_tok(wb, Twb, 512, tk)
                                ob, Tob = vob.next()
                                evac(ob[:], pp[:], [Tpp], [Tob])
                                dma("sp", VTOK_d[tk * 128:(tk + 1) * 128, jb * 512:(jb + 1) * 512], ob[:], reads=[Tob], awrites=[TVTOK])
                    S.barrier()
                    TZS.new_version()
                    with ExitStack() as ves:
                        zob = ring(ves, "zob", 3, [128, 512])
                        for jb in range(2):
                            wb, Twb = load_w5(OFF_Z + jb * 512, 512)
                            for tk in range(NCH):
                                pp, Tpp = proj_tok(wb, Twb, 512, tk)
                                ob, Tob = zob.next()
                                act(ob[:], pp[:], AF.Silu, [Tpp], [Tob])
                                dma("sp", ZS_d[tk * 128:(tk + 1) * 128, jb * 512:(jb + 1) * 512], ob[:], reads=[Tob], awrites=[TZS])
                    S.barrier()
                    with ExitStack() as ves:
                        negea = sbuf(ves, "negea", [128, 16])
                        Tne = T()
                        act(negea[:], rvl[:, 0:16], AF.Exp, [Tp], [Tne])
                        ts("dve", negea[:], negea[:], -1.0, ALU.mult, [Tne], [Tne])
                        xar = ring(ves, "xa", 2, [128, 16])
                        wb, Twb = load_w5(OFF_BETA, 32)
                        for tk in range(NCH):
                            pp, Tpp = proj_tok(wb, Twb, 32, tk)
                            act(BETA[:, tk, :], pp[:, 0:16], AF.Sigmoid, [Tpp], (), aw=[TG])
                            xa, Txa = xar.next()
                            tt("dve", xa[:], pp[:, 16:32], rvl[:, 16:32], ALU.add, [Tpp, Tp], [Txa])
                            act(xa[:], xa[:], AF.Exp, [Txa], [Txa])
                            act(xa[:], xa[:], AF.Ln, [Txa, Tc], [Txa], bias=ONE_AP)
                            tt("dve", GG[:, tk, :], xa[:], negea[:], ALU.mult, [Txa, Tne], (), aw=[TG])
                            ts("dve", NEGB[:, tk, :], BETA[:, tk, :], -1.0, ALU.mult, [TG], (), aw=[TG])
                            pg, Tpg = psB.next()
                            pgv = pg[:, 0:48].rearrange("p (a b) -> p a b", a=3)
                            for d in range(2):
                                A_, R_ = (UI, SL) if d == 0 else (LI, SU)
                                gsl = GG[:, tk, d * 8:(d + 1) * 8]
                                mm(pgv[:, 0, d * 8:(d + 1) * 8], A_, gsl, [TG, Tc], [Tpg])
                                mm(pgv[:, 1, d * 8:(d + 1) * 8], R_, gsl, [TG, Tc], [Tpg])
                                mm(pgv[:, 2, d * 8:(d + 1) * 8], ONES, gsl, [TG, Tc], [Tpg])
                            act(EG[:, tk, :, :], pgv, AF.Exp, [Tpg], (), aw=[TG])
                            tt("dve", BGC[:, tk, :], BETA[:, tk, :], EG[:, tk, 0, :], ALU.mult, [TG], (), aw=[TG])
                    if cfg.debug and cfg.stop in ("prep1", "gdn", "gdnprep") and l == cfg.stop_layer:
                        o_ = 0
                        for arr, w_ in ((GG, NCH * 16), (BETA, NCH * 16), (BGC, NCH * 16), (NEGB, NCH * 16), (EG, NCH * 48)):
                            dma("sp", DBGF_d[:, o_:o_ + w_], arr[:].rearrange("p a b -> p (a b)") if arr is not EG else arr[:].rearrange("p a b c -> p (a b c)"), reads=[TG])
                            o_ += w_
                    tes.close()
                    S.barrier()
                    TSG.new_version()
                    with ExitStack() as ves:
                        sgo = ring(ves, "sgo", 3, [128, 512], BF16)
                        for ch in range(24):
                            wb, Twb = load_w(OFF_MG + ch * 128, 128)
                            for (c0, n) in tiles:
                                pp, Tpp = proj_feat(wb, Twb, 128, c0, n)
                                ob, Tob = sgo.next()
                                act(ob[:, :n], pp[:, :n], AF.Sigmoid, [Tpp], [Tob])
                                dma("sp", SG_d[ch * 128:(ch + 1) * 128, c0:c0 + n], ob[:, :n], reads=[Tob], awrites=[TSG])
                    S.barrier()
                    if cfg.stop == "prep1" and l == cfg.stop_layer:
                        break
                    raw = sbuf(pes, "raw", [128, TP])
                    acc = sbuf(pes, "acc", [128, TP])
                    Traw, Tacc = T(), T()
                    S.op("pool", lambda e, o=raw[:]: e.memset(o, 0.0), (), [Traw])
                    S.op("pool", lambda e, o=acc[:]: e.memset(o, 0.0), (), [Tacc])

                    def fill_raw(col0):
                        wb, Twb = load_w(col0, 128)
                        Traw.new_version()
                        for (c0, n) in tiles:
                            pp, Tpp = proj_feat(wb, Twb, 128, c0, n)
                            evac(raw[:, ppos(c0):ppos(c0) + n], pp[:, :n], [Tpp], (), aw=[Traw])

                    def conv(wcol, bias_ap):
                        a_out = acc[:, 2:TP - 2]
                        if bias_ap is None:
                            act(a_out, raw[:, 0:TP - 4], AF.Identity, [Traw, Tp], [Tacc], scale=spl[:, wcol:wcol + 1], bias=ZERO_AP)
                        else:
                            act(a_out, raw[:, 0:TP - 4], AF.Identity, [Traw, Tp], [Tacc], scale=spl[:, wcol:wcol + 1], bias=bias_ap)
                        for j in range(1, 5):
                            stt(a_out, raw[:, j:TP - 4 + j], spl[:, wcol + j:wcol + j + 1], a_out, ALU.mult, ALU.add, [Traw, Tp, Tacc], [Tacc])

                    TGQKV.new_version()
                    with ExitStack() as ves:
                        sqr = ring(ves, "gsq", 2, [128, 512], BF16)
                        rtr = ring(ves, "grt", 2, [128, 512])
                        gor = ring(ves, "gout", 3, [128, 512])
                        for ch in range(24):
                            typ = ch // 8
                            fill_raw(OFF_BQKV + ch * 128)
                            conv(SP["gconv"] + ch * 5, None)
                            act(acc[:, 2:TP - 2], acc[:, 2:TP - 2], AF.Silu, [Tacc], [Tacc])
                            for (c0, n) in tiles:
                                src = acc[:, ppos(c0):ppos(c0) + n]
                                if typ == 2:
                                    dma("sp", GQKV_d[ch * 128:(ch + 1) * 128, c0:c0 + n], src, reads=[Tacc], awrites=[TGQKV])
                                    continue
                                sq, Tsq = sqr.next()
                                act(sq[:, :n], src, AF.Square, [Tacc], [Tsq])
                                pb, Tpb = psB.next()
                                mm(pb[:, :n], ONES_BF, sq[:, :n], [Tsq, Tc], [Tpb])
                                rt, Trt = rtr.next()
                                act(rt[:, :n], pb[:, :n], AF.Sqrt, [Tpb, Tc], [Trt], scale=1.0, bias=EPS_AP)
                                recip(rt[:, :n], rt[:, :n], [Trt], [Trt])
                                go, Tgo = gor.next()
                                stt(go[:, :n], src, (B_DIM ** -0.5) if typ == 0 else 1.0, rt[:, :n], ALU.mult, ALU.mult, [Tacc, Trt], [Tgo])
                                dma("sp", GQKV_d[ch * 128:(ch + 1) * 128, c0:c0 + n], go[:, :n], reads=[Tgo], awrites=[TGQKV])
                    S.barrier()
                    if cfg.stop == "gdnprep" and l == cfg.stop_layer:
                        break
                    TYR.new_version()
                    with ExitStack() as ves:
                        hsum = sbuf(ves, "hsum", [128, TT])
                        Ths = [T() for _ in tiles]
                        cn = sbuf(ves, "cneg", [128, 3, 16])
                        Tcn = T()
                        act(cn[:, 0, :], spl[:, SP["llam"]:SP["llam"] + 16], AF.Exp, [Tp], [Tcn], scale=-1.0)
                        act(cn[:, 0, :], cn[:, 0, :], AF.Ln, [Tcn, Tc], [Tcn], bias=ONE_AP)
                        ts("dve", cn[:, 1, :], cn[:, 0, :], -2.0 * LRU_C, ALU.mult, [Tcn], [Tcn])
                        ts("dve", cn[:, 2, :], cn[:, 0, :], LRU_C, ALU.mult, [Tcn], [Tcn])
                        ts("dve", cn[:, 0, :], cn[:, 0, :], -LRU_C, ALU.mult, [Tcn], [Tcn])
                        lwr = ring(ves, "lw", 2, [128, 2, 2, 128])
                        rr = ring(ves, "lr", 2, [128, 512])
                        ir = ring(ves, "li", 2, [128, 512])
                        ar = ring(ves, "la", 2, [128, 512])
                        a2r = ring(ves, "la2", 2, [128, 512])
                        thr = ring(ves, "lth", 2, [128, 512])
                        hbr = ring(ves, "lhb", 2, [128, 512])
                        glr = ring(ves, "lgl", 2, [128, 512])
                        yor = ring(ves, "lyo", 3, [128, 512], BF16)
                        ctiles = [t_ for t_ in enumerate(tiles) if t_[1][0] < TC]
                        ltiles = [t_ for t_ in enumerate(tiles) if t_[1][0] >= TC]
                        for c in range(KC):
                            fill_raw(OFF_LX + c * 128)
                            conv(SP["lconv"] + c * 5, spl[:, SP["lconvb"] + c:SP["lconvb"] + c + 1])
                            lw, Tlw = lwr.next()
                            for d in range(2):
                                for ri in range(2):
                                    dma("sp", lw[:, d, ri, :], lruw_d[l, d, ri, c], writes=[Tlw] if (d == 0 and ri == 0) else (), awrites=() if (d == 0 and ri == 0) else [Tlw])
                            for d in range(2):
                                order = (ctiles + ltiles) if d == 0 else (ctiles[::-1] + ltiles[::-1])
                                carry, Tcar = None, None
                                col = d * 8 + c
                                for (ti, (c0, n)) in order:
                                    xs = acc[:, ppos(c0):ppos(c0) + n]
                                    pr, Tpr = psA.next()
                                    mm(pr[:, :n], lw[:, d, 0, :], xs, [Tlw, Tacc], [Tpr])
                                    pi, Tpi = psA.next()
                                    mm(pi[:, :n], lw[:, d, 1, :], xs, [Tlw, Tacc], [Tpi])
                                    r_, Tr = rr.next()
                                    act(r_[:, :n], pr[:, :n], AF.Sigmoid, [Tpr, Tp], [Tr], bias=spl[:, SP["lbr"] + col:SP["lbr"] + col + 1])
                                    i_, Ti = ir.next()
                                    act(i_[:, :n], pi[:, :n], AF.Sigmoid, [Tpi, Tp], [Ti], bias=spl[:, SP["lbi"] + col:SP["lbi"] + col + 1])
                                    a_, Ta = ar.next()
                                    act(a_[:, :n], r_[:, :n], AF.Exp, [Tr, Tcn], [Ta], scale=cn[:, 0, col:col + 1])
                                    a2, Ta2 = a2r.next()
                                    act(a2[:, :n], r_[:, :n], AF.Exp, [Tr, Tcn], [Ta2], scale=cn[:, 1, col:col + 1])
                                    th, Tth = thr.next()
                                    act(th[:, :n], r_[:, :n], AF.Tanh, [Tr, Tcn], [Tth], scale=cn[:, 2, col:col + 1])
                                    stt(a2[:, :n], a2[:, :n], 1.0, th[:, :n], ALU.add, ALU.mult, [Ta2, Tth], [Ta2])
                                    act(a2[:, :n], a2[:, :n], AF.Sqrt, [Ta2], [Ta2])
                                    tt("dve", i_[:, :n], i_[:, :n], a2[:, :n], ALU.mult, [Ti, Ta2], [Ti])
                                    tt("pool", i_[:, :n], i_[:, :n], xs, ALU.mult, [Ti, Tacc], [Ti])
                                    init = 0.0 if carry is None else carry
                                    rds = [Ta, Ti] + ([Tcar] if Tcar is not None else [])
                                    if d == 0:
                                        S.op("dve", lambda e, o=hsum[:, c0:c0 + n], d0=a_[:, :n], d1=i_[:, :n], ini=init:
                                             e.tensor_tensor_scan(out=o, data0=d0, data1=d1, initial=ini, op0=ALU.mult, op1=ALU.add), rds, [Ths[ti]])
                                        carry, Tcar = hsum[:, c0 + n - 1:c0 + n], Ths[ti]
                                    else:
                                        hb, Thb = hbr.next()
                                        S.op("dve", lambda e, o=hb[:, :n][:, ::-1], d0=a_[:, :n][:, ::-1], d1=i_[:, :n][:, ::-1], ini=init:
                                             e.tensor_tensor_scan(out=o, data0=d0, data1=d1, initial=ini, op0=ALU.mult, op1=ALU.add), rds, [Thb])
                                        carry, Tcar = hb[:, 0:1], Thb
                                        tt("pool", hsum[:, c0:c0 + n], hsum[:, c0:c0 + n], hb[:, :n], ALU.add, [Ths[ti], Thb], [Ths[ti]])
                            wb, Twb = load_w(OFF_LG + c * 128, 128)
                            for ti, (c0, n) in enumerate(tiles):
                                pp, Tpp = proj_feat(wb, Twb, 128, c0, n)
                                gl, Tgl = glr.next()
                                act(gl[:, :n], pp[:, :n], AF.Gelu_apprx_tanh, [Tpp], [Tgl])
                                yo, Tyo = yor.next()
                                tt("dve", yo[:, :n], gl[:, :n], hsum[:, c0:c0 + n], ALU.mult, [Tgl, Ths[ti]], [Tyo])
                                dma("sp", YR_d[c * 128:(c + 1) * 128, c0:c0 + n], yo[:, :n], reads=[Tyo], awrites=[TYR])
                S.barrier()
                if cfg.stop == "prep" and l == cfg.stop_layer:
                    break
                TYA.new_version()
                with ExitStack() as aes:
                    kTr = ring(aes, "kTh", 2, [128, TT], BF16)
                    Vr = ring(aes, "Vh", 2, [128, NCH, 128], BF16)
                    qbr = ring(aes, "qb", 2, [128, 512], BF16)
                    ptr = ring(aes, "pt", 4, [128, 512], BF16)
                    r0r = ring(aes, "ar0", 2, [128, 512])
                    t0r = ring(aes, "at0", 2, [128, 512])
                    t1r = ring(aes, "at1", 2, [128, 512])
                    sqr = ring(aes, "asq2", 2, [128, 512], BF16)
                    yar = ring(aes, "aya", 2, [128, 512], BF16)
                    VTv = VTOK_d.rearrange("(n p) e -> p n e", p=128)
                    for h in range(A_HEADS):
                        kT, TkT = kTr.next()
                        dma("sp", kT[:], QKT_d[1024 + h * 128:1024 + (h + 1) * 128, :], reads=[TQKT], writes=[TkT])
                        V, TV = Vr.next()
                        TV.new_version()
                        for k0_ in range(0, NCH, 8):
                            k1_ = min(NCH, k0_ + 8)
                            dma("sp", V[:, k0_:k1_, :], VTv[:, k0_:k1_, h * 128:(h + 1) * 128], reads=[TVTOK], awrites=[TV])
                        blocks = ([(0, TC, 0, NCC)] if need_ctx else []) + [(c0, n, 0, NCH) for (c0, n) in tiles if c0 >= TC]
                        for (q0, nq, k0, k1) in blocks:
                            qb, Tqb = qbr.next()
                            dma("sp", qb[:, :nq], QKT_d[h * 128:(h + 1) * 128, q0:q0 + nq], reads=[TQKT], writes=[Tqb])
                            (O0, TO0), (Z0, TZ0), (O1, TO1), (Z1, TZ1) = PB[0], PB[1], PB[2], PB[3]
                            OZ = ((O0, TO0, Z0, TZ0), (O1, TO1, Z1, TZ1))
                            pend = []

                            def consume(item):
                                st, Tst, kt, c = item
                                pt, Tpt = ptr.next()
                                act(pt[:, :nq], st[:, :nq], AF.Exp, [Tst], [Tpt])
                                O_, TO_, Z_, TZ_ = OZ[c]
                                mm(O_[:, :nq], V[:, kt, :], pt[:, :nq], [TV, Tpt], [TO_], start=(kt == k0), stop=(kt == k1 - 1))
                                mm(Z_[:, :nq], ONES_BF, pt[:, :nq], [Tpt, Tc], [TZ_], start=(kt == k0), stop=(kt == k1 - 1))

                            for kt in range(k0, k1):
                                for c in range(2):
                                    st, Tst = psB.next()
                                    mm(st[:, :nq], kT[c * 64:(c + 1) * 64, kt * 128:(kt + 1) * 128], qb[c * 64:(c + 1) * 64, :nq], [TkT, Tqb], [Tst])
                                    pend.append((st, Tst, kt, c))
                                    if len(pend) > 2:
                                        consume(pend.pop(0))
                            while pend:
                                consume(pend.pop(0))
                            r0, Tr0 = r0r.next()
                            recip(r0[:, :nq], Z0[:, :nq], [TZ0], [Tr0])
                            t0, Tt0 = t0r.next()
                            tt("dve", t0[:, :nq], O0[:, :nq], r0[:, :nq], ALU.mult, [TO0, Tr0], [Tt0])
                            r1, Tr1 = r0r.next()
                            recip(r1[:, :nq], Z1[:, :nq], [TZ1], [Tr1])
                            t1, Tt1 = t1r.next()
                            tt("dve", t1[:, :nq], O1[:, :nq], r1[:, :nq], ALU.mult, [TO1, Tr1], [Tt1])
                            stt(t0[:, :nq], t1[:, :nq], NEGLAM, t0[:, :nq], ALU.mult, ALU.add, [Tt1, Tt0, Tp], [Tt0])
                            sq, Tsq = sqr.next()
                            act(sq[:, :nq], t0[:, :nq], AF.Square, [Tt0], [Tsq])
                            pss, Tpss = psB.next()
                            mm(pss[:, :nq], ONES_BF, sq[:, :nq], [Tsq, Tc], [Tpss])
                            act(r1[:, :nq], pss[:, :nq], AF.Sqrt, [Tpss, Tc], [Tr1], scale=1.0 / 128.0, bias=EPS_AP)
                            recip(r1[:, :nq], r1[:, :nq], [Tr1], [Tr1])
                            ya, Tya = yar.next()
                            stt(ya[:, :nq], t0[:, :nq], SUBG, r1[:, :nq], ALU.mult, ALU.mult, [Tt0, Tr1, Tp], [Tya])
                            dma("sp", YA_d[h * 128:(h + 1) * 128, q0:q0 + nq], ya[:, :nq], reads=[Tya], awrites=[TYA])
                S.barrier()
                if cfg.stop == "attn" and l == cfg.stop_layer:
                    break

                TOD.new_version()
                for d in range(2):
                    with ExitStack() as ges:
                        A_, B_, Ms, MiT = (UI, SL, SL, UI) if d == 0 else (LI, SU, SU, LI)
                        order = list(range(NCH)) if d == 0 else (list(range(NCC - 1, -1, -1)) + list(range(NCH - 1, NCC - 1, -1)))

                        def chain(h):
                            col = d * 8 + h
                            nm = f"g{d}{h}"
                            mk = lambda n_: (sbuf(ges, nm + n_, [128, 128]), T())
                            (St, TS), (qT, TqT), (kT, TkT), (vT, TvT) = mk("S"), mk("q"), mk("k"), mk("v")
                            (g2, Tg2), (e_, Te), (et_, Tet), (Ru, TRu), (Rw, TRw), (kd, Tkd) = mk("g2"), mk("e"), mk("et"), mk("Ru"), mk("Rw"), mk("kd")
                            (N0, TN0), (NT0, TNT0), (N1, TN1), (NT1, TNT1) = mk("N0"), mk("NT0"), mk("N1"), mk("NT1")
                            (P0, TP0), (P1, TP1), (qkT, Tqk) = mk("P0"), mk("P1"), mk("qk")
                            (D2, TD2), (E2, TE2), (Ysb, TY), (Zsb, TZ) = mk("D2"), mk("E2"), mk("Ysb"), mk("Zsb")
                            (u_, Tu), (wT, TwT), (vn, Tvn), (o1s, To1), (oo, Too) = mk("u"), mk("wT"), mk("vn"), mk("o1"), mk("oo")
                            S.op("pool", lambda e, o=St[:]: e.memset(o, 0.0), (), [TS])
                            Tb_ = PB[h][1]
                            Q = [PB[h][0][:, i_ * 128:(i_ + 1) * 128] for i_ in range(4)]
                            yield
                            for n in order:
                                c0 = n * 128
                                dma("sp", qT[:], GQKV_d[h * 128:(h + 1) * 128, c0:c0 + 128], reads=[TGQKV], writes=[TqT])
                                dma("sp", kT[:], GQKV_d[1024 + h * 128:1024 + (h + 1) * 128, c0:c0 + 128], reads=[TGQKV], writes=[TkT])
                                dma("sp", vT[:], GQKV_d[2048 + h * 128:2048 + (h + 1) * 128, c0:c0 + 128], reads=[TGQKV], writes=[TvT])
                                ts("pool", g2[:], B_, GG[:, n, col:col + 1], ALU.mult, [TG, Tc], [Tg2])
                                yield
                                pk, pv, pd, pdt = Q
                                tr(pk, kT[:], [TkT], [Tb_])
                                tr(pv, vT[:], [TvT], [Tb_])
                                mm(pd, A_, g2[:], [Tg2, Tc], [Tb_])
                                mm(pdt, g2[:], A_, [Tg2, Tc], [Tb_])
                                yield
                                act(e_[:], pd, AF.Exp, [Tb_], [Te])
                                act(et_[:], pdt, AF.Exp, [Tb_], [Tet])
                                act(kd[:], pk, AF.Identity, [Tb_, TG, Tc], [Tkd], scale=EG[:, n, 1, col:col + 1], bias=ZERO_AP)
                                yield
                                ts("dve", Ru[:], pv, BETA[:, n, col:col + 1], ALU.mult, [Tb_, TG], [TRu])
                                ts("dve", Rw[:], pk, BGC[:, n, col:col + 1], ALU.mult, [Tb_, TG], [TRw])
                                tt("pool", e_[:], e_[:], Ms, ALU.mult, [Te, Tc], [Te])
                                tt("pool", et_[:], et_[:], MiT, ALU.mult, [Tet, Tc], [Tet])
                                yield
                                pkk, pkq = Q[0], Q[1]
                                mm(pkk, kT[:], kT[:], [TkT], [Tb_])
                                mm(pkq, kT[:], qT[:], [TkT, TqT], [Tb_])
                                yield
                                stt(N0[:], pkk, NEGB[:, n, col:col + 1], e_[:], ALU.mult, ALU.mult, [Tb_, TG, Te], [TN0])
                                tt("dve", qkT[:], pkq, et_[:], ALU.mult, [Tb_, Tet], [Tqk])
                                yield
                                pn = Q[2]
                                tr(pn, N0[:], [TN0], [Tb_])
                                yield
                                cp("act", NT0[:], pn, [Tb_], [TNT0])
                                yield
                                OFFK = [consts[:, C_OFF + k_, :] for k_ in range(7)]
                                tt("pool", N1[:], N0[:], OFFK[0], ALU.mult, [TN0, Tc], [TN1])
                                tt("pool", NT1[:], NT0[:], OFFK[0], ALU.mult, [TNT0, Tc], [TNT1])
                                yield
                                tt("pool", P0[:], N1[:], IDENT, ALU.add, [TN1, Tc], [TP0])
                                tt("pool", P1[:], NT1[:], IDENT, ALU.add, [TNT1, Tc], [TP1])
                                Dc, Ec = (P0, TP0), (P1, TP1)
                                Dn, En = (D2, TD2), (E2, TE2)
                                for k in range(1, 7):
                                    last = (k == 6)
                                    tt("pool", N1[:], N0[:], OFFK[k], ALU.mult, [TN0, Tc], [TN1])
                                    if not last:
                                        tt("pool", NT1[:], NT0[:], OFFK[k], ALU.mult, [TNT0, Tc], [TNT1])
                                    yield
                                    mm(Q[0], N1[:], Ec[0][:], [TN1, Ec[1]], [Tb_])
                                    if not last:
                                        mm(Q[1], NT1[:], Dc[0][:], [TNT1, Dc[1]], [Tb_])
                                    yield
                                    cp("act", Ysb[:], Q[0], [Tb_], [TY])
                                    if not last:
                                        cp("dve", Zsb[:], Q[1], [Tb_], [TZ])
                                    yield
                                    mm(Q[2], Dc[0][:], Ysb[:], [Dc[1], TY], [Tb_])
                                    if not last:
                                        mm(Q[3], Ec[0][:], Zsb[:], [Ec[1], TZ], [Tb_])
                                    yield
                                    tt("dve", En[0][:], Q[2], Ec[0][:], ALU.add, [Tb_, Ec[1]], [En[1]])
                                    if not last:
                                        tt("dve", Dn[0][:], Q[3], Dc[0][:], ALU.add, [Tb_, Dc[1]], [Dn[1]])
                                    Dc, Dn = Dn, Dc
                                    Ec, En = En, Ec
                                    yield
                                Pc = Ec
                                PI, TPI = Pc
                                pu, pw_ = Q[0], Q[1]
                                mm(pu, PI[:], Ru[:], [TPI, TRu], [Tb_])
                                mm(pw_, Rw[:], PI[:], [TPI, TRw], [Tb_])
                                yield
                                cp("act", u_[:], pu, [Tb_], [Tu])
                                cp("act", wT[:], pw_, [Tb_], [TwT])
                                yield
                                pa, po = Q[2], Q[3]
                                mm(pa, wT[:], St[:], [TwT, TS], [Tb_])
                                mm(po, qT[:], St[:], [TqT, TS], [Tb_])
                                yield
                                tt("dve", vn[:], u_[:], pa, ALU.subtract, [Tu, Tb_], [Tvn])
                                ts("dve", o1s[:], po, EG[:, n, 0, col:col + 1], ALU.mult, [Tb_, TG], [To1])
                                yield
                                po2, ps_ = Q[0], Q[1]
                                mm(po2, qkT[:], vn[:], [Tqk, Tvn], [Tb_])
                                mm(ps_, kd[:], vn[:], [Tkd, Tvn], [Tb_])
                                yield
                                if cfg.debug and os.environ.get("K_GDBG") and d == 0 and h == 0 and n == order[0]:
                                    for i_, (arr_, T_) in enumerate(((e_, Te), (N0, TN0), (PI, TPI), (u_, Tu), (qkT, Tqk))):
                                        dma("sp", DBGF_d[:, i_ * 128:(i_ + 1) * 128], arr_[:], reads=[T_])
                                tt("dve", oo[:], po2, o1s[:], ALU.add, [Tb_, To1], [Too])
                                stt(St[:], St[:], EG[:, n, 2, col:col + 1], ps_, ALU.mult, ALU.add, [TS, TG, Tb_], [TS])
                                dma("sp", OD_d[d, c0:c0 + 128, h * 128:(h + 1) * 128], oo[:], reads=[Too], awrites=[TOD])
                                yield

                        interleave([chain(h) for h in range(B_HEADS)])
                    S.barrier()
                if cfg.stop == "gdn" and l == cfg.stop_layer:
                    break

                TYB.new_version()
                with ExitStack() as fes:
                    o0r = ring(fes, "fo0", 2, [128, D])
                    o1r = ring(fes, "fo1", 2, [128, D])
                    zsr = ring(fes, "fzs", 2, [128, D])
                    jkr = ring(fes, "fjk", 2, [128, 128])
                    ssr = ring(fes, "fss", 2, [128, 8])
                    ybr = ring(fes, "fyb", 3, [128, KC, 128], BF16)
                    for tk in range(NCH):
                        o0, To0 = o0r.next()
                        dma("sp", o0[:], OD_d[0, tk * 128:(tk + 1) * 128, :], reads=[TOD], writes=[To0])
                        o1, To1_ = o1r.next()
                        dma("sp", o1[:], OD_d[1, tk * 128:(tk + 1) * 128, :], reads=[TOD], writes=[To1_])
                        zs, Tzs = zsr.next()
                        dma("sp", zs[:], ZS_d[tk * 128:(tk + 1) * 128, :], reads=[TZS], writes=[Tzs])
                        tt("pool", o0[:], o0[:], o1[:], ALU.add, [To0, To1_], [To0])
                        ss, Tss = ssr.next()
                        Tss.new_version()
                        for h in range(B_HEADS):
                            jk, Tjk = jkr.next()
                            act(jk[:], o0[:, h * 128:(h + 1) * 128], AF.Square, [To0], [Tjk], aw=[Tss], accum_out=ss[:, h:h + 1])
                        act(ss[:], ss[:], AF.Sqrt, [Tss, Tc], [Tss], scale=1.0 / B_DIM, bias=EPS_AP)
                        recip(ss[:], ss[:], [Tss], [Tss])
                        o3 = o0[:].rearrange("p (h e) -> p h e", h=B_HEADS)
                        tt("dve", o3, o3, ss[:].unsqueeze(2).to_broadcast([128, B_HEADS, 128]), ALU.mult, [To0, Tss], [To0])
                        tt("dve", o3, o3, rvl[:, 32:160].unsqueeze(1).to_broadcast([128, B_HEADS, 128]), ALU.mult, [To0, Tp], [To0])
                        tt("pool", o0[:], o0[:], zs[:], ALU.mult, [To0, Tzs], [To0])
                        yb, Tyb = ybr.next()
                        Tyb.new_version()
                        for h in range(B_HEADS):
                            pq, Tpq = psQ.next()
                            tr(pq, o0[:, h * 128:(h + 1) * 128], [To0], [Tpq])
                            evac(yb[:, h, :], pq, [Tpq], (), aw=[Tyb])
                        dma("sp", YB_d.rearrange("(h p) t -> p h t", p=128)[:, :, tk * 128:(tk + 1) * 128], yb[:], reads=[Tyb], awrites=[TYB])
                S.barrier()
                if cfg.stop == "gdnfin" and l == cfg.stop_layer:
                    break

                TX.new_version()
                mtiles = []
                for (c0, n) in tiles:
                    for s0 in range(0, n, 256):
                        mtiles.append((c0 + s0, min(256, n - s0)))
                with ExitStack() as mes:
                    wbr = sbuf(mes, "wbr", [128, 3, KC, D], BF16)
                    wou = sbuf(mes, "wou", [128, KC, D], BF16)
                    Twm = T()
                    wstg = ring(mes, "wstg", 2, [128, KC, 256])
                    srcs = [(w_br_d[l, k].rearrange("(kc p) f -> p kc f", p=128), wbr[:, k]) for k in range(3)] + [(w_out_d[l].rearrange("(kc p) f -> p kc f", p=128), wou[:])]
                    for (sv, dv) in srcs:
                        for jb in range(4):
                            ws, Tws = wstg.next()
                            dma("sp", ws[:], sv[:, :, jb * 256:(jb + 1) * 256], writes=[Tws])
                            cp("pool", dv[:, :, jb * 256:(jb + 1) * 256], ws[:], [Tws], (), aw=[Twm])
                    ysr = [ring(mes, f"ys{k}", 2, [128, KC, 256], BF16) for k in range(3)]
                    sgr = ring(mes, "msg", 2, [128, KC, 256], BF16)
                    yacr = ring(mes, "yacc", 1, [128, KC, 256])
                    tmr = ring(mes, "mtm", 2, [128, 256])
                    ybfr = ring(mes, "ybf", 2, [128, KC, 256], BF16)
                    xtr = ring(mes, "mxt", 2, [128, KC, 256])
                    xor = ring(mes, "mxo", 2, [128, KC, 256])
                    YS_d = (YA_d, YB_d, YR_d)
                    TYS = (TYA, TYB, TYR)
                    SGv = SG_d.rearrange("(k c p) t -> p k c t", p=128, k=3)
                    for (c0, n) in mtiles:
                        if c0 < TC and not need_ctx:
                            continue
                        j = 1 if c0 < TC else 0
                        xt_, Txt = xtr.next()
                        dma("sp", xt_[:, :, :n], Xv[:, :, c0:c0 + n], reads=[TX], writes=[Txt])
                        yac, Tyac = yacr.next()
                        for k in range(3):
                            y_, Ty_ = ysr[k].next()
                            dma("sp", y_[:, :, :n], YS_d[k].rearrange("(kc p) t -> p kc t", p=128)[:, :, c0:c0 + n], reads=[TYS[k]], writes=[Ty_])
                            sg, Tsg = sgr.next()
                            dma("sp", sg[:, :, :n], SGv[:, k, :, c0:c0 + n], reads=[TSG], writes=[Tsg])
                            for fo in range(KC):
                                pp, Tpp = psAll.next()
                                for kc in range(KC):
                                    mm(pp[:, :n], wbr[:, k, kc, fo * 128:(fo + 1) * 128], y_[:, kc, :n], [Twm, Ty_], [Tpp], start=(kc == 0), stop=(kc == KC - 1))
                                if k == 0:
                                    tt("dve", yac[:, fo, :n], pp[:, :n], sg[:, fo, :n], ALU.mult, [Tpp, Tsg], (), aw=[Tyac])
                                else:
                                    tm, Ttm = tmr.next()
                                    tt("dve", tm[:, :n], pp[:, :n], sg[:, fo, :n], ALU.mult, [Tpp, Tsg], [Ttm])
                                    tt("pool", yac[:, fo, :n], yac[:, fo, :n], tm[:, :n], ALU.add, [Ttm, Tyac], (), aw=[Tyac])
                        ybf, Tybf = ybfr.next()
                        cp("act", ybf[:, :, :n], yac[:, :, :n], [Tyac], [Tybf])
                        Tyac.new_version()
                        xo, Txo = xor.next()
                        Txo.new_version()
                        for fo in range(KC):
                            pp, Tpp = psAll.next()
                            for kc in range(KC):
                                mm(pp[:, :n], wou[:, kc, fo * 128:(fo + 1) * 128], ybf[:, kc, :n], [Twm, Tybf], [Tpp], start=(kc == 0), stop=(kc == KC - 1))
                            stt(xo[:, fo, :n], pp[:, :n], gt1(j, fo), xt_[:, fo, :n], ALU.mult, ALU.add, [Tpp, Tp, Txt], (), aw=[Txo])
                        dma("sp", Xv[:, :, c0:c0 + n], xo[:, :, :n], reads=[Txo], awrites=[TX])
                S.barrier()
                if cfg.stop == "merge" and l == cfg.stop_layer:
                    break
                H2F_T = T()
                TH2.new_version()
                ptok0 = 0 if need_ctx else TC
                ptiles = [(c0, n) for (c0, n) in mtiles if c0 >= ptok0]
                H2F_d = GQKV_d[0:D, :]
                H2Fv = H2F_d.rearrange("(kc p) t -> p kc t", p=128)
                with ExitStack() as nes:
                    rings = (ring(nes, "n2x", 2, [128, KC, 512]), ring(nes, "n2sq", 2, [128, KC, 512], BF16),
                             ring(nes, "n2rt", 2, [128, 512]), ring(nes, "n2tm", 1, [128, KC, 512]))
                    h2r = ring(nes, "h2o", 2, [128, KC, 512])
                    htr = ring(nes, "h2t", 2, [128, D])

                    def out_h2(c0, n, j, tm_, Ttm, gs, shf):
                        if c0 + n <= ptok0:
                            return
                        h2, Th2 = h2r.next()
                        Th2.new_version()
                        for kc in range(KC):
                            act(h2[:, kc, :n], tm_[:, kc, :n], AF.Identity, [Ttm, Tp], (), aw=[Th2], scale=gs[:, j, kc:kc + 1], bias=shf(j, kc))
                        dma("sp", H2Fv[:, :, c0:c0 + n], h2[:, :, :n], reads=[Th2], awrites=[H2F_T])
                        for s0 in range(0, n, 128):
                            ht, Tht = htr.next()
                            Tht.new_version()
                            for kc in range(KC):
                                pq, Tpq = psQ.next()
                                tr(pq, h2[:, kc, s0:s0 + 128], [Th2], [Tpq])
                                evac(ht[:, kc * 128:(kc + 1) * 128], pq, [Tpq], (), aw=[Tht])
                            dma("sp", H2_d[c0 + s0:c0 + s0 + 128, :], ht[:], reads=[Tht], awrites=[TH2])
                    norm_mod(nes, "n2", gs2, sh2, out_h2, rings)
                S.barrier()
                if cfg.stop == "peernorm" and l == cfg.stop_layer:
                    break
                IDXT = sbuf(les, "IDXT", [128, TT], U32)
                GATET = sbuf(les, "GATET", [128, TT])
                TIG = T()
                with ExitStack() as qes:
                    skt = sbuf(qes, "skt", [128, 16, 128])
                    Tsk = T()
                    for g0_ in range(0, 16, 4):
                        dma("sp", skt[:, g0_:g0_ + 4, :], skT_d[l, g0_:g0_ + 4].rearrange("g d k -> d g k"), awrites=[Tsk])
                    wqv = wq_d[l].rearrange("(kc p) f -> p kc f", p=128)
                    wqr = ring(qes, "wq", 2, [128, KC, 128])
                    h2r = ring(qes, "h2i", 2, [128, KC, 256])
                    QN = sbuf(qes, "QN", [128, 16, 256])
                    TQN = T()
                    qfr = ring(qes, "pqf", 2, [128, 256])
                    sqr = ring(qes, "psq", 2, [128, 256])
                    rtr = ring(qes, "prt", 2, [128, 256])
                    SCr = ring(qes, "SC", 2, [128, 16, 128])
                    m1r = ring(qes, "m1", 2, [128, 16, 16])
                    ixr = ring(qes, "ix", 2, [128, 16, 16], U32)
                    wkr = ring(qes, "wk", 2, [128, 256])
                    wk16 = sbuf(qes, "wk16", [128, 16, 128])
                    Twk16 = [T() for _ in range(16)]
                    ixf = sbuf(qes, "ixf", [128, 16, 16])
                    cand = sbuf(qes, "cand", [128, 8, 256])
                    b1 = sbuf(qes, "b1", [128, 8, 16])
                    pos = sbuf(qes, "pos", [128, 8, 16], U32)
                    pab = sbuf(qes, "pab", [128, 2, 8, 16], U32)
                    pabf = sbuf(qes, "pabf", [128, 2, 8, 16])
                    oh = sbuf(qes, "oh", [128, 8, 16, 16])
                    isel = sbuf(qes, "isel", [128, 2, 8, 16])
                    idxf = sbuf(qes, "idxf", [128, 128])
                    gate = sbuf(qes, "gate", [128, 8, 16])
                    gsm = sbuf(qes, "gsm", [128, 8])
                    Tk_ = T()
                    IOTA16 = consts[:, C_IOTA0, 0:16]
                    for (c0, n) in ptiles:
                        h2, Th2 = h2r.next()
                        dma("sp", h2[:, :, :n], H2Fv[:, :, c0:c0 + n], reads=[H2F_T], writes=[Th2])
                        TQN.new_version()
                        for gi in range(16):
                            wq, Twq = wqr.next()
                            dma("sp", wq[:], wqv[:, :, gi * 128:(gi + 1) * 128], writes=[Twq])
                            pp, Tpp = psA.next()
                            for kc in range(KC):
                                mm(pp[:, :n], wq[:, kc, :], h2[:, kc, :n], [Twq, Th2], [Tpp], start=(kc == 0), stop=(kc == KC - 1))
                            qf, Tqf = qfr.next()
                            cp("dve", qf[:, :n], pp[:, :n], [Tpp], [Tqf])
                            sq, Tsq = sqr.next()
                            act(sq[:, :n], qf[:, :n], AF.Square, [Tqf], [Tsq])
                            pb, Tpb = psB.next()
                            mm(pb[:, :n], ONES, sq[:, :n], [Tsq, Tc], [Tpb])
                            rt, Trt = rtr.next()
                            act(rt[:, :n], pb[:, :n], AF.Sqrt, [Tpb, Tc], [Trt], scale=1.0 / 128.0, bias=EPS_AP)
                            recip(rt[:, :n], rt[:, :n], [Trt], [Trt])
                            tt("dve", QN[:, gi, :n], qf[:, :n], rt[:, :n], ALU.mult, [Tqf, Trt], (), aw=[TQN])
                        for s0 in range(0, n, 128):
                            tk0 = c0 + s0
                            SC, TSC = SCr.next()
                            TSC.new_version()
                            for gq in range(4):
                                pp, Tpp = psA.next()
                                for g4 in range(4):
                                    gi = gq * 4 + g4
                                    mm(pp[:, g4 * 128:(g4 + 1) * 128], QN[:, gi, s0:s0 + 128], skt[:, gi, :], [TQN, Tsk], [Tpp])
                                evac(SC[:, gq * 4:(gq + 1) * 4, :], pp[:].rearrange("p (a b) -> p a b", a=4), [Tpp], (), aw=[TSC])
                            m1, Tm1 = m1r.next()
                            ix, Tix = ixr.next()
                            Tg = [(T(), T(), T()) for _ in range(16)]
                            Tm1.new_version()
                            Tix.new_version()
                            for gi in range(16):
                                S.op("dve", lambda e, o=m1[:, gi, 0:8], i=SC[:, gi, :]: e.max(out=o, in_=i), [TSC], [Tg[gi][0]], [Tm1])
                            for gi in range(16):
                                S.op("dve", lambda e, o=wk16[:, gi, :], r=m1[:, gi, 0:8], i=SC[:, gi, :]: e.match_replace(out=o, in_to_replace=r, in_values=i, imm_value=-1e30), [TSC, Tg[gi][0]], [Twk16[gi]])
                            for gi in range(16):
                                S.op("dve", lambda e, o=m1[:, gi, 8:16], i=wk16[:, gi, :]: e.max(out=o, in_=i), [Twk16[gi]], [Tg[gi][1]], [Tm1])
                            for gi in range(16):
                                S.op("dve", lambda e, o=ix[:, gi, 0:8], r=m1[:, gi, 0:8], i=SC[:, gi, :]: e.max_index(out=o, in_max=r, in_values=i), [TSC, Tg[gi][0]], (), [Tix])
                            for gi in range(16):
                                S.op("dve", lambda e, o=ix[:, gi, 8:16], r=m1[:, gi, 8:16], i=SC[:, gi, :]: e.max_index(out=o, in_max=r, in_values=i), [TSC, Tg[gi][1]], (), [Tix])
                            cp("dve", ixf[:], ix[:], [Tix], [Tk_])
                            m1v = m1[:].rearrange("p (h c) k -> p h c k", c=2)
                            ixv = ixf[:].rearrange("p (h c) k -> p h c k", c=2)
                            cand4 = cand[:].rearrange("p h (a b) -> p h a b", a=16)
                            tt("dve", cand4, m1v[:, :, 0, :].unsqueeze(3).to_broadcast([128, 8, 16, 16]),
                               m1v[:, :, 1, :].unsqueeze(2).to_broadcast([128, 8, 16, 16]), ALU.add, [Tm1], [Tk_])
                            for h in range(P_HEADS):
                                wk, Twk = wkr.next()
                                S.op("dve", lambda e, o=b1[:, h, 0:8], i=cand[:, h, :]: e.max(out=o, in_=i), [Tk_], [Tk_])
                                S.op("dve", lambda e, o=wk[:], r=b1[:, h, 0:8], i=cand[:, h, :]: e.match_replace(out=o, in_to_replace=r, in_values=i, imm_value=-1e30), [Tk_], [Twk])
                                S.op("dve", lambda e, o=b1[:, h, 8:16], i=wk[:]: e.max(out=o, in_=i), [Twk], [Tk_])
                                S.op("dve", lambda e, o=pos[:, h, 0:8], r=b1[:, h, 0:8], i=cand[:, h, :]: e.max_index(out=o, in_max=r, in_values=i), [Tk_], [Tk_])
                                S.op("dve", lambda e, o=pos[:, h, 8:16], r=b1[:, h, 8:16], i=cand[:, h, :]: e.max_index(out=o, in_max=r, in_values=i), [Tk_], [Tk_])
                            ts("dve", pab[:, 0], pos[:], 4, ALU.logical_shift_right, [Tk_], [Tk_])
                            ts("dve", pab[:, 1], pos[:], 15, ALU.bitwise_and, [Tk_], [Tk_])
                            cp("dve", pabf[:], pab[:], [Tk_], [Tk_])
                            for c in range(2):
                                tt("dve", oh[:], IOTA16.unsqueeze(1).unsqueeze(1).to_broadcast([128, 8, 16, 16]),
                                   pabf[:, c].unsqueeze(3).to_broadcast([128, 8, 16, 16]), ALU.is_equal, [Tk_, Tc], [Tk_])
                                tt("dve", oh[:], oh[:], ixv[:, :, c, :].unsqueeze(2).to_broadcast([128, 8, 16, 16]), ALU.mult, [Tk_], [Tk_])
                                S.op("dve", lambda e, o=isel[:, c], i=oh[:]: e.tensor_reduce(out=o, in_=i, axis=AX.X, op=ALU.add), [Tk_], [Tk_])
                            stt(idxf[:].rearrange("p (h k) -> p h k", h=8), isel[:, 0], 128.0, isel[:, 1], ALU.mult, ALU.add, [Tk_], [Tk_])
                            tt("dve", gate[:], b1[:], b1[:, :, 0:1].to_broadcast([128, 8, 16]), ALU.subtract, [Tk_], [Tk_])
                            act(gate[:], gate[:], AF.Exp, [Tk_], [Tk_])
                            S.op("dve", lambda e, o=gsm[:], i=gate[:]: e.tensor_reduce(out=o, in_=i, axis=AX.X, op=ALU.add), [Tk_], [Tk_])
                            recip(gsm[:], gsm[:], [Tk_], [Tk_])
                            tt("dve", gate[:], gate[:], gsm[:].unsqueeze(2).to_broadcast([128, 8, 16]), ALU.mult, [Tk_], [Tk_])
                            pq, Tpq = psQ.next()
                            tr(pq, idxf[:], [Tk_], [Tpq])
                            cp("dve", IDXT[:, tk0:tk0 + 128], pq, [Tpq], (), aw=[TIG])
                            pq2, Tpq2 = psQ.next()
                            tr(pq2, gate[:].rearrange("p h k -> p (h k)"), [Tk_], [Tpq2])
                            cp("act", GATET[:, tk0:tk0 + 128], pq2, [Tpq2], (), aw=[TIG])
                S.barrier()
                if cfg.stop == "peertopk" and l == cfg.stop_layer:
                    if cfg.debug:
                        dma("sp", DBGF_d[:, ptok0:TT], GATET[:, ptok0:TT], reads=[TIG])
                        dma("sp", DBGI_d[:, ptok0:TT], IDXT[:, ptok0:TT], reads=[TIG])
                    break
                TX.new_version()
                with ExitStack() as ges:
                    SEL = sbuf(ges, "SEL", [128, 128, 128], BF16)
                    Tsel = T()
                    cp("dve", SEL[:], IDENT.unsqueeze(2).to_broadcast([128, 128, 128]), [Tc], [Tsel])
                    uvr = ring(ges, "uvg", 12, [128, 2 * D], BF16)
                    h2fr = ring(ges, "h2f", 2, [128, D])
                    h2br = ring(ges, "h2b", 2, [128, D], BF16)
                    jkr2 = ring(ges, "pjunk", 2, [128, 512], BF16)
                    dotr = ring(ges, "dots", 2, [128, 128, 2])
                    ctr = ring(ges, "ct", 2, [128, 128])
                    ctbr = ring(ges, "ctb", 2, [128, 128], BF16)
                    xtr = ring(ges, "pxt", 2, [128, KC, 128])
                    xor = ring(ges, "pxo", 2, [128, KC, 128])
                    eoff = l * P_KEYS * P_KEYS * 2 * D
                    pbank = 4
                    hbank = 0
                    for tk0 in range(ptok0, TT, 128):
                        j = 1 if tk0 < TC else 0
                        h2f, Th2f = h2fr.next()
                        dma("sp", h2f[:], H2_d[tk0:tk0 + 128, :], reads=[TH2], writes=[Th2f])
                        h2b, Th2b = h2br.next()
                        cp("act", h2b[:], h2f[:], [Th2f], [Th2b])
                        dots, Tdots = dotr.next()
                        Tdots.new_version()
                        ct, Tct = ctr.next()
                        Tct.new_version()
                        ctb, Tctb = ctbr.next()
                        Tctb.new_version()
                        (pA, TpA), (pB_, TpB) = PB[pbank], PB[pbank + 1]
                        pbank = 4 + (pbank - 4 + 2) % 4
                        for i in range(128):
                            n = tk0 + i
                            uv, Tuv = uvr.next()
                            S.dma("pool", lambda e, o=uv[:], ix=IDXT[:, n:n + 1], eoff=eoff: e.indirect_dma_start(
                                out=o, out_offset=None, in_=UV_d, in_offset=bass.IndirectOffsetOnAxis(ap=ix, axis=0), element_offset=eoff), [TIG, TUB], [Tuv])
                            for hf in range(2):
                                hbk, Thbk = PB[hbank]
                                hbank = (hbank + 1) % 4
                                mm(hbk[:, :], SEL[:, i, :], h2b[:, hf * 512:(hf + 1) * 512], [Tsel, Th2b], [Thbk])
                                junk, Tjunk = jkr2.next()
                                stt(junk[:], uv[:, hf * 512:(hf + 1) * 512], 1.0, hbk[:, :], ALU.mult, ALU.mult, [Tuv, Thbk], [Tjunk],
                                    accum_out=dots[:, i, hf:hf + 1], aw=[Tdots])
                            act(ct[:, i:i + 1], dots[:, i, 0:1], AF.Gelu_apprx_tanh, [Tdots], (), aw=[Tct], bias=dots[:, i, 1:2])
                            tt("dve", ctb[:, i:i + 1], ct[:, i:i + 1], GATET[:, n:n + 1], ALU.mult, [Tct, TIG], (), aw=[Tctb])
                            for kc in range(KC):
                                pt_, Tpt_ = (pA, TpA) if kc < 4 else (pB_, TpB)
                                S.op("pe", lambda e, o=pt_[:, (kc % 4) * 128 + i:(kc % 4) * 128 + i + 1], w=uv[:, D + kc * 128:D + (kc + 1) * 128], r=ctb[:, i:i + 1]:
                                     e.matmul(o, w, r, start=True, stop=True), [Tuv, Tctb], [Tpt_])
                        xt_, Txt = xtr.next()
                        dma("sp", xt_[:], Xv[:, :, tk0:tk0 + 128], reads=[TX], writes=[Txt])
                        xo, Txo = xor.next()
                        Txo.new_version()
                        for kc in range(KC):
                            pt_, Tpt_ = (pA, TpA) if kc < 4 else (pB_, TpB)
                            stt(xo[:, kc, :], pt_[:, (kc % 4) * 128:(kc % 4 + 1) * 128], gt2(j, kc), xt_[:, kc, :], ALU.mult, ALU.add, [Tpt_, Tp, Txt], (), aw=[Txo])
                        dma("sp", Xv[:, :, tk0:tk0 + 128], xo[:], reads=[Txo], awrites=[TX])
                S.barrier()
        else:
            dma("sp", out_d[:, :], X_d[:, TC:TT], reads=[TX])
        S.finish()
        S.emit()
    return nc


def _host_consts():
    c = np.zeros((NCONST, 128, 128), np.float32)
    c[C_ID] = np.eye(128, dtype=np.float32)
    c[C_ONES] = 1.0
    c[C_BD64, 0:64, 0:64] = 1.0
    c[C_BD64, 64:128, 64:128] = 1.0
    for p in range(128):
        dd = p % 64
        partner = p + 16 if (dd % 32) < 16 else p - 16
        c[C_PERM, partner, p] = 1.0
    ones = np.ones((128, 128), np.float32)
    c[C_UI] = np.triu(ones)
    c[C_LI] = np.tril(ones)
    c[C_SL] = np.tril(ones, -1)
    c[C_SU] = np.triu(ones, 1)
    c[C_IOTA0] = np.arange(128, dtype=np.float32)[None, :]
    c[C_IOTA1] = np.arange(128, dtype=np.float32)[None, :] + 128.0
    ii = np.arange(128)
    for k in range(7):
        s_ = 1 << k
        c[C_OFF + k] = ((ii[:, None] // (2 * s_) == ii[None, :] // (2 * s_)) & (ii[:, None] // s_ != ii[None, :] // s_)).astype(np.float32)
    return np.ascontiguousarray(c.transpose(1, 0, 2).reshape(128, NCONST * 128))


def _rope_tables(TL):
    t = np.arange(TL)
    row = (t // GRID_W).astype(np.float32)
    col = (t % GRID_W).astype(np.float32)
    nf = A_DIM // 4
    inv = (np.float32(ROPE_THETA) ** (-np.arange(nf, dtype=np.float32) / np.float32(nf))).astype(np.float32)
    out = np.zeros((2, 128, TL), np.float32)
    for p in range(128):
        dd = p % 64
        axis, half, f = dd // 32, (dd % 32) // 16, dd % 16
        ang = ((row if axis == 0 else col) * inv[f]).astype(np.float32)
        out[0, p] = np.cos(ang)
        out[1, p] = -np.sin(ang) if half == 0 else np.sin(ang)
    return out


def _chunkT(v, n):
    return np.ascontiguousarray(np.asarray(v, np.float32).reshape(n, 128).T)


def prepare_inputs(inp, cfg):
    L = cfg.L
    f = lambda a: np.asarray(a, np.float32)
    spar = np.zeros((L, 128, NSP), np.float32)
    rvec = np.zeros((L, NRV), np.float32)
    lruw = np.zeros((L, 2, 2, KC, 128, 128), np.float32)
    for l in range(L):
        s = spar[l]
        s[:, SP["b_ada"]:SP["b_ada"] + 48] = _chunkT(inp["b_ada"][l], 48)
        s[:, SP["n1g"]:SP["n1g"] + 8] = _chunkT(inp["norm1_g"][l], 8)
        s[:, SP["n2g"]:SP["n2g"] + 8] = _chunkT(inp["norm2_g"][l], 8)
        s[:, SP["gq"]] = np.tile(f(inp["attn_qn_g"][l]), 2)
        s[:, SP["gk"]] = np.tile(f(inp["attn_kn_g"][l]), 2)
        s[:, SP["subg"]] = f(inp["attn_sub_g"][l])
        for i, k in enumerate(("lam_q1", "lam_k1", "lam_q2", "lam_k2")):
            s[0:64, SP["lam"] + i] = f(inp[k][l])
        s[:, SP["gconv"]:SP["gconv"] + 120] = f(inp["gdn_conv_w"][l]).reshape(5, 24, 128).transpose(2, 1, 0).reshape(128, 120)
        s[:, SP["lconv"]:SP["lconv"] + 40] = f(inp["lru_conv_w"][l]).reshape(5, 8, 128).transpose(2, 1, 0).reshape(128, 40)
        s[:, SP["lconvb"]:SP["lconvb"] + 8] = _chunkT(inp["lru_conv_b"][l], 8)
        for nm, key in (("lbr", "lru_b_r"), ("lbi", "lru_b_i"), ("llam", "lru_lambda")):
            s[:, SP[nm]:SP[nm] + 16] = f(inp[key][l]).reshape(2, 8, 128).transpose(2, 0, 1).reshape(128, 16)
        rvec[l, 0:16] = f(inp["gdn_a_log"][l]).reshape(16)
        rvec[l, 16:32] = f(inp["gdn_dt_bias"][l]).reshape(16)
        rvec[l, 32:160] = f(inp["gdn_norm_g"][l])
        for d in range(2):
            for ri, key in enumerate(("lru_w_r", "lru_w_i")):
                w = f(inp[key][l, d])
                for c in range(KC):
                    lruw[l, d, ri, c, 0:64, 0:64] = w[2 * c]
                    lruw[l, d, ri, c, 64:128, 64:128] = w[2 * c + 1]
    skT = np.ascontiguousarray(f(inp["peer_subkeys"])[:L].reshape(L, 16, 128, 128).transpose(0, 1, 3, 2))
    shared = {
        "consts": _host_consts(), "rope": _rope_tables(cfg.TL), "spar": spar, "rvec": rvec,
        "w_ada": np.ascontiguousarray(f(inp["w_ada"])[:L]), "w_in": np.ascontiguousarray(f(inp["w_in"])[:L]),
        "w_branch": np.ascontiguousarray(f(inp["w_branch"])[:L]), "w_out": np.ascontiguousarray(f(inp["w_out"])[:L]),
        "lruw": lruw, "peer_wq": np.ascontiguousarray(f(inp["peer_wq"])[:L]), "skT": skT,
        "peer_u": np.ascontiguousarray(f(inp["peer_u"])[:L]), "peer_v": np.ascontiguousarray(f(inp["peer_v"])[:L]),
    }
    x, ctx, c, c_ctx = f(inp["x"]), f(inp["ctx"]), f(inp["c"]), f(inp["c_ctx"])
    maps = []
    for b in range(x.shape[0]):
        m = dict(shared)
        m["xT"] = np.ascontiguousarray(np.concatenate([ctx[b].T, x[b].T], axis=1))
        cond = np.stack([c[b].reshape(KC, 128).T, c_ctx.reshape(KC, 128).T], axis=2)
        m["cond"] = np.ascontiguousarray(cond.reshape(128, KC * 2))
        maps.append(m)
    return maps


_NC_CACHE = {}


def kernel(**inputs):
    x = np.asarray(inputs["x"])
    B, TL, _ = x.shape
    TC = np.asarray(inputs["ctx"]).shape[1]
    L = np.asarray(inputs["w_in"]).shape[0]
    cfg = Cfg(TC=TC, TL=TL, L=L)
    key = (TC, TL, L)
    if key not in _NC_CACHE:
        _NC_CACHE[key] = build(cfg)
    nc = _NC_CACHE[key]
    maps = prepare_inputs(inputs, cfg)
    res = run_bass_kernel_spmd(nc, maps, core_ids=list(range(B)))
    out = np.stack([np.ascontiguousarray(res.results[b]["outT"].T) for b in range(B)], axis=0)
    return out.astype(np.float32)
```
